# Optimizing a Trainium2 kernel written in Bass

```python
import jax
import jax.numpy as jnp
from jax import lax
import numpy as np

D_MODEL = 1024
BATCH = 4
SEQ = 4096
DEPTH = 1

MIX_WIDTH = 2 * D_MODEL
GDN_WIDTH = MIX_WIDTH // 2
MLSTM_WIDTH = MIX_WIDTH - GDN_WIDTH
GDN_HEADS = 8
GDN_DK = GDN_WIDTH // GDN_HEADS
GDN_DV = GDN_WIDTH // GDN_HEADS
MLSTM_HEADS = 8
MLSTM_DV = MLSTM_WIDTH // MLSTM_HEADS
MLSTM_DQK = MLSTM_DV // 2
CONV_K = 5
CHUNK = 64
RMS_EPS = 1e-6
L2_EPS = 1e-6
GDN_IN_SIZES = (GDN_WIDTH, GDN_WIDTH, GDN_WIDTH, GDN_WIDTH, 2 * GDN_HEADS, 2 * GDN_HEADS)
MLSTM_IN_SIZES = (MLSTM_HEADS * MLSTM_DQK, MLSTM_HEADS * MLSTM_DQK, MLSTM_WIDTH, MLSTM_WIDTH, MLSTM_WIDTH, 4 * MLSTM_HEADS)
GDN_IN = sum(GDN_IN_SIZES)
MLSTM_IN = sum(MLSTM_IN_SIZES)
IN_WIDTH = GDN_IN + MLSTM_IN

kernel_name = 'hybrid_gdn_mlstm_bidir_block'


def _split(t, sizes):
    return jnp.split(t, np.cumsum(sizes)[:-1].tolist(), axis=-1)


def _rms_norm(x, w):
    xf = x.astype(jnp.float32)
    xf = xf * lax.rsqrt(jnp.mean(xf * xf, axis=-1, keepdims=True) + RMS_EPS)
    return (xf * w.astype(jnp.float32)).astype(x.dtype)


def _head_rms(t):
    return t * lax.rsqrt(jnp.mean(t * t, axis=-1, keepdims=True) + RMS_EPS)


def _l2norm(t):
    return t * lax.rsqrt(jnp.sum(t * t, axis=-1, keepdims=True) + L2_EPS)


def _centred_dwconv(x, w):
    c = x.shape[-1]
    k = w.shape[0]
    return lax.conv_general_dilated(
        x, w[:, None, :].astype(x.dtype), window_strides=(1,), padding=[(k // 2, k // 2)],
        dimension_numbers=('NWC', 'WIO', 'NWC'), feature_group_count=c)


def _to_heads(t, n_heads):
    b, s, c = t.shape
    return t.reshape(b, s, n_heads, c // n_heads).transpose(0, 2, 1, 3).astype(jnp.float32)


def _from_heads(t):
    b, h, s, d = t.shape
    return t.transpose(0, 2, 1, 3).reshape(b, s, h * d)


def _gates_to_heads(t, n_groups, n_heads):
    b, s, _ = t.shape
    return t.reshape(b, s, n_groups, n_heads).transpose(2, 0, 3, 1).astype(jnp.float32)


def _to_chunks(t):
    b, h, s = t.shape[:3]
    return t.reshape(b, h, s // CHUNK, CHUNK, *t.shape[3:])


def _from_chunks(t):
    n, b, h, l, d = t.shape
    return jnp.moveaxis(t, 0, 2).reshape(b, h, n * l, d)


def _flip(t):
    return jnp.flip(t, axis=2)


def _gated_delta_chunked(q, k, v, g, beta):
    b_, h_, _, dk = q.shape
    dv = v.shape[-1]
    q, k, v, g, beta = (_to_chunks(t) for t in (q, k, v, g, beta))
    g = jnp.cumsum(g, axis=-1)
    lower = jnp.tril(jnp.ones((CHUNK, CHUNK), dtype=bool))
    strict = jnp.tril(jnp.ones((CHUNK, CHUNK), dtype=bool), -1)
    decay = jnp.exp(jnp.where(lower, g[..., :, None] - g[..., None, :], -jnp.inf))
    k_beta = k * beta[..., None]
    a_mat = jnp.where(strict, jnp.einsum('bhnid,bhnjd->bhnij', k_beta, k) * decay, 0.0)
    eye = jnp.eye(CHUNK, dtype=q.dtype)
    rhs = jnp.concatenate([v * beta[..., None], k_beta * jnp.exp(g)[..., None]], axis=-1)
    sol = lax.linalg.triangular_solve(a_mat + eye, rhs, left_side=True, lower=True, unit_diagonal=True)
    u, w = sol[..., :dv], sol[..., dv:]
    qk = jnp.where(lower, jnp.einsum('bhnid,bhnjd->bhnij', q, k) * decay, 0.0)
    q_dec = q * jnp.exp(g)[..., None]
    k_tail = k * jnp.exp(g[..., -1:] - g)[..., None]
    chunk_dec = jnp.exp(g[..., -1])
    xs = tuple(jnp.moveaxis(t, 2, 0) for t in (qk, q_dec, k_tail, u, w, chunk_dec))

    def step(state, inp):
        qk_c, qd_c, kt_c, u_c, w_c, cd_c = inp
        v_new = u_c - jnp.einsum('bhld,bhde->bhle', w_c, state)
        o = jnp.einsum('bhld,bhde->bhle', qd_c, state) + jnp.einsum('bhij,bhje->bhie', qk_c, v_new)
        state = state * cd_c[..., None, None] + jnp.einsum('bhld,bhle->bhde', kt_c, v_new)
        return state, o

    state0 = jnp.zeros((b_, h_, dk, dv), jnp.float32)
    _, o = lax.scan(step, state0, xs)
    return _from_chunks(o)


def _mlstm_chunked(q, k, v, i_pre, f_pre):
    b_, h_, _, dqk = q.shape
    dv = v.shape[-1]
    q, k, v, i_pre, f_pre = (_to_chunks(t) for t in (q, k, v, i_pre, f_pre))
    cum_logf = jnp.cumsum(jax.nn.log_sigmoid(f_pre), axis=-1)
    lower = jnp.tril(jnp.ones((CHUNK, CHUNK), dtype=bool))
    log_w = jnp.where(lower, cum_logf[..., :, None] - cum_logf[..., None, :] + i_pre[..., None, :], -jnp.inf)
    m_intra = jnp.max(log_w, axis=-1)
    qk = jnp.einsum('bhnid,bhnjd->bhnij', q, k)
    xs = tuple(jnp.moveaxis(t, 2, 0) for t in (q, k, v, cum_logf, log_w, m_intra, qk))

    def step(carry, inp):
        c_state, n_state, m_state = carry
        q_c, k_c, v_c, b_c, lw_c, mi_c, qk_c = inp
        m_t = jnp.maximum(b_c + m_state[..., None], mi_c)
        inter = jnp.exp(b_c + m_state[..., None] - m_t)
        w_intra = jnp.exp(lw_c - m_t[..., None]) * qk_c
        num = inter[..., None] * jnp.einsum('bhld,bhde->bhle', q_c, c_state) + jnp.einsum('bhts,bhse->bhte', w_intra, v_c)
        den = inter * jnp.einsum('bhld,bhd->bhl', q_c, n_state) + jnp.sum(w_intra, axis=-1)
        h = num / jnp.maximum(jnp.abs(den), jnp.exp(-m_t))[..., None]
        lw_last = lw_c[..., -1, :]
        m_new = jnp.maximum(b_c[..., -1] + m_state, mi_c[..., -1])
        carry_dec = jnp.exp(b_c[..., -1] + m_state - m_new)
        wk = jnp.exp(lw_last - m_new[..., None])[..., None] * k_c
        c_state = carry_dec[..., None, None] * c_state + jnp.einsum('bhld,bhle->bhde', wk, v_c)
        n_state = carry_dec[..., None] * n_state + jnp.sum(wk, axis=2)
        return (c_state, n_state, m_new), h

    carry0 = (jnp.zeros((b_, h_, dqk, dv), jnp.float32),
              jnp.zeros((b_, h_, dqk), jnp.float32),
              jnp.zeros((b_, h_), jnp.float32))
    _, h = lax.scan(step, carry0, xs)
    return _from_chunks(h)


def _gdn_branch(cols, conv_w, a_log, dt_bias, norm_w):
    q, k, v, z, a, b = _split(cols, GDN_IN_SIZES)
    qkv = jax.nn.silu(_centred_dwconv(jnp.concatenate([q, k, v], axis=-1), conv_w))
    q, k, v = _split(qkv, (GDN_WIDTH, GDN_WIDTH, GDN_WIDTH))
    q = _l2norm(_to_heads(q, GDN_HEADS)) * (GDN_DK ** -0.5)
    k = _l2norm(_to_heads(k, GDN_HEADS))
    v = _to_heads(v, GDN_HEADS)
    a = _gates_to_heads(a, 2, GDN_HEADS)
    b = _gates_to_heads(b, 2, GDN_HEADS)
    g = -jnp.exp(a_log.astype(jnp.float32))[:, None, :, None] * jax.nn.softplus(a + dt_bias.astype(jnp.float32)[:, None, :, None])
    beta = jax.nn.sigmoid(b)
    o_fwd = _gated_delta_chunked(q, k, v, g[0], beta[0])
    o_bwd = _flip(_gated_delta_chunked(_flip(q), _flip(k), _flip(v), _flip(g[1]), _flip(beta[1])))
    o = _head_rms(o_fwd + o_bwd) * norm_w.astype(jnp.float32)
    return (_from_heads(o) * jax.nn.silu(z.astype(jnp.float32))).astype(cols.dtype)


def _mlstm_branch(cols, conv_w, gate_bias, norm_w):
    q, k, v, o, z, gates = _split(cols, MLSTM_IN_SIZES)
    qk = jax.nn.silu(_centred_dwconv(jnp.concatenate([q, k], axis=-1), conv_w))
    q, k = _split(qk, (MLSTM_HEADS * MLSTM_DQK, MLSTM_HEADS * MLSTM_DQK))
    q = _to_heads(q, MLSTM_HEADS)
    k = _to_heads(k, MLSTM_HEADS) * (MLSTM_DQK ** -0.5)
    v = _to_heads(v, MLSTM_HEADS)
    gates = _gates_to_heads(gates, 4, MLSTM_HEADS) + gate_bias.astype(jnp.float32)[:, None, :, None]
    h_fwd = _mlstm_chunked(q, k, v, gates[0], gates[2])
    h_bwd = _flip(_mlstm_chunked(_flip(q), _flip(k), _flip(v), _flip(gates[1]), _flip(gates[3])))
    h = _from_heads(_head_rms(h_fwd + h_bwd)) * norm_w.astype(jnp.float32)
    h = h * jax.nn.sigmoid(o.astype(jnp.float32)) * jax.nn.silu(z.astype(jnp.float32))
    return h.astype(cols.dtype)


def setup_inputs(seed: int = 0) -> dict:
    key = jax.random.key(seed)
    ks = jax.random.split(key, 16)
    f32 = jnp.float32
    x = jax.random.normal(ks[0], (BATCH, SEQ, D_MODEL), f32)
    norm_pre_w = 1.0 + 0.02 * jax.random.normal(ks[1], (DEPTH, D_MODEL), f32)
    w_in = jax.random.normal(ks[2], (DEPTH, D_MODEL, IN_WIDTH), f32) * (D_MODEL ** -0.5)
    gdn_conv_w = jax.random.normal(ks[3], (DEPTH, CONV_K, 3 * GDN_WIDTH), f32) * (CONV_K ** -0.5)
    gdn_a_log = jnp.log(jax.random.uniform(ks[4], (DEPTH, 2, GDN_HEADS), f32, 1.0, 16.0))
    dt = jnp.exp(jax.random.uniform(ks[5], (DEPTH, 2, GDN_HEADS), f32, float(np.log(1e-3)), float(np.log(1e-1))))
    gdn_dt_bias = dt + jnp.log(-jnp.expm1(-dt))
    gdn_norm_w = 1.0 + 0.02 * jax.random.normal(ks[6], (DEPTH, GDN_DV), f32)
    mlstm_conv_w = jax.random.normal(ks[7], (DEPTH, CONV_K, 2 * MLSTM_HEADS * MLSTM_DQK), f32) * (CONV_K ** -0.5)
    i_bias = 0.1 * jax.random.normal(ks[8], (DEPTH, 2, MLSTM_HEADS), f32)
    f_bias = jnp.linspace(3.0, 6.0, MLSTM_HEADS, dtype=f32) + 0.1 * jax.random.normal(ks[9], (DEPTH, 2, MLSTM_HEADS), f32)
    mlstm_gate_bias = jnp.concatenate([i_bias, f_bias], axis=1)
    mlstm_norm_w = 1.0 + 0.02 * jax.random.normal(ks[10], (DEPTH, MLSTM_WIDTH), f32)
    w_out = jax.random.normal(ks[11], (DEPTH, MIX_WIDTH, D_MODEL), f32) * (MIX_WIDTH ** -0.5)
    norm_post_w = 1.0 + 0.02 * jax.random.normal(ks[12], (DEPTH, D_MODEL), f32)
    return {'x': x, 'norm_pre_w': norm_pre_w, 'w_in': w_in, 'gdn_conv_w': gdn_conv_w,
            'gdn_a_log': gdn_a_log, 'gdn_dt_bias': gdn_dt_bias, 'gdn_norm_w': gdn_norm_w,
            'mlstm_conv_w': mlstm_conv_w, 'mlstm_gate_bias': mlstm_gate_bias, 'mlstm_norm_w': mlstm_norm_w,
            'w_out': w_out, 'norm_post_w': norm_post_w}


def reference(x, norm_pre_w, w_in, gdn_conv_w, gdn_a_log, gdn_dt_bias, gdn_norm_w,
              mlstm_conv_w, mlstm_gate_bias, mlstm_norm_w, w_out, norm_post_w):
    for layer in range(DEPTH):
        h = _rms_norm(x, norm_pre_w[layer])
        proj = jnp.einsum('bsd,de->bse', h, w_in[layer])
        gdn_cols, mlstm_cols = jnp.split(proj, [GDN_IN], axis=-1)
        g_out = _gdn_branch(gdn_cols, gdn_conv_w[layer], gdn_a_log[layer], gdn_dt_bias[layer], gdn_norm_w[layer])
        m_out = _mlstm_branch(mlstm_cols, mlstm_conv_w[layer], mlstm_gate_bias[layer], mlstm_norm_w[layer])
        mixed = jnp.einsum('bse,ed->bsd', jnp.concatenate([g_out, m_out], axis=-1), w_out[layer])
        x = x + _rms_norm(mixed, norm_post_w[layer])
    return x
```

```python
import contextlib
import os
CUT = int(os.environ.get('S2CUT', '9'))
S3C = int(os.environ.get('S3C', '99'))
S3T = int(os.environ.get('S3T', '32'))
S3D = int(os.environ.get('S3D', '2'))
W_CL = int(os.environ.get('W_CL', '1'))
W_SC = int(os.environ.get('W_SC', '1'))
S3OFF = int(os.environ.get('S3OFF', '0'))
FINE = int(os.environ.get('FINE', '0'))
FINE2 = int(os.environ.get('FINE2', '0'))
DMAQ = int(os.environ.get('DMAQ', '1'))
STQ = os.environ.get('STQ', 'pool')
XRING = int(os.environ.get('XRING', '4'))
GORD = int(os.environ.get('GORD', '1'))
import numpy as np
import concourse.bass as bass
import concourse.mybir as mybir
from concourse.bass_utils import run_bass_kernel_spmd

F32 = mybir.dt.float32
BF16 = mybir.dt.bfloat16
AF = mybir.ActivationFunctionType
ALU = mybir.AluOpType

S = 4096
D = 1024
NT = S // 128
NG = S // 512
NFM = 16
DEBUG = None


class Buf:
    __slots__ = ("name", "lastw", "readers", "excl")

    def __init__(self, name="", excl=False):
        self.name = name
        self.lastw = None
        self.readers = []
        self.excl = excl


class V:
    __slots__ = ("ap", "buf")

    def __init__(self, ap, buf):
        self.ap = ap
        self.buf = buf

    def __getitem__(self, k):
        return V(self.ap[k], self.buf)

    def r(self, pat, **kw):
        return V(self.ap.rearrange(pat, **kw), self.buf)

    def bc(self, shape):
        return V(self.ap.to_broadcast(shape), self.buf)


class Prog:
    ENGS = ("pe", "act", "dve", "pool", "sp")

    def __init__(self, nc, n_dma_sems=32, self_sync=True):
        self.nc = nc
        self.self_sync = self_sync
        self.ops = {e: [] for e in self.ENGS}
        self.cnt = {e: 0 for e in self.ENGS}
        self.semobjs = {}
        for e in ("pe", "act", "dve", "pool"):
            self.semobjs["E" + e] = nc.alloc_semaphore("sem_" + e)
        self.ndma = n_dma_sems
        for i in range(n_dma_sems):
            self.semobjs["D%d" % i] = nc.alloc_semaphore("sem_dma%d" % i)
        self.semobjs["CC"] = nc.alloc_semaphore("sem_cc")
        self.cc_val = 0
        self.dma_next = 0
        self.dma_val = {("D%d" % i): 0 for i in range(n_dma_sems)}
        self.known = {e: {} for e in self.ENGS}

    def _collect(self, eng, reads, writes):
        waits = {}

        def add(ev):
            if ev is None:
                return
            k, v = ev
            if k == "E" + eng and (not self.self_sync or eng == "pe"):
                return
            if waits.get(k, 0) < v:
                waits[k] = v
        for b in reads:
            add(b.lastw)
            if b.excl:
                for r in b.readers:
                    if r[0] != "E" + eng:
                        add(r)
        for b in writes:
            add(b.lastw)
            for r in b.readers:
                if r[0] == "E" + eng:
                    continue
                add(r)
        out = []
        kn = self.known[eng]
        for k, v in waits.items():
            if kn.get(k, 0) < v:
                kn[k] = v
                out.append((k, v))
        return out

    def _update(self, ev, reads, writes):
        for b in reads:
            b.readers.append(ev)
            if len(b.readers) > 64:
                b.readers = b.readers[-48:]
        for b in writes:
            b.lastw = ev
            b.readers = []

    def op(self, eng, fn, reads=(), writes=()):
        waits = self._collect(eng, reads, writes)
        self.cnt[eng] += 1
        ev = ("E" + eng, self.cnt[eng])
        self.ops[eng].append((waits, fn, ev, 1))
        self._update(ev, reads, writes)
        return ev

    def dma(self, eng, out, in_):
        reads, writes = [in_.buf], [out.buf]
        k = "D%d" % self.dma_next
        self.dma_next = (self.dma_next + 1) % self.ndma
        waits = self._collect(eng, reads, writes)
        prev = self.dma_val[k]
        if prev > 0 and self.known[eng].get(k, 0) < prev:
            self.known[eng][k] = prev
            waits.append((k, prev))
        self.dma_val[k] = prev + 16
        ev = (k, prev + 16)
        oa, ia = out.ap, in_.ap

        def fn(e):
            return e.dma_start(out=oa, in_=ia)
        self.ops[eng].append((waits, fn, ev, 16))
        self._update(ev, reads, writes)
        return ev

    def collective(self, fn, reads, writes):
        waits = self._collect("pool", reads, writes)
        self.cc_val += 1
        ev = ("CC", self.cc_val)
        self.ops["pool"].append((waits, fn, ev, 1))
        self._update(ev, reads, writes)
        return ev

    def wait_event(self, eng, ev):
        k, v = ev
        if self.known[eng].get(k, 0) < v:
            self.known[eng][k] = v
            self.ops[eng].append(([(k, v)], None, None, 0))

    def barrier(self):
        for e in self.ENGS:
            waits = []
            for e2 in ("pe", "act", "dve", "pool"):
                v = self.cnt[e2]
                if e2 != e and v > 0 and self.known[e].get("E" + e2, 0) < v:
                    self.known[e]["E" + e2] = v
                    waits.append(("E" + e2, v))
            for kk, v in self.dma_val.items():
                if v > 0 and self.known[e].get(kk, 0) < v:
                    self.known[e][kk] = v
                    waits.append((kk, v))
            if self.cc_val > 0 and self.known[e].get("CC", 0) < self.cc_val:
                self.known[e]["CC"] = self.cc_val
                waits.append(("CC", self.cc_val))
            if waits:
                self.ops[e].append((waits, None, None, 0))

    def emit(self):
        nc = self.nc
        engmap = {"pe": "tensor", "act": "scalar", "dve": "vector", "pool": "gpsimd", "sp": "sync"}
        with nc.Block() as block:
            for ename in self.ENGS:
                ops = self.ops[ename]
                if not ops:
                    continue

                def body(e, ops=ops):
                    for waits, fn, ev, inc in ops:
                        for k, v in waits:
                            e.wait_ge(self.semobjs[k], v)
                        if fn is None:
                            continue
                        ins = fn(e)
                        ins.then_inc(self.semobjs[ev[0]], inc)
                getattr(block, engmap[ename])(body)
        self.ops = {e: [] for e in self.ENGS}


def _bufs(*vs):
    out = []
    for v in vs:
        if isinstance(v, V) and v.buf not in out:
            out.append(v.buf)
    return out


def _a(v):
    return v.ap if isinstance(v, V) else v


class K:
    def __init__(self, P):
        self.P = P
        self.rr = 0

    def act(self, out, in_, func, bias=None, scale=None, accum=None):
        kw = {}
        if bias is not None:
            kw["bias"] = _a(bias)
        if scale is not None:
            kw["scale"] = _a(scale)
        if accum is not None:
            kw["accum_out"] = _a(accum)
        o, i = out.ap, in_.ap
        self.P.op("act", lambda e: e.activation(out=o, in_=i, func=func, **kw),
                  _bufs(in_, bias, scale), _bufs(out, accum))

    def tt(self, eng, out, in0, in1, op):
        o, a, b = out.ap, in0.ap, in1.ap
        self.P.op(eng, lambda e: e.tensor_tensor(out=o, in0=a, in1=b, op=op), _bufs(in0, in1), _bufs(out))

    def ts(self, eng, out, in0, s1, op0, s2=None, op1=None):
        o, a, x1, x2 = out.ap, in0.ap, _a(s1), _a(s2)
        if op1 is None:
            self.P.op(eng, lambda e: e.tensor_scalar(out=o, in0=a, scalar1=x1, scalar2=None, op0=op0), _bufs(in0, s1), _bufs(out))
        else:
            self.P.op(eng, lambda e: e.tensor_scalar(out=o, in0=a, scalar1=x1, scalar2=x2, op0=op0, op1=op1), _bufs(in0, s1, s2), _bufs(out))

    def stt(self, eng, out, in0, scalar, in1, op0, op1):
        o, a, s, b = out.ap, in0.ap, _a(scalar), in1.ap
        eng = "dve"
        self.P.op(eng, lambda e: e.scalar_tensor_tensor(out=o, in0=a, scalar=s, in1=b, op0=op0, op1=op1), _bufs(in0, scalar, in1), _bufs(out))

    def cp(self, eng, out, in_):
        o, i = out.ap, in_.ap
        if eng == "act":
            self.P.op("act", lambda e: e.activation(out=o, in_=i, func=AF.Copy), _bufs(in_), _bufs(out))
        else:
            self.P.op(eng, lambda e: e.tensor_copy(out=o, in_=i), _bufs(in_), _bufs(out))

    def recip(self, out, in_):
        o, i = out.ap, in_.ap
        self.P.op("dve", lambda e: e.reciprocal(out=o, in_=i), _bufs(in_), _bufs(out))

    def memset(self, eng, out, val):
        o = out.ap
        self.P.op(eng, lambda e: e.memset(o, val), [], _bufs(out))

    def mm(self, out, lhsT, rhs, start=True, stop=True):
        o, l, r = out.ap, lhsT.ap, rhs.ap
        self.P.op("pe", lambda e: e.matmul(o, lhsT=l, rhs=r, start=start, stop=stop), _bufs(lhsT, rhs), _bufs(out))

    def tr(self, out, in_, ident):
        o, i, d = out.ap, in_.ap, ident.ap
        self.P.op("pe", lambda e: e.transpose(o, i, d), _bufs(in_, ident), _bufs(out))

    def dma(self, out, in_, eng=None):
        if eng is None:
            eng = ("sp", "act")[self.rr % 2] if DMAQ == 2 else "sp"
            self.rr += 1
        return self.P.dma(eng, out, in_)


def build_nc(stages=99):
    nc = bass.Bass("TRN2", target_bir_lowering=False)
    P = Prog(nc)
    k = K(P)

    def din(name, shape, dt=F32):
        return V(nc.dram_tensor(name, list(shape), dt, kind="ExternalInput").ap(), Buf(name))

    def dscr(name, shape, dt):
        kind = "ExternalOutput" if (DEBUG and name in DEBUG) else "Internal"
        return nc.dram_tensor(name, list(shape), dt, kind=kind).ap()

    ARENA = 98 * 1024
    arena = nc.alloc_sbuf_tensor("arena", [128, ARENA], BF16)
    aoff = [0, 0]

    def sb(name, shape, dt, n=1):
        vs = []
        for i in range(n):
            ne = 1
            for d_ in shape[1:]:
                ne *= d_
            nb = ne * (2 if dt == F32 else 1)
            nb = (nb + 15) // 16 * 16
            off = aoff[0]
            aoff[0] += nb
            assert aoff[0] <= ARENA, ("arena overflow", name, aoff[0])
            ap = arena[0:shape[0], off:off + ne * (2 if dt == F32 else 1)]
            if dt == F32:
                ap = ap.bitcast(F32)
            if len(shape) == 3:
                ap = ap.rearrange("p (a b) -> p a b", a=shape[1])
            elif len(shape) == 4:
                ap = ap.rearrange("p (a b c) -> p a b c", a=shape[1], b=shape[2])
            vs.append(V(ap, Buf(name)))
        return vs if n > 1 else vs[0]

    def stage_begin():
        aoff[0] = aoff[1]

    def stage_end():
        P.barrier()

    def psum(name, shape, dt=F32):
        return V(nc.alloc_psum_tensor(name, list(shape), dt)[:], Buf(name, excl=True))

    x_in = din("x", [S, D])
    xh_in = din("xh", [S // 2, D])
    win_in = din("w_in", [D, 4128])
    wout_in = din("w_out", [2048, D])
    npre_in = din("npre", [128, 8])
    cwg_in = din("cw", [128, NFM * 5])
    gpar_in = din("gpar", [128, 48])
    nw_in = din("nw", [128, 1024 + 1024])
    msk_in = din("masks", [128, 12 * 128])
    sel_in = din("sel", [128, 2])
    y_out = V(nc.dram_tensor("y", [S // 2, D], F32, kind="ExternalOutput").ap(), Buf("y"))

    PT = dscr("PT", [128, NFM, S + 4], BF16)
    ZG = dscr("ZG", [S, 512], BF16)
    MVd = dscr("MV", [S, 512], BF16)
    ZO = dscr("ZO", [S, 512], BF16)
    GT = dscr("GT", [S, 32], F32)
    QTd = dscr("QT", [128, 4, S], BF16)
    KTd = dscr("KT", [128, 4, S], BF16)
    Kd = dscr("Ktm", [S, 512], BF16)
    Vd = dscr("Vtm", [S, 512], BF16)
    MQTd = dscr("MQT", [128, 2, S], BF16)
    MKTd = dscr("MKT", [128, 2, S], BF16)
    MKd = dscr("MKtm", [S, 256], BF16)
    OFd = dscr("OF", [S, 1024], F32)
    CINf = [nc.dram_tensor("CIN%d" % g, [128, 2048], F32).ap() for g in range(NG)]
    COUTf = [nc.dram_tensor("COUT%d" % g, [256, 2048], F32).ap() for g in range(NG)]
    CINv = [a.bitcast(BF16).rearrange("q (a t) -> (q a) t", a=8) for a in CINf]
    COUTv = [a.bitcast(BF16).rearrange("q (a t) -> (q a) t", a=8) for a in COUTf]
    CDBG = dscr("CIN", [1024, S], BF16) if (DEBUG and "CIN" in DEBUG) else None
    dbuf = {}

    def DB(name, i):
        key = (name, i)
        if key not in dbuf:
            dbuf[key] = Buf("%s%d" % key)
        return dbuf[key]

    DBG_AT = tuple(int(v) for v in os.environ.get("DBGAT", "0,0").split(","))

    def dump(name, v, d, ti):
        if not DEBUG or name not in DEBUG or (d, ti) != DBG_AT:
            return
        t = nc.dram_tensor(name, list(v.ap.shape), v.ap.dtype, kind="ExternalOutput").ap()
        k.dma(V(t, DB(name, 0)), v)

    mskf = sb("mskf", [128, 12 * 128], F32)
    k.dma(mskf, msk_in)
    mskb = sb("mskb", [128, 12 * 128], BF16)
    k.cp("dve", mskb, mskf)

    def MF(i):
        return mskf[:, i * 128:(i + 1) * 128]

    def MB(i):
        return mskb[:, i * 128:(i + 1) * 128]
    IDENT, ONES, BLK, SEL0 = 0, 1, 2, 3
    identb = MB(IDENT)
    npre = sb("npre", [128, 8], F32)
    k.dma(npre, npre_in)
    cw = sb("cw", [128, NFM * 5], F32)
    k.dma(cw, cwg_in)
    gpar = sb("gpar", [128, 48], F32)
    k.dma(gpar, gpar_in)
    nw = sb("nw", [128, 2048], F32)
    k.dma(nw, nw_in)
    gA = sb("gA", [128, 8], F32)
    k.act(gA, gpar[:, 0:8], AF.Exp)
    zeros = sb("zeros", [128, 64], BF16)
    k.memset("pool", zeros, 0.0)
    PTb = Buf("PT")
    PTv = V(PT, PTb)
    k.dma(V(PT[:, :, 0:2], PTb), zeros[:, 0:32].r("p (b t) -> p b t", t=2))
    k.dma(V(PT[:, :, S + 2:S + 4], PTb), zeros[:, 0:32].r("p (b t) -> p b t", t=2))

    eps_t = sb("eps", [128, 2], F32)
    k.memset("pool", eps_t[:, 0:1], 1e-6)
    k.memset("pool", eps_t[:, 1:2], 1.0)
    eps6 = eps_t[:, 0:1]
    one1 = eps_t[:, 1:2]
    aoff[1] = aoff[0]
    stage_begin()
    Wb0 = sb("Wb", [128, 8, 4128], BF16)
    WPC = [(i * 256, (i + 1) * 256) for i in range(8)] + [(2048, 2560), (2560, 3072), (3072, 3584), (3584, 4128)]
    Wbufs = [Buf("Wb%d" % i) for i in range(len(WPC))]

    def Wsl(kc, c0, c1):
        for i, (a0, a1) in enumerate(WPC):
            if a0 <= c0 and c1 <= a1:
                return V(Wb0.ap[:, kc, c0:c1], Wbufs[i])
        raise AssertionError((c0, c1))
    wst = sb("wst", [128, 8, 544], F32, n=2)
    win_v = win_in.r("(kc p) c -> p kc c", p=128)
    for i in range(len(WPC)):
        c0, c1 = WPC[i]
        st = wst[i % 2][:, :, 0:c1 - c0]
        k.dma(st, win_v[:, :, c0:c1])
        k.tt(("dve", "pool")[i % 2], V(Wb0.ap[:, :, c0:c1], Wbufs[i]), st, npre.r("p (k o) -> p k o", o=1).bc([128, 8, c1 - c0]), ALU.mult)

    banks = [psum("bk%d" % i, [128, 512]) for i in range(8)]

    def pbview(v):
        return V(v.ap.bitcast(BF16), v.buf)
    ps = banks[:7]
    pb = pbview(banks[7])

    xt = sb("xt", [128, D], F32, n=XRING)
    junk = sb("junk", [128, D], BF16)
    st1 = sb("st1", [128, 4], F32, n=2)
    hb = sb("hb", [128, D], BF16, n=2)
    hT = sb("hT", [128, 8, 512], BF16, n=2)
    stg = sb("stg", [128, NFM, 512], BF16, n=1)
    stg = [stg, stg]
    tmz = sb("tmz", [128, 512], BF16, n=3)
    tmo = sb("tmo", [128, 512], F32, n=2)
    tmg = sb("tmg", [128, 32], F32, n=2)
    s1c = {"prep": 0, "mm": 0}

    def s1_prep():
        for g in range(NG):
            while s1c["mm"] < g - 1:
                yield
            hTg = hT[g % 2]
            for t in range(4):
                ti = g * 4 + t
                x_ = xt[ti % XRING]
                s_ = st1[ti % 2]
                h_ = hb[ti % 2]
                k.dma(x_, x_in[ti * 128:(ti + 1) * 128, :])
                k.act(junk, x_, AF.Square, accum=s_[:, 0:1])
                k.act(s_[:, 1:2], s_[:, 0:1], AF.Sqrt, bias=eps6, scale=1.0 / D)
                k.recip(s_[:, 2:3], s_[:, 1:2])
                k.ts("dve", h_, x_, s_[:, 2:3], ALU.mult)
                yield
                for kc in range(8):
                    k.tr(pb[:, kc * 128:(kc + 1) * 128], h_[:, kc * 128:(kc + 1) * 128], identb)
                k.cp("act", hTg[:, :, t * 128:(t + 1) * 128], pb.r("p (k c) -> p k c", k=8))
                yield
            s1c["prep"] = g + 1
            yield

    def s1_mm():
        cnt = 0
        for g in range(NG):
            while s1c["prep"] <= g:
                yield
            hTg = hT[g % 2]
            sg = stg[g % 2]
            for blk in range(NFM):
                p_ = ps[blk % 2]
                for kc in range(8):
                    k.mm(p_, Wsl(kc, blk * 128, (blk + 1) * 128), hTg[:, kc, :], start=(kc == 0), stop=(kc == 7))
                k.cp(("act", "dve")[blk % 2], sg[:, blk, :], p_)
                yield
            k.dma(V(PT[:, :, 2 + g * 512:2 + (g + 1) * 512], PTb), sg, eng=STQ)
            for t in range(4):
                ti = g * 4 + t
                lh = hTg[:, :, t * 128:(t + 1) * 128]
                rows = slice(ti * 128, (ti + 1) * 128)
                for cb in range(4):
                    p_ = ps[2 + cb]
                    for kc in range(8):
                        k.mm(p_, lh[:, kc, :], Wsl(kc, 2048 + cb * 512, 2048 + (cb + 1) * 512), start=(kc == 0), stop=(kc == 7))
                pg = ps[6][:, 0:32]
                for kc in range(8):
                    k.mm(pg, lh[:, kc, :], Wsl(kc, 4096, 4128), start=(kc == 0), stop=(kc == 7))
                z_ = tmz[cnt % 3]; cnt += 1
                k.act(z_, ps[2], AF.Silu)
                k.dma(V(ZG[rows, :], DB("ZG", ti)), z_, eng=STQ)
                z_ = tmz[cnt % 3]; cnt += 1
                k.cp("dve", z_, ps[3])
                k.dma(V(MVd[rows, :], DB("MV", ti)), z_, eng=STQ)
                o_ = tmo[ti % 2]
                k.act(o_, ps[4], AF.Sigmoid)
                o2 = tmo[(ti + 1) % 2]
                k.act(o2, ps[5], AF.Silu)
                z_ = tmz[cnt % 3]; cnt += 1
                k.tt("dve", z_, o_, o2, ALU.mult)
                k.dma(V(ZO[rows, :], DB("ZO", ti)), z_, eng=STQ)
                g_ = tmg[ti % 2]
                k.cp("dve", g_, pg)
                k.dma(V(GT[rows, :], DB("GT", ti)), g_, eng=STQ)
                yield
            s1c["mm"] = g + 1
            yield

    if stages >= 1:
        _gens = [s1_prep(), s1_mm()]
        while _gens:
            for _g in list(_gens):
                try:
                    next(_g)
                except StopIteration:
                    _gens.remove(_g)

    stage_end()
    stage_begin()
    ptg = sb("ptg", [128, NFM, 516], BF16, n=2)
    dg = sb("dg", [128, NFM * 5, 128], BF16)
    for i in range(NFM * 5):
        k.ts(("dve", "pool")[i % 2], dg[:, i, :], MF(IDENT), cw[:, i:i + 1], ALU.mult)
    sl = sb("sl", [128, 512], F32, n=10)
    sq = sb("sq", [128, 512], BF16, n=3)
    rn = sb("rn", [128, 512], F32, n=3)
    fmo = sb("fmo", [128, NFM, 512], BF16, n=2)
    tmk = sb("tmk", [128, 512], BF16, n=4)
    onesb = MB(ONES)
    s2c = {"a": 0, "b": 0, "c": 0}

    def s2_conv():
        for g in range(NG):
            while s2c["b"] < g or s2c["c"] < g - 1:
                yield
            pt_ = ptg[g % 2]
            k.dma(pt_, V(PT[:, :, g * 512:g * 512 + 516], PTb))
            fo = fmo[g % 2]
            for blk in range(NFM):
                a_ = ps[blk % 4]
                for j in range(5):
                    k.mm(a_, dg[:, blk * 5 + j, :], pt_[:, blk, j:j + 512], start=(j == 0), stop=(j == 4))
                if blk < 8:
                    k.act(sl[blk], a_, AF.Silu)
                elif blk < 14:
                    k.act(fo[:, blk, :], a_, AF.Silu)
                else:
                    s_ = sl[8 + blk % 2]
                    k.act(s_, a_, AF.Silu)
                    k.ts("dve", fo[:, blk, :], s_, 0.125, ALU.mult)
                yield
            s2c["a"] = g + 1
            yield

    def s2_norm():
        i3 = 0
        for g in range(NG):
            while s2c["a"] <= g:
                yield
            fo = fmo[g % 2]
            for blk in range(8):
                s_ = sl[blk]
                q_ = sq[i3 % 3]
                r_ = rn[i3 % 3]
                p_ = ps[4 + i3 % 3]
                i3 += 1
                k.act(q_, s_, AF.Square)
                k.mm(p_, onesb, q_)
                k.act(r_, p_, AF.Sqrt, bias=eps6)
                yield
                k.recip(r_, r_)
                if blk < 4:
                    k.stt("dve", fo[:, blk, :], s_, 128.0 ** -0.5, r_, ALU.mult, ALU.mult)
                else:
                    k.tt("dve", fo[:, blk, :], s_, r_, ALU.mult)
                yield
            s2c["b"] = g + 1
            yield

    def s2_out():
        for g in range(NG):
            while s2c["b"] <= g:
                yield
            fo = fmo[g % 2]
            cs = slice(g * 512, (g + 1) * 512)
            k.dma(V(QTd[:, :, cs], DB("QT", g)), fo[:, 0:4, :], eng=STQ)
            k.dma(V(KTd[:, :, cs], DB("KT", g)), fo[:, 4:8, :], eng=STQ)
            k.dma(V(MQTd[:, :, cs], DB("MQT", g)), fo[:, 12:14, :], eng=STQ)
            k.dma(V(MKTd[:, :, cs], DB("MKT", g)), fo[:, 14:16, :], eng=STQ)
            for t in range(4):
                ti = g * 4 + t
                rows = slice(ti * 128, (ti + 1) * 128)
                tsl = slice(t * 128, (t + 1) * 128)
                for h in range(4):
                    k.tr(pb[:, h * 128:(h + 1) * 128], fo[:, 4 + h, tsl], identb)
                for h in range(4):
                    k.tr(pb[:, 512 + h * 128:512 + (h + 1) * 128], fo[:, 8 + h, tsl], identb)
                a_ = tmk[(2 * ti) % 4]
                k.cp("act", a_, pb[:, 0:512])
                k.dma(V(Kd[rows, :], DB("Ktm", ti)), a_, eng=STQ)
                b_ = tmk[(2 * ti + 1) % 4]
                k.cp("dve", b_, pb[:, 512:1024])
                k.dma(V(Vd[rows, :], DB("Vtm", ti)), b_, eng=STQ)
                yield
                for b2 in range(2):
                    k.tr(pb[:, b2 * 128:(b2 + 1) * 128], fo[:, 14 + b2, tsl], identb)
                c_ = tmk[(2 * ti) % 4]
                k.cp("act", c_[:, 0:256], pb[:, 0:256])
                k.dma(V(MKd[rows, :], DB("MKtm", ti)), c_[:, 0:256], eng=STQ)
                yield
            s2c["c"] = g + 1
            yield

    if stages >= 2:
        _gens = [s2_conv(), s2_norm(), s2_out()]
        while _gens:
            for _g in list(_gens):
                try:
                    next(_g)
                except StopIteration:
                    _gens.remove(_g)

    stage_end()
    stage_begin()
    OBd = dscr("OB", [S, 1024], F32)

    def mkbufs(sx):
        B = {}

        def a(name, shape, dt):
            B[name] = sb(name + sx, shape, dt)
        for nm in ("qT", "kT", "Kt", "Vt", "Bk0", "Bk1", "kbg", "bv", "vn0", "vn1", "mqp", "Sb16",
                   "ktl0", "ktl1", "nWT0", "nWT1", "qkm0", "qkm1", "qd0", "qd1", "PTm0", "PTm1"):
            a(nm, [128, 4, 128], BF16)
        for nm in ("mqT", "mkT", "mqd0", "mqd1"):
            a(nm, [128, 2, 128], BF16)
        for nm in ("mKt", "wk00", "wk10", "wk01", "wk11"):
            a(nm, [128, 4, 64], BF16)
        for nm in ("CT0", "CT1"):
            a(nm, [128, 4, 2, 128], BF16)
        a("Sf", [128, 4, 128], F32)
        for nm in ("gUh", "gUl", "gXh", "gXl", "fUh", "fUl"):
            a(nm, [128, 4, 128], BF16)
        a("fXh", [128, 4, 64], BF16)
        a("fXl", [128, 4, 64], BF16)
        for nm in ("Eij", "Eji", "tm1", "tm2", "erow", "Wji", "Ut0", "Ut1"):
            a(nm, [128, 512], F32)
        for nm in ("Og0", "Og1", "Om0", "Om1"):
            a(nm, [128, 4, 128], F32)
        a("vaug0", [128, 4, 130], BF16)
        a("vaug1", [128, 4, 130], BF16)
        a("Cf", [128, 4, 130], F32)
        a("Cb16", [128, 4, 130], BF16)
        a("dn", [128, 8], F32)
        for nm in ("vn0", "vn1", "wk00", "wk10", "wk01", "wk11", "mqp"):
            k.memset("pool", B[nm], 0.0)
        for nm in ("vaug0", "vaug1"):
            k.memset("pool", B[nm], 1.0)
        B["cl_done"] = 0
        B["sg_done"] = 0
        B["sm_done"] = 0
        off = 0 if sx == "a" else 4
        B["ps"] = [banks[(i + off) % 8] for i in range(7)]
        B["pb"] = pbview(banks[(7 + off) % 8])
        return B

    def bc4(v, n=128, p=128):
        return v.r("p (h o) -> p h o", o=1).bc([p, 4, n])

    def mbc(i, n=128):
        return MF(i)[:, 0:n].r("p (o c) -> p o c", o=1).bc([128, 4, n])

    def fl(v):
        return v.r("p h c -> p (h c)")

    def h4(v):
        return v.r("p (h c) -> p h c", h=4)

    GP = {}
    NEG = {}
    if stages >= 3:
        for d_ in range(2):
            for nm_, mi_ in (("S", 7 + 3 * (1 - d_)), ("V", 5 + 3 * d_)):
                t_ = sb("neg%s%d" % (nm_, d_), [128, 4, 128], BF16)
                k.ts("dve", t_, MF(mi_).r("p (o c) -> p o c", o=1).bc([128, 4, 128]), -1.0, ALU.add, 30000.0, ALU.mult)
                NEG[(nm_, d_)] = t_
        GAt = sb("GAt", [128, NT, 32], F32)
        k.dma(GAt, V(GT.rearrange("(t p) c -> p t c", p=128), Buf("GTall")))
        for nm in ("Gg", "Bt", "IPb", "LF", "GG", "GTt", "CD0", "CD1", "BB", "BTt", "CM0", "CM1", "EG", "ET", "BG", "EW"):
            GP[nm] = sb("gp_" + nm, [128, 2, NT, 4], F32)

        def gcol(c0, d_):
            return GAt[:, :, c0 + 4 * d_:c0 + 4 * d_ + 4]

        def pbc(v):
            return v.r("p (o h) -> p o h", o=1).bc([128, NT, 4])

        def f2(v):
            return v.r("p t h -> p (t h)")
        for d_ in range(2):
            k.tt("dve", GP["Gg"][:, d_], gcol(0, d_), pbc(gpar[:, 8 + 4 * d_:12 + 4 * d_]), ALU.add)
            k.tt("dve", GP["LF"][:, d_], gcol(24, d_), pbc(gpar[:, 24 + 4 * d_:28 + 4 * d_]), ALU.add)
            k.tt("pool", GP["IPb"][:, d_], gcol(16, d_), pbc(gpar[:, 16 + 4 * d_:20 + 4 * d_]), ALU.add)
        for d_ in range(2):
            k.act(GP["Gg"][:, d_], GP["Gg"][:, d_], AF.Exp)
            k.act(GP["LF"][:, d_], GP["LF"][:, d_], AF.Exp, scale=-1.0)
        for d_ in range(2):
            k.act(GP["Gg"][:, d_], GP["Gg"][:, d_], AF.Ln, bias=one1)
            k.act(GP["LF"][:, d_], GP["LF"][:, d_], AF.Ln, bias=one1)
        for d_ in range(2):
            k.act(GP["Bt"][:, d_], gcol(8, d_), AF.Sigmoid)
            k.stt("dve", GP["Gg"][:, d_], GP["Gg"][:, d_], -1.0, pbc(gA[:, 4 * d_:4 * d_ + 4]), ALU.mult, ALU.mult)
            k.ts("dve", GP["LF"][:, d_], GP["LF"][:, d_], -1.0, ALU.mult)
        for nm in ("Ghi", "Glo", "Lhi", "Llo"):
            GP[nm] = sb("gp_" + nm, [128, 2, NT, 4], BF16)
        gtmp = sb("gp_tmp", [128, 2, NT, 4], F32)
        for (s_, h_, l_) in (("Gg", "Ghi", "Glo"), ("LF", "Lhi", "Llo")):
            k.cp("dve", GP[h_], GP[s_])
            k.tt("dve", gtmp, GP[s_], GP[h_], ALU.subtract)
            k.cp("dve", GP[l_], gtmp)
        for d_ in range(2):
            for si, (srcn, dsts) in enumerate((("Gg", ("GG", "GTt", "CD0", "CD1")), ("LF", ("BB", "BTt", "CM0", "CM1")))):
                bank = ps[2 * d_ + si]
                rhs_ = f2(GP[srcn][:, d_])
                for mi, msk in enumerate((5 + 3 * d_, BLK, SEL0, SEL0 + 1)):
                    k.mm(bank[:, mi * 128:(mi + 1) * 128], MF(msk), rhs_)
                k.cp("dve", f2(GP[dsts[0]][:, d_]), bank[:, 0:128])
                k.cp("dve", f2(GP[dsts[1]][:, d_]), bank[:, 128:256])
                k.act(f2(GP[dsts[2]][:, d_]), bank[:, 256:384], AF.Exp)
                k.act(f2(GP[dsts[3]][:, d_]), bank[:, 384:512], AF.Exp)
        for d_ in range(2):
            k.act(GP["EG"][:, d_], GP["GG"][:, d_], AF.Exp)
            k.tt("dve", GP["ET"][:, d_], GP["GTt"][:, d_], GP["GG"][:, d_], ALU.subtract)
            k.act(GP["ET"][:, d_], GP["ET"][:, d_], AF.Exp)
            k.tt("dve", GP["BG"][:, d_], GP["Bt"][:, d_], GP["EG"][:, d_], ALU.mult)
            k.tt("dve", GP["EW"][:, d_], GP["BTt"][:, d_], GP["BB"][:, d_], ALU.subtract)
            k.tt("dve", GP["EW"][:, d_], GP["EW"][:, d_], GP["IPb"][:, d_], ALU.add)
            k.act(GP["EW"][:, d_], GP["EW"][:, d_], AF.Exp)

    def cl_gen(d, B):
        ps, pb = B["ps"], B["pb"]
        CUMT, AFTER = 5 + 3 * d, 6 + 3 * d
        STRIJ = 7 + 3 * (1 - d)
        order = list(range(NT)) if d == 0 else list(range(NT - 1, -1, -1))
        Od_mine, Od_other = (OFd, OBd) if d == 0 else (OBd, OFd)
        nm_mine, nm_other = ("OF", "OB") if d == 0 else ("OB", "OF")
        for it, ti in enumerate(order[:S3T]):
            while min(B["sg_done"], B["sm_done"]) < it - 1:
                yield
            sl_ = it % 2
            rows = slice(ti * 128, (ti + 1) * 128)
            cs = rows
            gq = ti // 4
            q_, k_, K_, V_ = B["qT"], B["kT"], B["Kt"], B["Vt"]
            mq_, mk_, mK_ = B["mqT"], B["mkT"], B["mKt"]
            k.dma(k_, V(KTd[:, :, cs], DB("KT", gq)))
            k.dma(q_, V(QTd[:, :, cs], DB("QT", gq)))
            k.dma(fl(K_), V(Kd[rows, :], DB("Ktm", ti)))
            k.dma(fl(V_), V(Vd[rows, :], DB("Vtm", ti)))
            k.dma(mq_, V(MQTd[:, :, cs], DB("MQT", gq)))
            k.dma(mk_, V(MKTd[:, :, cs], DB("MKT", gq)))
            k.dma(fl(mK_), V(MKd[rows, :], DB("MKtm", ti)))
            va = B["vaug%d" % sl_]
            k.dma(va[:, :, 0:128], V(MVd[rows, :].rearrange("p (h c) -> p h c", h=4), DB("MV", ti)))
            mqp_ = B["mqp"]
            mqpv = mqp_.r("p (b e) c -> p b e c", e=2)
            k.dma(mqpv[0:64, :, 0, :], V(MQTd[0:64, :, cs], DB("MQT", gq)))
            k.dma(mqpv[64:128, :, 1, :], V(MQTd[64:128, :, cs], DB("MQT", gq)))
            g_c, be_c, bg_c, et_c = GP["Gg"][:, d, ti, :], GP["Bt"][:, d, ti, :], GP["BG"][:, d, ti, :], GP["ET"][:, d, ti, :]
            lf_c, ip_c, ew_c = GP["LF"][:, d, ti, :], GP["IPb"][:, d, ti, :], GP["EW"][:, d, ti, :]
            ghi_c, glo_c = GP["Ghi"][:, d, ti, :], GP["Glo"][:, d, ti, :]
            lhi_c, llo_c = GP["Lhi"][:, d, ti, :], GP["Llo"][:, d, ti, :]
            yield
            def mbcb(i, n=128):
                return MB(i)[:, 0:n].r("p (o c) -> p o c", o=1).bc([128, 4, n])
            gU2 = (B["gUh"], B["gUl"])
            gX2 = (B["gXh"], B["gXl"])
            fU2 = (B["fUh"], B["fUl"])
            fX2 = (B["fXh"], B["fXl"])
            for x_, (gc_, lc_) in enumerate(((ghi_c, lhi_c), (glo_c, llo_c))):
                k.tt("pool", gU2[x_], mbcb(AFTER), bc4(gc_), ALU.mult)
                k.tt("pool", gX2[x_], mbcb(ONES), bc4(gc_), ALU.mult)
                k.tt("pool", fU2[x_], mbcb(AFTER), bc4(lc_), ALU.mult)
                k.tt("pool", fX2[x_], mbcb(ONES, 64), bc4(lc_, 64), ALU.mult)
            yield
            for x_ in range(2):
                k.mm(ps[0], MB(CUMT), fl(gU2[x_]), start=(x_ == 0), stop=(x_ == 1))
            E1 = B["Eij"]
            k.act(E1, ps[0], AF.Exp)
            if FINE2:
                yield
            for h in range(4):
                for x_ in range(2):
                    k.mm(ps[1][:, h * 128:(h + 1) * 128], gU2[x_][:, h, :], MB(CUMT), start=(x_ == 0), stop=(x_ == 1))
            E2 = B["Eji"]
            k.act(E2, ps[1], AF.Exp)
            yield
            for h in range(4):
                for x_ in range(2):
                    k.mm(ps[2][:, h * 128:(h + 1) * 128], gX2[x_][:, h, :], MB(CUMT), start=(x_ == 0), stop=(x_ == 1))
            er = B["erow"]
            k.act(er, ps[2], AF.Exp)
            qd_ = B["qd%d" % sl_]
            k.tt("dve", fl(qd_), fl(q_), er, ALU.mult)
            if FINE2:
                yield
            for h in range(4):
                k.mm(ps[3][:, h * 128:(h + 1) * 128], k_[:, h, :], k_[:, h, :])
            t1 = B["tm1"]
            k.tt("pool", h4(t1), h4(E1), mbc(STRIJ), ALU.mult)
            k.tt("dve", t1, t1, ps[3], ALU.mult)
            B0 = B["Bk0"]
            k.tt("dve", B0, h4(t1), bc4(be_c), ALU.mult)
            yield
            for h in range(4):
                k.tr(pb[:, h * 128:(h + 1) * 128], B0[:, h, :], identb)
            C0 = B["CT0"]
            k.cp("act", C0[:, :, 0, :], h4(pb[:, 0:512]))
            k.tt("dve", C0[:, :, 1, :], MB(IDENT).r("p (o c) -> p o c", o=1).bc([128, 4, 128]), h4(pb[:, 0:512]), ALU.subtract)
            if FINE2:
                yield
            qk_ = B["qkm%d" % sl_]
            for h in range(4):
                k.mm(ps[4][:, h * 128:(h + 1) * 128], k_[:, h, :], q_[:, h, :])
            t2 = B["tm2"]
            k.tt("pool", h4(t2), h4(E2), mbc(CUMT), ALU.mult)
            k.tt("dve", fl(qk_), t2, ps[4], ALU.mult)
            yield
            Bk = [B["Bk0"], B["Bk1"]]
            CT = [B["CT0"], B["CT1"]]
            for h in range(4):
                k.mm(ps[0][:, h * 128:(h + 1) * 128], CT[0][:, h, 0, :], Bk[0][:, h, :])
            for h in range(4):
                k.mm(ps[1][:, h * 128:(h + 1) * 128], Bk[0][:, h, :], CT[0][:, h, 0, :])
            k.cp("act", Bk[1], h4(ps[0]))
            k.cp("dve", CT[1][:, :, 0, :], h4(ps[1]))
            k.cp("pool", CT[1][:, :, 1, :], CT[0][:, :, 1, :])
            yield
            cur = 1
            for m in range(1, 6):
                nxt = 1 - cur
                last = (m == 5)
                for h in range(4):
                    pp = ps[2 + h // 2][:, (h % 2) * 256:(h % 2) * 256 + 256]
                    if last:
                        k.mm(pp[:, 128:256], Bk[cur][:, h, :], CT[cur][:, h, 1, :])
                    else:
                        k.mm(pp, Bk[cur][:, h, :], CT[cur][:, h, :, :].r("p t c -> p (t c)"))
                if not last:
                    if FINE2:
                        yield
                    for h in range(4):
                        k.mm(ps[0][:, h * 128:(h + 1) * 128], CT[cur][:, h, 0, :], Bk[cur][:, h, :])
                    k.cp("act", Bk[nxt], h4(ps[0]))
                for hh2 in range(2):
                    pv = ps[2 + hh2].r("p (h t c) -> p h t c", h=2, t=2)
                    if not last:
                        k.cp("act", CT[nxt][:, 2 * hh2:2 * hh2 + 2, 0, :], pv[:, :, 0, :])
                    k.tt("dve", CT[nxt][:, 2 * hh2:2 * hh2 + 2, 1, :], CT[cur][:, 2 * hh2:2 * hh2 + 2, 1, :], pv[:, :, 1, :], ALU.add)
                cur = nxt
                yield
            TT = CT[cur]
            kb_, bv_, kt_ = B["kbg"], B["bv"], B["ktl%d" % sl_]
            k.tt("pool", kb_, K_, bc4(bg_c), ALU.mult)
            k.tt("pool", bv_, V_, bc4(be_c), ALU.mult)
            k.tt("pool", kt_, K_, bc4(et_c), ALU.mult)
            for h in range(4):
                k.mm(ps[5][:, h * 128:(h + 1) * 128], kb_[:, h, :], TT[:, h, 1, :])
            nW = B["nWT%d" % sl_]
            k.ts("dve", fl(nW), ps[5], -1.0, ALU.mult)
            yield
            for h in range(4):
                for x_ in range(2):
                    k.mm(ps[0][:, h * 128:(h + 1) * 128], fU2[x_][:, h, :], MB(CUMT), start=(x_ == 0), stop=(x_ == 1))
            Wj = B["Wji"]
            for h in range(4):
                k.act(Wj[:, h * 128:(h + 1) * 128], ps[0][:, h * 128:(h + 1) * 128], AF.Exp, bias=ip_c[:, h:h + 1])
            for h in range(4):
                k.mm(ps[1][:, h * 128:(h + 1) * 128], mk_[:, h // 2, :], mqp_[:, h, :])
            k.tt("pool", h4(Wj), h4(Wj), mbc(CUMT), ALU.mult)
            PT_ = B["PTm%d" % sl_]
            k.tt("dve", fl(PT_), Wj, ps[1], ALU.mult)
            yield
            for b2 in range(2):
                for x_ in range(2):
                    k.mm(ps[2][:, b2 * 128:(b2 + 1) * 128], fX2[x_][:, 2 * b2:2 * b2 + 2, :].r("p h c -> p (h c)"), MB(CUMT), start=(x_ == 0), stop=(x_ == 1))
            k.act(er[:, 0:256], ps[2][:, 0:256], AF.Exp)
            mqd_ = B["mqd%d" % sl_]
            k.tt("dve", mqd_.r("p b c -> p (b c)"), mq_.r("p b c -> p (b c)"), er[:, 0:256], ALU.mult)
            wkc = [B["wk0%d" % sl_], B["wk1%d" % sl_]]
            for c in range(2):
                rc = slice(64 * c, 64 * c + 64)
                k.tt("pool", wkc[c][rc], mK_[rc], bc4(ew_c[rc], 64, 64), ALU.mult)
            yield
            U_ = B["Ut%d" % sl_]
            for h in range(4):
                k.mm(ps[3][:, h * 128:(h + 1) * 128], TT[:, h, 1, :], bv_[:, h, :])
            k.cp("act", U_, ps[3])
            for nm_, v_ in (("d_U", U_), ("d_nW", nW), ("d_qd", qd_), ("d_kt", kt_), ("d_PT", PT_), ("d_va", va)):
                dump(nm_, v_, d, it)
            B["cl_done"] = it + 1
            yield

    def scan_g_gen(d, B):
        ps = B["ps"]
        Sf, Sb16 = B["Sf"], B["Sb16"]
        k.memset("dve", Sf, 0.0)
        k.memset("dve", Sb16, 0.0)
        order = list(range(NT)) if d == 0 else list(range(NT - 1, -1, -1))
        Od_mine = OFd if d == 0 else OBd
        nm_mine = "OFg" if d == 0 else "OBg"
        vn = [B["vn0"], B["vn1"]]
        CDg = (GP["CD0"], GP["CD1"])
        for it, ti in enumerate(order[:S3T]):
            while B["cl_done"] <= it:
                yield
            sl_ = it % 2
            rows = slice(ti * 128, (ti + 1) * 128)
            qd_, qk_, kt_, nW = B["qd%d" % sl_], B["qkm%d" % sl_], B["ktl%d" % sl_], B["nWT%d" % sl_]
            U_ = B["Ut%d" % sl_]
            O_ = B["Og%d" % sl_]
            for c in ((0, 1) if d == 0 else (1, 0)):
                rc = slice(64 * c, 64 * c + 64)
                vn_ = vn[c]
                for h in range(4):
                    k.mm(ps[3][:, h * 128:(h + 1) * 128], nW[:, h, :], Sb16[:, h, :])
                k.tt("dve", fl(vn_[rc]), U_[rc], ps[3][rc], ALU.add)
                yield
                for h in range(4):
                    k.mm(ps[4][:, h * 128:(h + 1) * 128], qd_[:, h, :], Sb16[:, h, :], start=True, stop=False)
                    k.mm(ps[4][:, h * 128:(h + 1) * 128], qk_[:, h, :], vn_[:, h, :], start=False, stop=True)
                for h in range(4):
                    k.mm(ps[5][:, h * 128:(h + 1) * 128], kt_[:, h, :], vn_[:, h, :])
                k.cp("act", fl(O_[rc]), ps[4][rc])
                for h in range(4):
                    k.stt("dve", Sf[:, h, :], Sf[:, h, :], CDg[c][:, d, ti, h:h + 1], ps[5][:, h * 128:(h + 1) * 128], ALU.mult, ALU.add)
                k.cp("act", Sb16, Sf)
                yield
            k.dma(V(Od_mine[rows, 0:512], DB(nm_mine, ti)), fl(O_))
            B["sg_done"] = it + 1
            yield

    def scan_m_gen(d, B):
        ps = B["ps"]
        Cf, Cb16 = B["Cf"], B["Cb16"]
        k.memset("pool", Cf, 0.0)
        k.memset("pool", Cb16, 0.0)
        order = list(range(NT)) if d == 0 else list(range(NT - 1, -1, -1))
        Od_mine = OFd if d == 0 else OBd
        nm_mine = "OFm" if d == 0 else "OBm"
        dn_ = B["dn"]
        CDm = (GP["CM0"], GP["CM1"])
        for it, ti in enumerate(order[:S3T]):
            while B["cl_done"] <= it:
                yield
            sl_ = it % 2
            rows = slice(ti * 128, (ti + 1) * 128)
            PT_, mqd_, va = B["PTm%d" % sl_], B["mqd%d" % sl_], B["vaug%d" % sl_]
            wkc = [B["wk0%d" % sl_], B["wk1%d" % sl_]]
            O_ = B["Om%d" % sl_]
            for c in ((0, 1) if d == 0 else (1, 0)):
                rc = slice(64 * c, 64 * c + 64)
                for h in range(4):
                    pp = ps[h // 2][:, (h % 2) * 130:(h % 2) * 130 + 130]
                    k.mm(pp, mqd_[:, h // 2, :], Cb16[:, h, :], start=True, stop=False)
                    k.mm(pp, PT_[:, h, :], va[:, h, :], start=False, stop=True)
                for h in range(4):
                    pp = (ps[2] if h < 2 else ps[6])[:, 32 + (h % 2) * 130:32 + (h % 2) * 130 + 130]
                    k.mm(pp, wkc[c][:, 2 * (h // 2):2 * (h // 2) + 2, :].r("p h c -> p (h c)"), va[:, h, :])
                for b2 in range(2):
                    pv = ps[b2][:, 0:260].r("p (h c) -> p h c", h=2)
                    k.act(dn_[rc, 2 * b2:2 * b2 + 2].r("p (h o) -> p h o", o=1), pv[rc, :, 128:129], AF.Abs)
                k.ts("dve", dn_[rc, 0:4], dn_[rc, 0:4], 1.0, ALU.max)
                k.recip(dn_[rc, 4:8], dn_[rc, 0:4])
                for b2 in range(2):
                    pv = ps[b2][:, 0:260].r("p (h c) -> p h c", h=2)
                    k.tt("dve", O_[rc, 2 * b2:2 * b2 + 2, :], pv[rc, :, 0:128], dn_[rc, 4 + 2 * b2:6 + 2 * b2].r("p (h o) -> p h o", o=1).bc([64, 2, 128]), ALU.mult)
                for h in range(4):
                    pr = slice(64 * (h % 2), 64 * (h % 2) + 64)
                    pp = (ps[2] if h < 2 else ps[6])[:, 32 + (h % 2) * 130:32 + (h % 2) * 130 + 130]
                    k.stt("dve", Cf[pr, h, :], Cf[pr, h, :], CDm[c][pr, d, ti, h:h + 1], pp[pr, :], ALU.mult, ALU.add)
                k.cp("act", Cb16, Cf)
                yield
            k.dma(V(Od_mine[rows, 512:1024], DB(nm_mine, ti)), fl(O_))
            B["sm_done"] = it + 1
            yield

    if stages >= 3:
        Ba, Bb = mkbufs("a"), mkbufs("b")
        gb_ = cl_gen(1, Bb)
        for _ in range(S3OFF):
            next(gb_)
        gens = [(cl_gen(0, Ba), W_CL), (scan_g_gen(0, Ba), W_SC), (scan_m_gen(0, Ba), W_SC), (gb_, W_CL), (scan_g_gen(1, Bb), W_SC), (scan_m_gen(1, Bb), W_SC)]
        if GORD == 1:
            gens = [gens[0], gens[3], gens[1], gens[4], gens[2], gens[5]]
        while gens:
            for gw_ in list(gens):
                g_, n_ = gw_
                for _ in range(n_):
                    try:
                        next(g_)
                    except StopIteration:
                        gens.remove(gw_)
                        break
    print("S3 arena", aoff[0])
    stage_end()
    stage_begin()
    if stages >= 3:
        RD = 3
        ofl = sb("ofl", [128, 8, 128], F32, n=RD)
        obl = sb("obl", [128, 8, 128], F32, n=RD)
        zgl = sb("zgl", [128, 512], BF16, n=RD)
        zol = sb("zol", [128, 512], BF16, n=RD)
        znl = sb("znl", [128, 1024], F32, n=RD)
        cst = sb("cst", [128, 16], F32, n=RD)
        mix = sb("mix", [128, 8, 128], BF16, n=RD)
        mixT = sb("mixT", [128, 8, 128], BF16, n=RD)
        jk2 = sb("jk2", [128, 128], F32)
        def s3b_gen(r2):
            for ti in range(r2, NT, RD):
                rows = slice(ti * 128, (ti + 1) * 128)
                gq = ti // 4
                of_, ob_, zg_, zo_ = ofl[r2], obl[r2], zgl[r2], zol[r2]
                zn_ = znl[r2]
                k.dma(fl(of_), V(OFd[rows, :], DB("OF", ti)))
                k.dma(fl(ob_), V(OBd[rows, :], DB("OB", ti)))
                k.dma(zg_, V(ZG[rows, :], DB("ZG", ti)))
                k.dma(zo_, V(ZO[rows, :], DB("ZO", ti)))
                yield
                k.tt("pool", zn_[:, 0:512], zg_, nw[:, 0:512], ALU.mult)
                k.tt("pool", zn_[:, 512:1024], zo_, nw[:, 512:1024], ALU.mult)
                k.tt("dve", of_, of_, ob_, ALU.add)
                yield
                c_ = cst[r2]
                for h in range(8):
                    k.act(jk2, of_[:, h, :], AF.Square, accum=c_[:, h:h + 1])
                k.act(c_[:, 8:16], c_[:, 0:8], AF.Sqrt, bias=eps6, scale=1.0 / 128)
                yield
                k.recip(c_[:, 0:8], c_[:, 8:16])
                k.tt("dve", of_, of_, c_[:, 0:8].r("p (h o) -> p h o", o=1).bc([128, 8, 128]), ALU.mult)
                mx = mix[r2]
                k.tt("dve", fl(mx), fl(of_), zn_, ALU.mult)
                yield
                for h in range(8):
                    k.tr(pb[:, h * 128:(h + 1) * 128], mx[:, h, :], identb)
                mt = mixT[r2]
                k.cp("act", fl(mt), pb)
                tq = slice((ti % 4) * 128, (ti % 4) * 128 + 128)
                k.dma(V(CINv[gq].rearrange("(h p) t -> p h t", p=128)[:, :, tq], DB("CIN", gq)), mt, eng="pool")
                if CDBG is not None:
                    k.dma(V(CDBG.rearrange("(h p) t -> p h t", p=128)[:, :, rows], DB("CINdbg", 0)), mt, eng="pool")
                if ti % 4 == 3 and stages >= 4:
                    ci_, co_ = CINf[gq], COUTf[gq]
                    P.collective(lambda e, ci_=ci_, co_=co_: e.collective_compute(
                        "AllGather", ALU.bypass, replica_groups=[[0, 1], [2, 3], [4, 5], [6, 7]],
                        ins=[ci_.opt()], outs=[co_.opt()]), [DB("CIN", gq)], [DB("COUT", gq)])
                yield
        _gens = [s3b_gen(0), s3b_gen(1), s3b_gen(2)]
        while _gens:
            for _g in list(_gens):
                try:
                    next(_g)
                except StopIteration:
                    _gens.remove(_g)
    stage_end()
    stage_begin()
    def ring(name, shape, dt, n=2):
        return sb(name, shape, dt, n=n)
    if stages >= 4:
        Wo = sb("Wo", [128, 16, D], BF16)
        wov = wout_in.r("(kc p) c -> p kc c", p=128)
        wst2 = sb("wst2", [128, 2, D], F32, n=2)
        for i in range(8):
            stv = wst2[i % 2]
            k.dma(stv, wov[:, 2 * i:2 * i + 2, :])
            k.cp(("dve", "pool")[i % 2], Wo[:, 2 * i:2 * i + 2, :], stv)
        mxl = ring("mxl", [128, 16, 128], BF16, n=3)
        xr = ring("xr", [128, D], F32, n=3)
        yo = ring("yo", [128, D], F32, n=3)
        c4 = ring("c4", [128, 4], F32, n=3)
        selv = sb("selv", [128, 2], F32)
        k.dma(selv, sel_in)
        mxa = ring("mxa", [128, 16, 128], BF16, n=3)
        mxb = ring("mxb", [128, 16, 128], BF16, n=3)
        coutv = [a.rearrange("(kc p) t -> p kc t", p=128) for a in COUTv]
        evs = []
        def s4_gen(r2):
            pA, pB = ps[2 * r2], ps[2 * r2 + 1]
            for t in range(r2, 16, 3):
                m_ = mxl[r2]
                ma, mb_ = mxa[r2], mxb[r2]
                tq = slice((t % 4) * 128, (t % 4) * 128 + 128)
                k.dma(ma, V(coutv[t // 4][:, :, tq], DB("COUT", t // 4)))
                k.dma(mb_, V(coutv[4 + t // 4][:, :, tq], DB("COUT", 4 + t // 4)))
                x_ = xr[r2]
                k.dma(x_, xh_in[t * 128:(t + 1) * 128, :])
                yield
                k.ts("dve", ma, ma, selv[:, 0:1], ALU.mult)
                k.stt("dve", m_, mb_, selv[:, 1:2], ma, ALU.mult, ALU.add)
                yield
                for nb, pp in ((0, pA), (1, pB)):
                    for kc in range(16):
                        k.mm(pp, m_[:, kc, :], Wo[:, kc, nb * 512:(nb + 1) * 512], start=(kc == 0), stop=(kc == 15))
                c_ = c4[r2]
                y_ = yo[r2]
                for nb, pp in ((0, pA), (1, pB)):
                    k.act(y_[:, nb * 512:(nb + 1) * 512], pp, AF.Square, accum=c_[:, nb:nb + 1])
                k.tt("dve", c_[:, 2:3], c_[:, 0:1], c_[:, 1:2], ALU.add)
                k.act(c_[:, 3:4], c_[:, 2:3], AF.Sqrt, bias=eps6, scale=1.0 / D)
                k.recip(c_[:, 2:3], c_[:, 3:4])
                for nb, pp in ((0, pA), (1, pB)):
                    k.stt("dve", y_[:, nb * 512:(nb + 1) * 512], pp, c_[:, 2:3], nw[:, 1024 + nb * 512:1024 + (nb + 1) * 512], ALU.mult, ALU.mult)
                k.tt("dve", y_, y_, x_, ALU.add)
                evs.append(k.dma(y_out[t * 128:(t + 1) * 128, :], y_, eng=STQ))
                yield
        _gens = [s4_gen(0), s4_gen(1), s4_gen(2)]
        while _gens:
            for _g in list(_gens):
                try:
                    next(_g)
                except StopIteration:
                    _gens.remove(_g)
        for ev in evs:
            P.wait_event("sp", ev)
    else:
        for (name, i), b in list(dbuf.items()):
            if b.lastw is not None:
                P.wait_event("sp", b.lastw)
        if PTb.lastw is not None:
            P.wait_event("sp", PTb.lastw)
    stage_end()
    P.emit()
    return nc


def _masks():
    idx = np.arange(128)
    same = (idx[:, None] // 64) == (idx[None, :] // 64)
    m = np.zeros((12, 128, 128), np.float32)
    m[0] = np.eye(128)
    m[1] = 1.0
    m[2] = same
    m[3] = (idx[:, None] < 64) * np.ones((1, 128))
    m[4] = (idx[:, None] >= 64) * np.ones((1, 128))
    r, c = idx[:, None], idx[None, :]
    for d in range(2):
        le = (r <= c) if d == 0 else (r >= c)
        lt = (r < c) if d == 0 else (r > c)
        gt = (r > c) if d == 0 else (r < c)
        m[5 + 3 * d] = same & le
        m[6 + 3 * d] = same & gt
        m[7 + 3 * d] = same & lt
    return np.ascontiguousarray(m.transpose(1, 0, 2).reshape(128, 12 * 128)).astype(np.float32)


def _core_inputs(c, x, norm_pre_w, w_in, gdn_conv_w, gdn_a_log, gdn_dt_bias, gdn_norm_w,
                 mlstm_conv_w, mlstm_gate_bias, mlstm_norm_w, w_out, norm_post_w):
    b, hh = c // 2, c % 2
    H = [4 * hh + i for i in range(4)]
    hs = np.concatenate([np.arange(h * 128, (h + 1) * 128) for h in H])
    M0 = 4128
    mqk = np.arange(4 * hh * 64, (4 * hh + 4) * 64)
    fm = np.concatenate([hs, 1024 + hs, 2048 + hs, M0 + mqk, M0 + 512 + mqk])
    tm = np.concatenate([3072 + hs, M0 + 1024 + hs, M0 + 2048 + hs, M0 + 3072 + hs])
    h4 = np.array(H)
    gates = np.concatenate([4096 + h4, 4096 + 8 + h4, 4112 + h4, 4112 + 8 + h4,
                            M0 + 4096 + h4, M0 + 4096 + 8 + h4, M0 + 4096 + 16 + h4, M0 + 4096 + 24 + h4])
    cols = np.concatenate([fm, tm, gates])
    W = np.ascontiguousarray(w_in[0][:, cols])
    gch = np.concatenate([hs, 1024 + hs, 2048 + hs])
    mch = np.concatenate([mqk, 512 + mqk])
    cwfull = np.concatenate([gdn_conv_w[0][:, gch], mlstm_conv_w[0][:, mch]], axis=1)
    cw = np.ascontiguousarray(cwfull.reshape(5, NFM, 128).transpose(2, 1, 0).reshape(128, NFM * 5))
    gp = np.zeros((48,), np.float32)
    gp[0:8] = gdn_a_log[0][:, h4].reshape(-1)
    gp[8:16] = gdn_dt_bias[0][:, h4].reshape(-1)
    gp[16:32] = mlstm_gate_bias[0][:, h4].reshape(-1)
    gpar = np.ascontiguousarray(np.broadcast_to(gp[None, :], (128, 48)))
    nwv = np.concatenate([np.tile(gdn_norm_w[0], 4), mlstm_norm_w[0][hs], norm_post_w[0]])
    nw = np.ascontiguousarray(np.broadcast_to(nwv[None, :], (128, 2048)))
    npre = np.ascontiguousarray(norm_pre_w[0].reshape(8, 128).T)
    rows = []
    for r in range(2):
        hr = np.concatenate([np.arange(h * 128, (h + 1) * 128) for h in range(4 * r, 4 * r + 4)])
        rows += [hr, 1024 + hr]
    wo = np.ascontiguousarray(w_out[0][np.concatenate(rows), :])
    sel = np.zeros((128, 2), np.float32)
    sel[:, hh] = 1.0
    return {"x": np.ascontiguousarray(x[b]), "xh": np.ascontiguousarray(x[b, hh * 2048:(hh + 1) * 2048]),
            "w_in": W, "w_out": wo, "npre": npre, "cw": cw, "gpar": gpar, "nw": nw,
            "masks": _masks(), "sel": sel}


def kernel(x, norm_pre_w, w_in, gdn_conv_w, gdn_a_log, gdn_dt_bias, gdn_norm_w,
           mlstm_conv_w, mlstm_gate_bias, mlstm_norm_w, w_out, norm_post_w):
    args = [np.asarray(a, dtype=np.float32) for a in (x, norm_pre_w, w_in, gdn_conv_w, gdn_a_log, gdn_dt_bias, gdn_norm_w,
                                                      mlstm_conv_w, mlstm_gate_bias, mlstm_norm_w, w_out, norm_post_w)]
    nc = build_nc()
    in_maps = [_core_inputs(c, *args) for c in range(8)]
    res = run_bass_kernel_spmd(nc, in_maps, core_ids=list(range(8)))
    out = np.zeros((4, S, D), np.float32)
    for c in range(8):
        b, hh = c // 2, c % 2
        out[b, hh * 2048:(hh + 1) * 2048] = res.results[c]["y"]
    return out
```

```python
import contextlib
import os
CUT = int(os.environ.get('S2CUT', '9'))
S3C = int(os.environ.get('S3C', '99'))
S3T = int(os.environ.get('S3T', '32'))
S3D = int(os.environ.get('S3D', '2'))
W_CL = int(os.environ.get('W_CL', '1'))
W_SC = int(os.environ.get('W_SC', '1'))
S3OFF = int(os.environ.get('S3OFF', '0'))
FINE = int(os.environ.get('FINE', '0'))
FINE2 = int(os.environ.get('FINE2', '0'))
DMAQ = int(os.environ.get('DMAQ', '1'))
STQ = os.environ.get('STQ', 'pool')
XRING = int(os.environ.get('XRING', '4'))
FMEV = int(os.environ.get('FMEV', '1'))
GORD = int(os.environ.get('GORD', '1'))
import numpy as np
import concourse.bass as bass
import concourse.mybir as mybir
from concourse.bass_utils import run_bass_kernel_spmd

F32 = mybir.dt.float32
BF16 = mybir.dt.bfloat16
AF = mybir.ActivationFunctionType
ALU = mybir.AluOpType

S = 4096
D = 1024
NT = S // 128
NG = S // 512
NFM = 16
DEBUG = None


class Buf:
    __slots__ = ("name", "lastw", "readers", "excl")

    def __init__(self, name="", excl=False):
        self.name = name
        self.lastw = None
        self.readers = []
        self.excl = excl


class V:
    __slots__ = ("ap", "buf")

    def __init__(self, ap, buf):
        self.ap = ap
        self.buf = buf

    def __getitem__(self, k):
        return V(self.ap[k], self.buf)

    def r(self, pat, **kw):
        return V(self.ap.rearrange(pat, **kw), self.buf)

    def bc(self, shape):
        return V(self.ap.to_broadcast(shape), self.buf)


class Prog:
    ENGS = ("pe", "act", "dve", "pool", "sp")

    def __init__(self, nc, n_dma_sems=32, self_sync=True):
        self.nc = nc
        self.self_sync = self_sync
        self.ops = {e: [] for e in self.ENGS}
        self.cnt = {e: 0 for e in self.ENGS}
        self.semobjs = {}
        for e in ("pe", "act", "dve", "pool"):
            self.semobjs["E" + e] = nc.alloc_semaphore("sem_" + e)
        self.ndma = n_dma_sems
        for i in range(n_dma_sems):
            self.semobjs["D%d" % i] = nc.alloc_semaphore("sem_dma%d" % i)
        self.semobjs["CC"] = nc.alloc_semaphore("sem_cc")
        self.cc_val = 0
        self.dma_next = 0
        self.dma_val = {("D%d" % i): 0 for i in range(n_dma_sems)}
        self.known = {e: {} for e in self.ENGS}

    def _collect(self, eng, reads, writes):
        waits = {}

        def add(ev):
            if ev is None:
                return
            k, v = ev
            if k == "E" + eng and (not self.self_sync or eng == "pe"):
                return
            if waits.get(k, 0) < v:
                waits[k] = v
        for b in reads:
            add(b.lastw)
            if b.excl:
                for r in b.readers:
                    if r[0] != "E" + eng:
                        add(r)
        for b in writes:
            add(b.lastw)
            for r in b.readers:
                if r[0] == "E" + eng:
                    continue
                add(r)
        out = []
        kn = self.known[eng]
        for k, v in waits.items():
            if kn.get(k, 0) < v:
                kn[k] = v
                out.append((k, v))
        return out

    def _update(self, ev, reads, writes):
        for b in reads:
            b.readers.append(ev)
            if len(b.readers) > 64:
                b.readers = b.readers[-48:]
        for b in writes:
            b.lastw = ev
            b.readers = []

    def op(self, eng, fn, reads=(), writes=()):
        waits = self._collect(eng, reads, writes)
        self.cnt[eng] += 1
        ev = ("E" + eng, self.cnt[eng])
        self.ops[eng].append((waits, fn, ev, 1))
        self._update(ev, reads, writes)
        return ev

    def dma(self, eng, out, in_):
        reads, writes = [in_.buf], [out.buf]
        k = "D%d" % self.dma_next
        self.dma_next = (self.dma_next + 1) % self.ndma
        waits = self._collect(eng, reads, writes)
        prev = self.dma_val[k]
        if prev > 0 and self.known[eng].get(k, 0) < prev:
            self.known[eng][k] = prev
            waits.append((k, prev))
        self.dma_val[k] = prev + 16
        ev = (k, prev + 16)
        oa, ia = out.ap, in_.ap

        def fn(e):
            return e.dma_start(out=oa, in_=ia)
        self.ops[eng].append((waits, fn, ev, 16))
        self._update(ev, reads, writes)
        return ev

    def collective(self, fn, reads, writes):
        waits = self._collect("pool", reads, writes)
        self.cc_val += 1
        ev = ("CC", self.cc_val)
        self.ops["pool"].append((waits, fn, ev, 1))
        self._update(ev, reads, writes)
        return ev

    def wait_event(self, eng, ev):
        k, v = ev
        if self.known[eng].get(k, 0) < v:
            self.known[eng][k] = v
            self.ops[eng].append(([(k, v)], None, None, 0))

    def barrier(self):
        for e in self.ENGS:
            waits = []
            for e2 in ("pe", "act", "dve", "pool"):
                v = self.cnt[e2]
                if e2 != e and v > 0 and self.known[e].get("E" + e2, 0) < v:
                    self.known[e]["E" + e2] = v
                    waits.append(("E" + e2, v))
            for kk, v in self.dma_val.items():
                if v > 0 and self.known[e].get(kk, 0) < v:
                    self.known[e][kk] = v
                    waits.append((kk, v))
            if self.cc_val > 0 and self.known[e].get("CC", 0) < self.cc_val:
                self.known[e]["CC"] = self.cc_val
                waits.append(("CC", self.cc_val))
            if waits:
                self.ops[e].append((waits, None, None, 0))

    def emit(self):
        nc = self.nc
        engmap = {"pe": "tensor", "act": "scalar", "dve": "vector", "pool": "gpsimd", "sp": "sync"}
        with nc.Block() as block:
            for ename in self.ENGS:
                ops = self.ops[ename]
                if not ops:
                    continue

                def body(e, ops=ops):
                    for waits, fn, ev, inc in ops:
                        for k, v in waits:
                            e.wait_ge(self.semobjs[k], v)
                        if fn is None:
                            continue
                        ins = fn(e)
                        ins.then_inc(self.semobjs[ev[0]], inc)
                getattr(block, engmap[ename])(body)
        self.ops = {e: [] for e in self.ENGS}


def _bufs(*vs):
    out = []
    for v in vs:
        if isinstance(v, V) and v.buf not in out:
            out.append(v.buf)
    return out


def _a(v):
    return v.ap if isinstance(v, V) else v


class K:
    def __init__(self, P):
        self.P = P
        self.rr = 0

    def act(self, out, in_, func, bias=None, scale=None, accum=None):
        kw = {}
        if bias is not None:
            kw["bias"] = _a(bias)
        if scale is not None:
            kw["scale"] = _a(scale)
        if accum is not None:
            kw["accum_out"] = _a(accum)
        o, i = out.ap, in_.ap
        self.P.op("act", lambda e: e.activation(out=o, in_=i, func=func, **kw),
                  _bufs(in_, bias, scale), _bufs(out, accum))

    def tt(self, eng, out, in0, in1, op):
        o, a, b = out.ap, in0.ap, in1.ap
        self.P.op(eng, lambda e: e.tensor_tensor(out=o, in0=a, in1=b, op=op), _bufs(in0, in1), _bufs(out))

    def ts(self, eng, out, in0, s1, op0, s2=None, op1=None):
        o, a, x1, x2 = out.ap, in0.ap, _a(s1), _a(s2)
        if op1 is None:
            self.P.op(eng, lambda e: e.tensor_scalar(out=o, in0=a, scalar1=x1, scalar2=None, op0=op0), _bufs(in0, s1), _bufs(out))
        else:
            self.P.op(eng, lambda e: e.tensor_scalar(out=o, in0=a, scalar1=x1, scalar2=x2, op0=op0, op1=op1), _bufs(in0, s1, s2), _bufs(out))

    def stt(self, eng, out, in0, scalar, in1, op0, op1):
        o, a, s, b = out.ap, in0.ap, _a(scalar), in1.ap
        eng = "dve"
        self.P.op(eng, lambda e: e.scalar_tensor_tensor(out=o, in0=a, scalar=s, in1=b, op0=op0, op1=op1), _bufs(in0, scalar, in1), _bufs(out))

    def cp(self, eng, out, in_):
        o, i = out.ap, in_.ap
        if eng == "act":
            self.P.op("act", lambda e: e.activation(out=o, in_=i, func=AF.Copy), _bufs(in_), _bufs(out))
        else:
            self.P.op(eng, lambda e: e.tensor_copy(out=o, in_=i), _bufs(in_), _bufs(out))

    def recip(self, out, in_):
        o, i = out.ap, in_.ap
        self.P.op("dve", lambda e: e.reciprocal(out=o, in_=i), _bufs(in_), _bufs(out))

    def memset(self, eng, out, val):
        o = out.ap
        self.P.op(eng, lambda e: e.memset(o, val), [], _bufs(out))

    def mm(self, out, lhsT, rhs, start=True, stop=True):
        o, l, r = out.ap, lhsT.ap, rhs.ap
        self.P.op("pe", lambda e: e.matmul(o, lhsT=l, rhs=r, start=start, stop=stop), _bufs(lhsT, rhs), _bufs(out))

    def tr(self, out, in_, ident):
        o, i, d = out.ap, in_.ap, ident.ap
        self.P.op("pe", lambda e: e.transpose(o, i, d), _bufs(in_, ident), _bufs(out))

    def dma(self, out, in_, eng=None):
        if eng is None:
            eng = ("sp", "act")[self.rr % 2] if DMAQ == 2 else "sp"
            self.rr += 1
        return self.P.dma(eng, out, in_)


def build_nc(stages=99):
    nc = bass.Bass("TRN2", target_bir_lowering=False)
    P = Prog(nc)
    k = K(P)

    def din(name, shape, dt=F32):
        return V(nc.dram_tensor(name, list(shape), dt, kind="ExternalInput").ap(), Buf(name))

    def dscr(name, shape, dt):
        kind = "ExternalOutput" if (DEBUG and name in DEBUG) else "Internal"
        return nc.dram_tensor(name, list(shape), dt, kind=kind).ap()

    ARENA = 98 * 1024
    arena = nc.alloc_sbuf_tensor("arena", [128, ARENA], BF16)
    aoff = [0, 0]

    def sb(name, shape, dt, n=1):
        vs = []
        for i in range(n):
            ne = 1
            for d_ in shape[1:]:
                ne *= d_
            nb = ne * (2 if dt == F32 else 1)
            nb = (nb + 15) // 16 * 16
            off = aoff[0]
            aoff[0] += nb
            assert aoff[0] <= ARENA, ("arena overflow", name, aoff[0])
            ap = arena[0:shape[0], off:off + ne * (2 if dt == F32 else 1)]
            if dt == F32:
                ap = ap.bitcast(F32)
            if len(shape) == 3:
                ap = ap.rearrange("p (a b) -> p a b", a=shape[1])
            elif len(shape) == 4:
                ap = ap.rearrange("p (a b c) -> p a b c", a=shape[1], b=shape[2])
            vs.append(V(ap, Buf(name)))
        return vs if n > 1 else vs[0]

    def stage_begin():
        aoff[0] = aoff[1]

    def stage_end():
        P.barrier()

    def psum(name, shape, dt=F32):
        return V(nc.alloc_psum_tensor(name, list(shape), dt)[:], Buf(name, excl=True))

    x_in = din("x", [S, D])
    xh_in = din("xh", [S // 2, D])
    win_in = din("w_in", [D, 4128])
    wout_in = din("w_out", [2048, D])
    npre_in = din("npre", [128, 8])
    cwg_in = din("cw", [128, NFM * 5])
    gpar_in = din("gpar", [128, 48])
    nw_in = din("nw", [128, 1024 + 1024])
    msk_in = din("masks", [128, 12 * 128])
    sel_in = din("sel", [128, 2])
    y_out = V(nc.dram_tensor("y", [S // 2, D], F32, kind="ExternalOutput").ap(), Buf("y"))

    PT = dscr("PT", [128, NFM, S + 4], BF16)
    ZG = dscr("ZG", [S, 512], BF16)
    MVd = dscr("MV", [S, 512], BF16)
    ZO = dscr("ZO", [S, 512], BF16)
    GT = dscr("GT", [S, 32], F32)
    QTd = dscr("QT", [128, 4, S], BF16)
    KTd = dscr("KT", [128, 4, S], BF16)
    Kd = dscr("Ktm", [S, 512], BF16)
    Vd = dscr("Vtm", [S, 512], BF16)
    MQTd = dscr("MQT", [128, 2, S], BF16)
    MKTd = dscr("MKT", [128, 2, S], BF16)
    MKd = dscr("MKtm", [S, 256], BF16)
    OFd = dscr("OF", [S, 1024], F32)
    CINf = [nc.dram_tensor("CIN%d" % g, [128, 2048], F32).ap() for g in range(NG)]
    COUTf = [nc.dram_tensor("COUT%d" % g, [256, 2048], F32).ap() for g in range(NG)]
    CINv = [a.bitcast(BF16).rearrange("q (a t) -> (q a) t", a=8) for a in CINf]
    COUTv = [a.bitcast(BF16).rearrange("q (a t) -> (q a) t", a=8) for a in COUTf]
    CDBG = dscr("CIN", [1024, S], BF16) if (DEBUG and "CIN" in DEBUG) else None
    dbuf = {}

    def DB(name, i):
        key = (name, i)
        if key not in dbuf:
            dbuf[key] = Buf("%s%d" % key)
        return dbuf[key]

    DBG_AT = tuple(int(v) for v in os.environ.get("DBGAT", "0,0").split(","))

    def dump(name, v, d, ti):
        if not DEBUG or name not in DEBUG or (d, ti) != DBG_AT:
            return
        t = nc.dram_tensor(name, list(v.ap.shape), v.ap.dtype, kind="ExternalOutput").ap()
        k.dma(V(t, DB(name, 0)), v)

    mskf = sb("mskf", [128, 12 * 128], F32)
    k.dma(mskf, msk_in)
    mskb = sb("mskb", [128, 12 * 128], BF16)
    k.cp("dve", mskb, mskf)

    def MF(i):
        return mskf[:, i * 128:(i + 1) * 128]

    def MB(i):
        return mskb[:, i * 128:(i + 1) * 128]
    IDENT, ONES, BLK, SEL0 = 0, 1, 2, 3
    identb = MB(IDENT)
    npre = sb("npre", [128, 8], F32)
    k.dma(npre, npre_in)
    cw = sb("cw", [128, NFM * 5], F32)
    k.dma(cw, cwg_in)
    gpar = sb("gpar", [128, 48], F32)
    k.dma(gpar, gpar_in)
    nw = sb("nw", [128, 2048], F32)
    k.dma(nw, nw_in)
    gA = sb("gA", [128, 8], F32)
    k.act(gA, gpar[:, 0:8], AF.Exp)
    zeros = sb("zeros", [128, 64], BF16)
    k.memset("pool", zeros, 0.0)
    PTb = Buf("PT")
    PTv = V(PT, PTb)
    k.dma(V(PT[:, :, 0:2], PTb), zeros[:, 0:32].r("p (b t) -> p b t", t=2))
    k.dma(V(PT[:, :, S + 2:S + 4], PTb), zeros[:, 0:32].r("p (b t) -> p b t", t=2))

    eps_t = sb("eps", [128, 2], F32)
    k.memset("pool", eps_t[:, 0:1], 1e-6)
    k.memset("pool", eps_t[:, 1:2], 1.0)
    eps6 = eps_t[:, 0:1]
    one1 = eps_t[:, 1:2]
    aoff[1] = aoff[0]
    stage_begin()
    Wb0 = sb("Wb", [128, 8, 4128], BF16)
    WPC = [(i * 256, (i + 1) * 256) for i in range(8)] + [(2048, 2560), (2560, 3072), (3072, 3584), (3584, 4128)]
    Wbufs = [Buf("Wb%d" % i) for i in range(len(WPC))]

    def Wsl(kc, c0, c1):
        for i, (a0, a1) in enumerate(WPC):
            if a0 <= c0 and c1 <= a1:
                return V(Wb0.ap[:, kc, c0:c1], Wbufs[i])
        raise AssertionError((c0, c1))
    wst = sb("wst", [128, 8, 544], F32, n=2)
    win_v = win_in.r("(kc p) c -> p kc c", p=128)
    for i in range(len(WPC)):
        c0, c1 = WPC[i]
        st = wst[i % 2][:, :, 0:c1 - c0]
        k.dma(st, win_v[:, :, c0:c1])
        k.tt(("dve", "pool")[i % 2], V(Wb0.ap[:, :, c0:c1], Wbufs[i]), st, npre.r("p (k o) -> p k o", o=1).bc([128, 8, c1 - c0]), ALU.mult)

    banks = [psum("bk%d" % i, [128, 512]) for i in range(8)]

    def pbview(v):
        return V(v.ap.bitcast(BF16), v.buf)
    ps = banks[:7]
    pb = pbview(banks[7])

    xt = sb("xt", [128, D], F32, n=XRING)
    junk = sb("junk", [128, D], BF16)
    st1 = sb("st1", [128, 4], F32, n=2)
    hb = sb("hb", [128, D], BF16, n=2)
    hT = sb("hT", [128, 8, 512], BF16, n=2)
    stg = sb("stg", [128, NFM, 512], BF16, n=1)
    stg = [stg, stg]
    tmz = sb("tmz", [128, 512], BF16, n=3)
    tmo = sb("tmo", [128, 512], F32, n=2)
    tmg = sb("tmg", [128, 32], F32, n=2)
    s1c = {"prep": 0, "mm": 0}

    def s1_prep():
        for g in range(NG):
            while s1c["mm"] < g - 1:
                yield
            hTg = hT[g % 2]
            for t in range(4):
                ti = g * 4 + t
                x_ = xt[ti % XRING]
                s_ = st1[ti % 2]
                h_ = hb[ti % 2]
                k.dma(x_, x_in[ti * 128:(ti + 1) * 128, :])
                k.act(junk, x_, AF.Square, accum=s_[:, 0:1])
                k.act(s_[:, 1:2], s_[:, 0:1], AF.Sqrt, bias=eps6, scale=1.0 / D)
                k.recip(s_[:, 2:3], s_[:, 1:2])
                k.ts("dve", h_, x_, s_[:, 2:3], ALU.mult)
                yield
                for kc in range(8):
                    k.tr(pb[:, kc * 128:(kc + 1) * 128], h_[:, kc * 128:(kc + 1) * 128], identb)
                k.cp("act", hTg[:, :, t * 128:(t + 1) * 128], pb.r("p (k c) -> p k c", k=8))
                yield
            s1c["prep"] = g + 1
            yield

    def s1_mm():
        cnt = 0
        for g in range(NG):
            while s1c["prep"] <= g:
                yield
            hTg = hT[g % 2]
            sg = stg[g % 2]
            for blk in range(NFM):
                p_ = ps[blk % 2]
                for kc in range(8):
                    k.mm(p_, Wsl(kc, blk * 128, (blk + 1) * 128), hTg[:, kc, :], start=(kc == 0), stop=(kc == 7))
                k.cp(("act", "dve")[blk % 2] if FMEV == 0 else ("dve" if (FMEV == 1 or blk % 4) else "act"), sg[:, blk, :], p_)
                yield
            k.dma(V(PT[:, :, 2 + g * 512:2 + (g + 1) * 512], PTb), sg, eng=STQ)
            for t in range(4):
                ti = g * 4 + t
                lh = hTg[:, :, t * 128:(t + 1) * 128]
                rows = slice(ti * 128, (ti + 1) * 128)
                for cb in range(4):
                    p_ = ps[2 + cb]
                    for kc in range(8):
                        k.mm(p_, lh[:, kc, :], Wsl(kc, 2048 + cb * 512, 2048 + (cb + 1) * 512), start=(kc == 0), stop=(kc == 7))
                pg = ps[6][:, 0:32]
                for kc in range(8):
                    k.mm(pg, lh[:, kc, :], Wsl(kc, 4096, 4128), start=(kc == 0), stop=(kc == 7))
                z_ = tmz[cnt % 3]; cnt += 1
                k.act(z_, ps[2], AF.Silu)
                k.dma(V(ZG[rows, :], DB("ZG", ti)), z_, eng=STQ)
                z_ = tmz[cnt % 3]; cnt += 1
                k.cp("dve", z_, ps[3])
                k.dma(V(MVd[rows, :], DB("MV", ti)), z_, eng=STQ)
                o_ = tmo[ti % 2]
                k.act(o_, ps[4], AF.Sigmoid)
                o2 = tmo[(ti + 1) % 2]
                k.act(o2, ps[5], AF.Silu)
                z_ = tmz[cnt % 3]; cnt += 1
                k.tt("dve", z_, o_, o2, ALU.mult)
                k.dma(V(ZO[rows, :], DB("ZO", ti)), z_, eng=STQ)
                g_ = tmg[ti % 2]
                k.cp("dve", g_, pg)
                k.dma(V(GT[rows, :], DB("GT", ti)), g_, eng=STQ)
                yield
            s1c["mm"] = g + 1
            yield

    if stages >= 1:
        _gens = [s1_prep(), s1_mm()]
        while _gens:
            for _g in list(_gens):
                try:
                    next(_g)
                except StopIteration:
                    _gens.remove(_g)

    stage_end()
    stage_begin()
    ptg = sb("ptg", [128, NFM, 516], BF16, n=2)
    dg = sb("dg", [128, NFM * 5, 128], BF16)
    for i in range(NFM * 5):
        k.ts(("dve", "pool")[i % 2], dg[:, i, :], MF(IDENT), cw[:, i:i + 1], ALU.mult)
    sl = sb("sl", [128, 512], F32, n=10)
    sq = sb("sq", [128, 512], BF16, n=3)
    rn = sb("rn", [128, 512], F32, n=3)
    fmo = sb("fmo", [128, NFM, 512], BF16, n=2)
    tmk = sb("tmk", [128, 512], BF16, n=4)
    onesb = MB(ONES)
    s2c = {"a": 0, "b": 0, "c": 0}

    def s2_conv():
        for g in range(NG):
            while s2c["b"] < g or s2c["c"] < g - 1:
                yield
            pt_ = ptg[g % 2]
            k.dma(pt_, V(PT[:, :, g * 512:g * 512 + 516], PTb))
            fo = fmo[g % 2]
            for blk in range(NFM):
                a_ = ps[blk % 4]
                for j in range(5):
                    k.mm(a_, dg[:, blk * 5 + j, :], pt_[:, blk, j:j + 512], start=(j == 0), stop=(j == 4))
                if blk < 8:
                    k.act(sl[blk], a_, AF.Silu)
                elif blk < 14:
                    k.act(fo[:, blk, :], a_, AF.Silu)
                else:
                    s_ = sl[8 + blk % 2]
                    k.act(s_, a_, AF.Silu)
                    k.ts("dve", fo[:, blk, :], s_, 0.125, ALU.mult)
                yield
            s2c["a"] = g + 1
            yield

    def s2_norm():
        i3 = 0
        for g in range(NG):
            while s2c["a"] <= g:
                yield
            fo = fmo[g % 2]
            for blk in range(8):
                s_ = sl[blk]
                q_ = sq[i3 % 3]
                r_ = rn[i3 % 3]
                p_ = ps[4 + i3 % 3]
                i3 += 1
                k.act(q_, s_, AF.Square)
                k.mm(p_, onesb, q_)
                k.act(r_, p_, AF.Sqrt, bias=eps6)
                yield
                k.recip(r_, r_)
                if blk < 4:
                    k.stt("dve", fo[:, blk, :], s_, 128.0 ** -0.5, r_, ALU.mult, ALU.mult)
                else:
                    k.tt("dve", fo[:, blk, :], s_, r_, ALU.mult)
                yield
            s2c["b"] = g + 1
            yield

    def s2_out():
        for g in range(NG):
            while s2c["b"] <= g:
                yield
            fo = fmo[g % 2]
            cs = slice(g * 512, (g + 1) * 512)
            k.dma(V(QTd[:, :, cs], DB("QT", g)), fo[:, 0:4, :], eng=STQ)
            k.dma(V(KTd[:, :, cs], DB("KT", g)), fo[:, 4:8, :], eng=STQ)
            k.dma(V(MQTd[:, :, cs], DB("MQT", g)), fo[:, 12:14, :], eng=STQ)
            k.dma(V(MKTd[:, :, cs], DB("MKT", g)), fo[:, 14:16, :], eng=STQ)
            for t in range(4):
                ti = g * 4 + t
                rows = slice(ti * 128, (ti + 1) * 128)
                tsl = slice(t * 128, (t + 1) * 128)
                for h in range(4):
                    k.tr(pb[:, h * 128:(h + 1) * 128], fo[:, 4 + h, tsl], identb)
                for h in range(4):
                    k.tr(pb[:, 512 + h * 128:512 + (h + 1) * 128], fo[:, 8 + h, tsl], identb)
                a_ = tmk[(2 * ti) % 4]
                k.cp("act", a_, pb[:, 0:512])
                k.dma(V(Kd[rows, :], DB("Ktm", ti)), a_, eng=STQ)
                b_ = tmk[(2 * ti + 1) % 4]
                k.cp("dve", b_, pb[:, 512:1024])
                k.dma(V(Vd[rows, :], DB("Vtm", ti)), b_, eng=STQ)
                yield
                for b2 in range(2):
                    k.tr(pb[:, b2 * 128:(b2 + 1) * 128], fo[:, 14 + b2, tsl], identb)
                c_ = tmk[(2 * ti) % 4]
                k.cp("act", c_[:, 0:256], pb[:, 0:256])
                k.dma(V(MKd[rows, :], DB("MKtm", ti)), c_[:, 0:256], eng=STQ)
                yield
            s2c["c"] = g + 1
            yield

    if stages >= 2:
        _gens = [s2_conv(), s2_norm(), s2_out()]
        while _gens:
            for _g in list(_gens):
                try:
                    next(_g)
                except StopIteration:
                    _gens.remove(_g)

    stage_end()
    stage_begin()
    OBd = dscr("OB", [S, 1024], F32)

    def mkbufs(sx):
        B = {}

        def a(name, shape, dt):
            B[name] = sb(name + sx, shape, dt)
        for nm in ("qT", "kT", "Kt", "Vt", "Bk0", "Bk1", "kbg", "bv", "vn0", "vn1", "mqp", "Sb16",
                   "ktl0", "ktl1", "nWT0", "nWT1", "qkm0", "qkm1", "qd0", "qd1", "PTm0", "PTm1"):
            a(nm, [128, 4, 128], BF16)
        for nm in ("mqT", "mkT", "mqd0", "mqd1"):
            a(nm, [128, 2, 128], BF16)
        for nm in ("mKt", "wk00", "wk10", "wk01", "wk11"):
            a(nm, [128, 4, 64], BF16)
        for nm in ("CT0", "CT1"):
            a(nm, [128, 4, 2, 128], BF16)
        a("Sf", [128, 4, 128], F32)
        for nm in ("gUh", "gUl", "gXh", "gXl", "fUh", "fUl"):
            a(nm, [128, 4, 128], BF16)
        a("fXh", [128, 4, 64], BF16)
        a("fXl", [128, 4, 64], BF16)
        for nm in ("Eij", "Eji", "tm1", "tm2", "erow", "Wji", "Ut0", "Ut1"):
            a(nm, [128, 512], F32)
        for nm in ("Og0", "Og1", "Om0", "Om1"):
            a(nm, [128, 4, 128], F32)
        a("vaug0", [128, 4, 130], BF16)
        a("vaug1", [128, 4, 130], BF16)
        a("Cf", [128, 4, 130], F32)
        a("Cb16", [128, 4, 130], BF16)
        a("dn", [128, 8], F32)
        for nm in ("vn0", "vn1", "wk00", "wk10", "wk01", "wk11", "mqp"):
            k.memset("pool", B[nm], 0.0)
        for nm in ("vaug0", "vaug1"):
            k.memset("pool", B[nm], 1.0)
        B["cl_done"] = 0
        B["sg_done"] = 0
        B["sm_done"] = 0
        off = 0 if sx == "a" else 4
        B["ps"] = [banks[(i + off) % 8] for i in range(7)]
        B["pb"] = pbview(banks[(7 + off) % 8])
        return B

    def bc4(v, n=128, p=128):
        return v.r("p (h o) -> p h o", o=1).bc([p, 4, n])

    def mbc(i, n=128):
        return MF(i)[:, 0:n].r("p (o c) -> p o c", o=1).bc([128, 4, n])

    def fl(v):
        return v.r("p h c -> p (h c)")

    def h4(v):
        return v.r("p (h c) -> p h c", h=4)

    GP = {}
    NEG = {}
    if stages >= 3:
        for d_ in range(2):
            for nm_, mi_ in (("S", 7 + 3 * (1 - d_)), ("V", 5 + 3 * d_)):
                t_ = sb("neg%s%d" % (nm_, d_), [128, 4, 128], BF16)
                k.ts("dve", t_, MF(mi_).r("p (o c) -> p o c", o=1).bc([128, 4, 128]), -1.0, ALU.add, 30000.0, ALU.mult)
                NEG[(nm_, d_)] = t_
        GAt = sb("GAt", [128, NT, 32], F32)
        k.dma(GAt, V(GT.rearrange("(t p) c -> p t c", p=128), Buf("GTall")))
        for nm in ("Gg", "Bt", "IPb", "LF", "GG", "GTt", "CD0", "CD1", "BB", "BTt", "CM0", "CM1", "EG", "ET", "BG", "EW"):
            GP[nm] = sb("gp_" + nm, [128, 2, NT, 4], F32)

        def gcol(c0, d_):
            return GAt[:, :, c0 + 4 * d_:c0 + 4 * d_ + 4]

        def pbc(v):
            return v.r("p (o h) -> p o h", o=1).bc([128, NT, 4])

        def f2(v):
            return v.r("p t h -> p (t h)")
        for d_ in range(2):
            k.tt("dve", GP["Gg"][:, d_], gcol(0, d_), pbc(gpar[:, 8 + 4 * d_:12 + 4 * d_]), ALU.add)
            k.tt("dve", GP["LF"][:, d_], gcol(24, d_), pbc(gpar[:, 24 + 4 * d_:28 + 4 * d_]), ALU.add)
            k.tt("pool", GP["IPb"][:, d_], gcol(16, d_), pbc(gpar[:, 16 + 4 * d_:20 + 4 * d_]), ALU.add)
        for d_ in range(2):
            k.act(GP["Gg"][:, d_], GP["Gg"][:, d_], AF.Exp)
            k.act(GP["LF"][:, d_], GP["LF"][:, d_], AF.Exp, scale=-1.0)
        for d_ in range(2):
            k.act(GP["Gg"][:, d_], GP["Gg"][:, d_], AF.Ln, bias=one1)
            k.act(GP["LF"][:, d_], GP["LF"][:, d_], AF.Ln, bias=one1)
        for d_ in range(2):
            k.act(GP["Bt"][:, d_], gcol(8, d_), AF.Sigmoid)
            k.stt("dve", GP["Gg"][:, d_], GP["Gg"][:, d_], -1.0, pbc(gA[:, 4 * d_:4 * d_ + 4]), ALU.mult, ALU.mult)
            k.ts("dve", GP["LF"][:, d_], GP["LF"][:, d_], -1.0, ALU.mult)
        for nm in ("Ghi", "Glo", "Lhi", "Llo"):
            GP[nm] = sb("gp_" + nm, [128, 2, NT, 4], BF16)
        gtmp = sb("gp_tmp", [128, 2, NT, 4], F32)
        for (s_, h_, l_) in (("Gg", "Ghi", "Glo"), ("LF", "Lhi", "Llo")):
            k.cp("dve", GP[h_], GP[s_])
            k.tt("dve", gtmp, GP[s_], GP[h_], ALU.subtract)
            k.cp("dve", GP[l_], gtmp)
        for d_ in range(2):
            for si, (srcn, dsts) in enumerate((("Gg", ("GG", "GTt", "CD0", "CD1")), ("LF", ("BB", "BTt", "CM0", "CM1")))):
                bank = ps[2 * d_ + si]
                rhs_ = f2(GP[srcn][:, d_])
                for mi, msk in enumerate((5 + 3 * d_, BLK, SEL0, SEL0 + 1)):
                    k.mm(bank[:, mi * 128:(mi + 1) * 128], MF(msk), rhs_)
                k.cp("dve", f2(GP[dsts[0]][:, d_]), bank[:, 0:128])
                k.cp("dve", f2(GP[dsts[1]][:, d_]), bank[:, 128:256])
                k.act(f2(GP[dsts[2]][:, d_]), bank[:, 256:384], AF.Exp)
                k.act(f2(GP[dsts[3]][:, d_]), bank[:, 384:512], AF.Exp)
        for d_ in range(2):
            k.act(GP["EG"][:, d_], GP["GG"][:, d_], AF.Exp)
            k.tt("dve", GP["ET"][:, d_], GP["GTt"][:, d_], GP["GG"][:, d_], ALU.subtract)
            k.act(GP["ET"][:, d_], GP["ET"][:, d_], AF.Exp)
            k.tt("dve", GP["BG"][:, d_], GP["Bt"][:, d_], GP["EG"][:, d_], ALU.mult)
            k.tt("dve", GP["EW"][:, d_], GP["BTt"][:, d_], GP["BB"][:, d_], ALU.subtract)
            k.tt("dve", GP["EW"][:, d_], GP["EW"][:, d_], GP["IPb"][:, d_], ALU.add)
            k.act(GP["EW"][:, d_], GP["EW"][:, d_], AF.Exp)

    def cl_gen(d, B):
        ps, pb = B["ps"], B["pb"]
        CUMT, AFTER = 5 + 3 * d, 6 + 3 * d
        STRIJ = 7 + 3 * (1 - d)
        order = list(range(NT)) if d == 0 else list(range(NT - 1, -1, -1))
        Od_mine, Od_other = (OFd, OBd) if d == 0 else (OBd, OFd)
        nm_mine, nm_other = ("OF", "OB") if d == 0 else ("OB", "OF")
        for it, ti in enumerate(order[:S3T]):
            while min(B["sg_done"], B["sm_done"]) < it - 1:
                yield
            sl_ = it % 2
            rows = slice(ti * 128, (ti + 1) * 128)
            cs = rows
            gq = ti // 4
            q_, k_, K_, V_ = B["qT"], B["kT"], B["Kt"], B["Vt"]
            mq_, mk_, mK_ = B["mqT"], B["mkT"], B["mKt"]
            k.dma(k_, V(KTd[:, :, cs], DB("KT", gq)))
            k.dma(q_, V(QTd[:, :, cs], DB("QT", gq)))
            k.dma(fl(K_), V(Kd[rows, :], DB("Ktm", ti)))
            k.dma(fl(V_), V(Vd[rows, :], DB("Vtm", ti)))
            k.dma(mq_, V(MQTd[:, :, cs], DB("MQT", gq)))
            k.dma(mk_, V(MKTd[:, :, cs], DB("MKT", gq)))
            k.dma(fl(mK_), V(MKd[rows, :], DB("MKtm", ti)))
            va = B["vaug%d" % sl_]
            k.dma(va[:, :, 0:128], V(MVd[rows, :].rearrange("p (h c) -> p h c", h=4), DB("MV", ti)))
            mqp_ = B["mqp"]
            mqpv = mqp_.r("p (b e) c -> p b e c", e=2)
            k.dma(mqpv[0:64, :, 0, :], V(MQTd[0:64, :, cs], DB("MQT", gq)))
            k.dma(mqpv[64:128, :, 1, :], V(MQTd[64:128, :, cs], DB("MQT", gq)))
            g_c, be_c, bg_c, et_c = GP["Gg"][:, d, ti, :], GP["Bt"][:, d, ti, :], GP["BG"][:, d, ti, :], GP["ET"][:, d, ti, :]
            lf_c, ip_c, ew_c = GP["LF"][:, d, ti, :], GP["IPb"][:, d, ti, :], GP["EW"][:, d, ti, :]
            ghi_c, glo_c = GP["Ghi"][:, d, ti, :], GP["Glo"][:, d, ti, :]
            lhi_c, llo_c = GP["Lhi"][:, d, ti, :], GP["Llo"][:, d, ti, :]
            yield
            def mbcb(i, n=128):
                return MB(i)[:, 0:n].r("p (o c) -> p o c", o=1).bc([128, 4, n])
            gU2 = (B["gUh"], B["gUl"])
            gX2 = (B["gXh"], B["gXl"])
            fU2 = (B["fUh"], B["fUl"])
            fX2 = (B["fXh"], B["fXl"])
            for x_, (gc_, lc_) in enumerate(((ghi_c, lhi_c), (glo_c, llo_c))):
                k.tt("pool", gU2[x_], mbcb(AFTER), bc4(gc_), ALU.mult)
                k.tt("pool", gX2[x_], mbcb(ONES), bc4(gc_), ALU.mult)
                k.tt("pool", fU2[x_], mbcb(AFTER), bc4(lc_), ALU.mult)
                k.tt("pool", fX2[x_], mbcb(ONES, 64), bc4(lc_, 64), ALU.mult)
            yield
            for x_ in range(2):
                k.mm(ps[0], MB(CUMT), fl(gU2[x_]), start=(x_ == 0), stop=(x_ == 1))
            E1 = B["Eij"]
            k.act(E1, ps[0], AF.Exp)
            if FINE2:
                yield
            for h in range(4):
                for x_ in range(2):
                    k.mm(ps[1][:, h * 128:(h + 1) * 128], gU2[x_][:, h, :], MB(CUMT), start=(x_ == 0), stop=(x_ == 1))
            E2 = B["Eji"]
            k.act(E2, ps[1], AF.Exp)
            yield
            for h in range(4):
                for x_ in range(2):
                    k.mm(ps[2][:, h * 128:(h + 1) * 128], gX2[x_][:, h, :], MB(CUMT), start=(x_ == 0), stop=(x_ == 1))
            er = B["erow"]
            k.act(er, ps[2], AF.Exp)
            qd_ = B["qd%d" % sl_]
            k.tt("dve", fl(qd_), fl(q_), er, ALU.mult)
            if FINE2:
                yield
            for h in range(4):
                k.mm(ps[3][:, h * 128:(h + 1) * 128], k_[:, h, :], k_[:, h, :])
            t1 = B["tm1"]
            k.tt("pool", h4(t1), h4(E1), mbc(STRIJ), ALU.mult)
            k.tt("dve", t1, t1, ps[3], ALU.mult)
            B0 = B["Bk0"]
            k.tt("dve", B0, h4(t1), bc4(be_c), ALU.mult)
            yield
            for h in range(4):
                k.tr(pb[:, h * 128:(h + 1) * 128], B0[:, h, :], identb)
            C0 = B["CT0"]
            k.cp("act", C0[:, :, 0, :], h4(pb[:, 0:512]))
            k.tt("dve", C0[:, :, 1, :], MB(IDENT).r("p (o c) -> p o c", o=1).bc([128, 4, 128]), h4(pb[:, 0:512]), ALU.subtract)
            if FINE2:
                yield
            qk_ = B["qkm%d" % sl_]
            for h in range(4):
                k.mm(ps[4][:, h * 128:(h + 1) * 128], k_[:, h, :], q_[:, h, :])
            t2 = B["tm2"]
            k.tt("pool", h4(t2), h4(E2), mbc(CUMT), ALU.mult)
            k.tt("dve", fl(qk_), t2, ps[4], ALU.mult)
            yield
            Bk = [B["Bk0"], B["Bk1"]]
            CT = [B["CT0"], B["CT1"]]
            for h in range(4):
                k.mm(ps[0][:, h * 128:(h + 1) * 128], CT[0][:, h, 0, :], Bk[0][:, h, :])
            for h in range(4):
                k.mm(ps[1][:, h * 128:(h + 1) * 128], Bk[0][:, h, :], CT[0][:, h, 0, :])
            k.cp("act", Bk[1], h4(ps[0]))
            k.cp("dve", CT[1][:, :, 0, :], h4(ps[1]))
            k.cp("pool", CT[1][:, :, 1, :], CT[0][:, :, 1, :])
            yield
            cur = 1
            for m in range(1, 6):
                nxt = 1 - cur
                last = (m == 5)
                for h in range(4):
                    pp = ps[2 + h // 2][:, (h % 2) * 256:(h % 2) * 256 + 256]
                    if last:
                        k.mm(pp[:, 128:256], Bk[cur][:, h, :], CT[cur][:, h, 1, :])
                    else:
                        k.mm(pp, Bk[cur][:, h, :], CT[cur][:, h, :, :].r("p t c -> p (t c)"))
                if not last:
                    if FINE2:
                        yield
                    for h in range(4):
                        k.mm(ps[0][:, h * 128:(h + 1) * 128], CT[cur][:, h, 0, :], Bk[cur][:, h, :])
                    k.cp("act", Bk[nxt], h4(ps[0]))
                for hh2 in range(2):
                    pv = ps[2 + hh2].r("p (h t c) -> p h t c", h=2, t=2)
                    if not last:
                        k.cp("act", CT[nxt][:, 2 * hh2:2 * hh2 + 2, 0, :], pv[:, :, 0, :])
                    k.tt("dve", CT[nxt][:, 2 * hh2:2 * hh2 + 2, 1, :], CT[cur][:, 2 * hh2:2 * hh2 + 2, 1, :], pv[:, :, 1, :], ALU.add)
                cur = nxt
                yield
            TT = CT[cur]
            kb_, bv_, kt_ = B["kbg"], B["bv"], B["ktl%d" % sl_]
            k.tt("pool", kb_, K_, bc4(bg_c), ALU.mult)
            k.tt("pool", bv_, V_, bc4(be_c), ALU.mult)
            k.tt("pool", kt_, K_, bc4(et_c), ALU.mult)
            for h in range(4):
                k.mm(ps[5][:, h * 128:(h + 1) * 128], kb_[:, h, :], TT[:, h, 1, :])
            nW = B["nWT%d" % sl_]
            k.ts("dve", fl(nW), ps[5], -1.0, ALU.mult)
            yield
            for h in range(4):
                for x_ in range(2):
                    k.mm(ps[0][:, h * 128:(h + 1) * 128], fU2[x_][:, h, :], MB(CUMT), start=(x_ == 0), stop=(x_ == 1))
            Wj = B["Wji"]
            for h in range(4):
                k.act(Wj[:, h * 128:(h + 1) * 128], ps[0][:, h * 128:(h + 1) * 128], AF.Exp, bias=ip_c[:, h:h + 1])
            for h in range(4):
                k.mm(ps[1][:, h * 128:(h + 1) * 128], mk_[:, h // 2, :], mqp_[:, h, :])
            k.tt("pool", h4(Wj), h4(Wj), mbc(CUMT), ALU.mult)
            PT_ = B["PTm%d" % sl_]
            k.tt("dve", fl(PT_), Wj, ps[1], ALU.mult)
            yield
            for b2 in range(2):
                for x_ in range(2):
                    k.mm(ps[2][:, b2 * 128:(b2 + 1) * 128], fX2[x_][:, 2 * b2:2 * b2 + 2, :].r("p h c -> p (h c)"), MB(CUMT), start=(x_ == 0), stop=(x_ == 1))
            k.act(er[:, 0:256], ps[2][:, 0:256], AF.Exp)
            mqd_ = B["mqd%d" % sl_]
            k.tt("dve", mqd_.r("p b c -> p (b c)"), mq_.r("p b c -> p (b c)"), er[:, 0:256], ALU.mult)
            wkc = [B["wk0%d" % sl_], B["wk1%d" % sl_]]
            for c in range(2):
                rc = slice(64 * c, 64 * c + 64)
                k.tt("pool", wkc[c][rc], mK_[rc], bc4(ew_c[rc], 64, 64), ALU.mult)
            yield
            U_ = B["Ut%d" % sl_]
            for h in range(4):
                k.mm(ps[3][:, h * 128:(h + 1) * 128], TT[:, h, 1, :], bv_[:, h, :])
            k.cp("act", U_, ps[3])
            for nm_, v_ in (("d_U", U_), ("d_nW", nW), ("d_qd", qd_), ("d_kt", kt_), ("d_PT", PT_), ("d_va", va)):
                dump(nm_, v_, d, it)
            B["cl_done"] = it + 1
            yield

    def scan_g_gen(d, B):
        ps = B["ps"]
        Sf, Sb16 = B["Sf"], B["Sb16"]
        k.memset("dve", Sf, 0.0)
        k.memset("dve", Sb16, 0.0)
        order = list(range(NT)) if d == 0 else list(range(NT - 1, -1, -1))
        Od_mine = OFd if d == 0 else OBd
        nm_mine = "OFg" if d == 0 else "OBg"
        vn = [B["vn0"], B["vn1"]]
        CDg = (GP["CD0"], GP["CD1"])
        for it, ti in enumerate(order[:S3T]):
            while B["cl_done"] <= it:
                yield
            sl_ = it % 2
            rows = slice(ti * 128, (ti + 1) * 128)
            qd_, qk_, kt_, nW = B["qd%d" % sl_], B["qkm%d" % sl_], B["ktl%d" % sl_], B["nWT%d" % sl_]
            U_ = B["Ut%d" % sl_]
            O_ = B["Og%d" % sl_]
            for c in ((0, 1) if d == 0 else (1, 0)):
                rc = slice(64 * c, 64 * c + 64)
                vn_ = vn[c]
                for h in range(4):
                    k.mm(ps[3][:, h * 128:(h + 1) * 128], nW[:, h, :], Sb16[:, h, :])
                k.tt("dve", fl(vn_[rc]), U_[rc], ps[3][rc], ALU.add)
                yield
                for h in range(4):
                    k.mm(ps[4][:, h * 128:(h + 1) * 128], qd_[:, h, :], Sb16[:, h, :], start=True, stop=False)
                    k.mm(ps[4][:, h * 128:(h + 1) * 128], qk_[:, h, :], vn_[:, h, :], start=False, stop=True)
                for h in range(4):
                    k.mm(ps[5][:, h * 128:(h + 1) * 128], kt_[:, h, :], vn_[:, h, :])
                k.cp("act", fl(O_[rc]), ps[4][rc])
                for h in range(4):
                    k.stt("dve", Sf[:, h, :], Sf[:, h, :], CDg[c][:, d, ti, h:h + 1], ps[5][:, h * 128:(h + 1) * 128], ALU.mult, ALU.add)
                k.cp("act", Sb16, Sf)
                yield
            k.dma(V(Od_mine[rows, 0:512], DB(nm_mine, ti)), fl(O_))
            B["sg_done"] = it + 1
            yield

    def scan_m_gen(d, B):
        ps = B["ps"]
        Cf, Cb16 = B["Cf"], B["Cb16"]
        k.memset("pool", Cf, 0.0)
        k.memset("pool", Cb16, 0.0)
        order = list(range(NT)) if d == 0 else list(range(NT - 1, -1, -1))
        Od_mine = OFd if d == 0 else OBd
        nm_mine = "OFm" if d == 0 else "OBm"
        dn_ = B["dn"]
        CDm = (GP["CM0"], GP["CM1"])
        for it, ti in enumerate(order[:S3T]):
            while B["cl_done"] <= it:
                yield
            sl_ = it % 2
            rows = slice(ti * 128, (ti + 1) * 128)
            PT_, mqd_, va = B["PTm%d" % sl_], B["mqd%d" % sl_], B["vaug%d" % sl_]
            wkc = [B["wk0%d" % sl_], B["wk1%d" % sl_]]
            O_ = B["Om%d" % sl_]
            for c in ((0, 1) if d == 0 else (1, 0)):
                rc = slice(64 * c, 64 * c + 64)
                for h in range(4):
                    pp = ps[h // 2][:, (h % 2) * 130:(h % 2) * 130 + 130]
                    k.mm(pp, mqd_[:, h // 2, :], Cb16[:, h, :], start=True, stop=False)
                    k.mm(pp, PT_[:, h, :], va[:, h, :], start=False, stop=True)
                for h in range(4):
                    pp = (ps[2] if h < 2 else ps[6])[:, 32 + (h % 2) * 130:32 + (h % 2) * 130 + 130]
                    k.mm(pp, wkc[c][:, 2 * (h // 2):2 * (h // 2) + 2, :].r("p h c -> p (h c)"), va[:, h, :])
                for b2 in range(2):
                    pv = ps[b2][:, 0:260].r("p (h c) -> p h c", h=2)
                    k.act(dn_[rc, 2 * b2:2 * b2 + 2].r("p (h o) -> p h o", o=1), pv[rc, :, 128:129], AF.Abs)
                k.ts("dve", dn_[rc, 0:4], dn_[rc, 0:4], 1.0, ALU.max)
                k.recip(dn_[rc, 4:8], dn_[rc, 0:4])
                for b2 in range(2):
                    pv = ps[b2][:, 0:260].r("p (h c) -> p h c", h=2)
                    k.tt("dve", O_[rc, 2 * b2:2 * b2 + 2, :], pv[rc, :, 0:128], dn_[rc, 4 + 2 * b2:6 + 2 * b2].r("p (h o) -> p h o", o=1).bc([64, 2, 128]), ALU.mult)
                for h in range(4):
                    pr = slice(64 * (h % 2), 64 * (h % 2) + 64)
                    pp = (ps[2] if h < 2 else ps[6])[:, 32 + (h % 2) * 130:32 + (h % 2) * 130 + 130]
                    k.stt("dve", Cf[pr, h, :], Cf[pr, h, :], CDm[c][pr, d, ti, h:h + 1], pp[pr, :], ALU.mult, ALU.add)
                k.cp("act", Cb16, Cf)
                yield
            k.dma(V(Od_mine[rows, 512:1024], DB(nm_mine, ti)), fl(O_))
            B["sm_done"] = it + 1
            yield

    if stages >= 3:
        Ba, Bb = mkbufs("a"), mkbufs("b")
        gb_ = cl_gen(1, Bb)
        for _ in range(S3OFF):
            next(gb_)
        gens = [(cl_gen(0, Ba), W_CL), (scan_g_gen(0, Ba), W_SC), (scan_m_gen(0, Ba), W_SC), (gb_, W_CL), (scan_g_gen(1, Bb), W_SC), (scan_m_gen(1, Bb), W_SC)]
        if GORD == 1:
            gens = [gens[0], gens[3], gens[1], gens[4], gens[2], gens[5]]
        while gens:
            for gw_ in list(gens):
                g_, n_ = gw_
                for _ in range(n_):
                    try:
                        next(g_)
                    except StopIteration:
                        gens.remove(gw_)
                        break
    print("S3 arena", aoff[0])
    stage_end()
    stage_begin()
    if stages >= 3:
        RD = 3
        ofl = sb("ofl", [128, 8, 128], F32, n=RD)
        obl = sb("obl", [128, 8, 128], F32, n=RD)
        zgl = sb("zgl", [128, 512], BF16, n=RD)
        zol = sb("zol", [128, 512], BF16, n=RD)
        znl = sb("znl", [128, 1024], F32, n=RD)
        cst = sb("cst", [128, 16], F32, n=RD)
        mix = sb("mix", [128, 8, 128], BF16, n=RD)
        mixT = sb("mixT", [128, 8, 128], BF16, n=RD)
        jk2 = sb("jk2", [128, 128], F32)
        def s3b_gen(r2):
            for ti in range(r2, NT, RD):
                rows = slice(ti * 128, (ti + 1) * 128)
                gq = ti // 4
                of_, ob_, zg_, zo_ = ofl[r2], obl[r2], zgl[r2], zol[r2]
                zn_ = znl[r2]
                k.dma(fl(of_), V(OFd[rows, :], DB("OF", ti)))
                k.dma(fl(ob_), V(OBd[rows, :], DB("OB", ti)))
                k.dma(zg_, V(ZG[rows, :], DB("ZG", ti)))
                k.dma(zo_, V(ZO[rows, :], DB("ZO", ti)))
                yield
                k.tt("pool", zn_[:, 0:512], zg_, nw[:, 0:512], ALU.mult)
                k.tt("pool", zn_[:, 512:1024], zo_, nw[:, 512:1024], ALU.mult)
                k.tt("dve", of_, of_, ob_, ALU.add)
                yield
                c_ = cst[r2]
                for h in range(8):
                    k.act(jk2, of_[:, h, :], AF.Square, accum=c_[:, h:h + 1])
                k.act(c_[:, 8:16], c_[:, 0:8], AF.Sqrt, bias=eps6, scale=1.0 / 128)
                yield
                k.recip(c_[:, 0:8], c_[:, 8:16])
                k.tt("dve", of_, of_, c_[:, 0:8].r("p (h o) -> p h o", o=1).bc([128, 8, 128]), ALU.mult)
                mx = mix[r2]
                k.tt("dve", fl(mx), fl(of_), zn_, ALU.mult)
                yield
                for h in range(8):
                    k.tr(pb[:, h * 128:(h + 1) * 128], mx[:, h, :], identb)
                mt = mixT[r2]
                k.cp("act", fl(mt), pb)
                tq = slice((ti % 4) * 128, (ti % 4) * 128 + 128)
                k.dma(V(CINv[gq].rearrange("(h p) t -> p h t", p=128)[:, :, tq], DB("CIN", gq)), mt, eng="pool")
                if CDBG is not None:
                    k.dma(V(CDBG.rearrange("(h p) t -> p h t", p=128)[:, :, rows], DB("CINdbg", 0)), mt, eng="pool")
                if ti % 4 == 3 and stages >= 4:
                    ci_, co_ = CINf[gq], COUTf[gq]
                    P.collective(lambda e, ci_=ci_, co_=co_: e.collective_compute(
                        "AllGather", ALU.bypass, replica_groups=[[0, 1], [2, 3], [4, 5], [6, 7]],
                        ins=[ci_.opt()], outs=[co_.opt()]), [DB("CIN", gq)], [DB("COUT", gq)])
                yield
        _gens = [s3b_gen(0), s3b_gen(1), s3b_gen(2)]
        while _gens:
            for _g in list(_gens):
                try:
                    next(_g)
                except StopIteration:
                    _gens.remove(_g)
    stage_end()
    stage_begin()
    def ring(name, shape, dt, n=2):
        return sb(name, shape, dt, n=n)
    if stages >= 4:
        Wo = sb("Wo", [128, 16, D], BF16)
        wov = wout_in.r("(kc p) c -> p kc c", p=128)
        wst2 = sb("wst2", [128, 2, D], F32, n=2)
        for i in range(8):
            stv = wst2[i % 2]
            k.dma(stv, wov[:, 2 * i:2 * i + 2, :])
            k.cp(("dve", "pool")[i % 2], Wo[:, 2 * i:2 * i + 2, :], stv)
        mxl = ring("mxl", [128, 16, 128], BF16, n=3)
        xr = ring("xr", [128, D], F32, n=3)
        yo = ring("yo", [128, D], F32, n=3)
        c4 = ring("c4", [128, 4], F32, n=3)
        selv = sb("selv", [128, 2], F32)
        k.dma(selv, sel_in)
        mxa = ring("mxa", [128, 16, 128], BF16, n=3)
        mxb = ring("mxb", [128, 16, 128], BF16, n=3)
        coutv = [a.rearrange("(kc p) t -> p kc t", p=128) for a in COUTv]
        evs = []
        def s4_gen(r2):
            pA, pB = ps[2 * r2], ps[2 * r2 + 1]
            for t in range(r2, 16, 3):
                m_ = mxl[r2]
                ma, mb_ = mxa[r2], mxb[r2]
                tq = slice((t % 4) * 128, (t % 4) * 128 + 128)
                k.dma(ma, V(coutv[t // 4][:, :, tq], DB("COUT", t // 4)))
                k.dma(mb_, V(coutv[4 + t // 4][:, :, tq], DB("COUT", 4 + t // 4)))
                x_ = xr[r2]
                k.dma(x_, xh_in[t * 128:(t + 1) * 128, :])
                yield
                k.ts("dve", ma, ma, selv[:, 0:1], ALU.mult)
                k.stt("dve", m_, mb_, selv[:, 1:2], ma, ALU.mult, ALU.add)
                yield
                for nb, pp in ((0, pA), (1, pB)):
                    for kc in range(16):
                        k.mm(pp, m_[:, kc, :], Wo[:, kc, nb * 512:(nb + 1) * 512], start=(kc == 0), stop=(kc == 15))
                c_ = c4[r2]
                y_ = yo[r2]
                for nb, pp in ((0, pA), (1, pB)):
                    k.act(y_[:, nb * 512:(nb + 1) * 512], pp, AF.Square, accum=c_[:, nb:nb + 1])
                k.tt("dve", c_[:, 2:3], c_[:, 0:1], c_[:, 1:2], ALU.add)
                k.act(c_[:, 3:4], c_[:, 2:3], AF.Sqrt, bias=eps6, scale=1.0 / D)
                k.recip(c_[:, 2:3], c_[:, 3:4])
                for nb, pp in ((0, pA), (1, pB)):
                    k.stt("dve", y_[:, nb * 512:(nb + 1) * 512], pp, c_[:, 2:3], nw[:, 1024 + nb * 512:1024 + (nb + 1) * 512], ALU.mult, ALU.mult)
                k.tt("dve", y_, y_, x_, ALU.add)
                evs.append(k.dma(y_out[t * 128:(t + 1) * 128, :], y_, eng=STQ))
                yield
        _gens = [s4_gen(0), s4_gen(1), s4_gen(2)]
        while _gens:
            for _g in list(_gens):
                try:
                    next(_g)
                except StopIteration:
                    _gens.remove(_g)
        for ev in evs:
            P.wait_event("sp", ev)
    else:
        for (name, i), b in list(dbuf.items()):
            if b.lastw is not None:
                P.wait_event("sp", b.lastw)
        if PTb.lastw is not None:
            P.wait_event("sp", PTb.lastw)
    stage_end()
    P.emit()
    return nc


def _masks():
    idx = np.arange(128)
    same = (idx[:, None] // 64) == (idx[None, :] // 64)
    m = np.zeros((12, 128, 128), np.float32)
    m[0] = np.eye(128)
    m[1] = 1.0
    m[2] = same
    m[3] = (idx[:, None] < 64) * np.ones((1, 128))
    m[4] = (idx[:, None] >= 64) * np.ones((1, 128))
    r, c = idx[:, None], idx[None, :]
    for d in range(2):
        le = (r <= c) if d == 0 else (r >= c)
        lt = (r < c) if d == 0 else (r > c)
        gt = (r > c) if d == 0 else (r < c)
        m[5 + 3 * d] = same & le
        m[6 + 3 * d] = same & gt
        m[7 + 3 * d] = same & lt
    return np.ascontiguousarray(m.transpose(1, 0, 2).reshape(128, 12 * 128)).astype(np.float32)


def _core_inputs(c, x, norm_pre_w, w_in, gdn_conv_w, gdn_a_log, gdn_dt_bias, gdn_norm_w,
                 mlstm_conv_w, mlstm_gate_bias, mlstm_norm_w, w_out, norm_post_w):
    b, hh = c // 2, c % 2
    H = [4 * hh + i for i in range(4)]
    hs = np.concatenate([np.arange(h * 128, (h + 1) * 128) for h in H])
    M0 = 4128
    mqk = np.arange(4 * hh * 64, (4 * hh + 4) * 64)
    fm = np.concatenate([hs, 1024 + hs, 2048 + hs, M0 + mqk, M0 + 512 + mqk])
    tm = np.concatenate([3072 + hs, M0 + 1024 + hs, M0 + 2048 + hs, M0 + 3072 + hs])
    h4 = np.array(H)
    gates = np.concatenate([4096 + h4, 4096 + 8 + h4, 4112 + h4, 4112 + 8 + h4,
                            M0 + 4096 + h4, M0 + 4096 + 8 + h4, M0 + 4096 + 16 + h4, M0 + 4096 + 24 + h4])
    cols = np.concatenate([fm, tm, gates])
    W = np.ascontiguousarray(w_in[0][:, cols])
    gch = np.concatenate([hs, 1024 + hs, 2048 + hs])
    mch = np.concatenate([mqk, 512 + mqk])
    cwfull = np.concatenate([gdn_conv_w[0][:, gch], mlstm_conv_w[0][:, mch]], axis=1)
    cw = np.ascontiguousarray(cwfull.reshape(5, NFM, 128).transpose(2, 1, 0).reshape(128, NFM * 5))
    gp = np.zeros((48,), np.float32)
    gp[0:8] = gdn_a_log[0][:, h4].reshape(-1)
    gp[8:16] = gdn_dt_bias[0][:, h4].reshape(-1)
    gp[16:32] = mlstm_gate_bias[0][:, h4].reshape(-1)
    gpar = np.ascontiguousarray(np.broadcast_to(gp[None, :], (128, 48)))
    nwv = np.concatenate([np.tile(gdn_norm_w[0], 4), mlstm_norm_w[0][hs], norm_post_w[0]])
    nw = np.ascontiguousarray(np.broadcast_to(nwv[None, :], (128, 2048)))
    npre = np.ascontiguousarray(norm_pre_w[0].reshape(8, 128).T)
    rows = []
    for r in range(2):
        hr = np.concatenate([np.arange(h * 128, (h + 1) * 128) for h in range(4 * r, 4 * r + 4)])
        rows += [hr, 1024 + hr]
    wo = np.ascontiguousarray(w_out[0][np.concatenate(rows), :])
    sel = np.zeros((128, 2), np.float32)
    sel[:, hh] = 1.0
    return {"x": np.ascontiguousarray(x[b]), "xh": np.ascontiguousarray(x[b, hh * 2048:(hh + 1) * 2048]),
            "w_in": W, "w_out": wo, "npre": npre, "cw": cw, "gpar": gpar, "nw": nw,
            "masks": _masks(), "sel": sel}


def kernel(x, norm_pre_w, w_in, gdn_conv_w, gdn_a_log, gdn_dt_bias, gdn_norm_w,
           mlstm_conv_w, mlstm_gate_bias, mlstm_norm_w, w_out, norm_post_w):
    args = [np.asarray(a, dtype=np.float32) for a in (x, norm_pre_w, w_in, gdn_conv_w, gdn_a_log, gdn_dt_bias, gdn_norm_w,
                                                      mlstm_conv_w, mlstm_gate_bias, mlstm_norm_w, w_out, norm_post_w)]
    nc = build_nc()
    in_maps = [_core_inputs(c, *args) for c in range(8)]
    res = run_bass_kernel_spmd(nc, in_maps, core_ids=list(range(8)))
    out = np.zeros((4, S, D), np.float32)
    for c in range(8):
        b, hh = c // 2, c % 2
        out[b, hh * 2048:(hh + 1) * 2048] = res.results[c]["y"]
    return out
```

```python
import contextlib
import os
CUT = int(os.environ.get('S2CUT', '9'))
S3C = int(os.environ.get('S3C', '99'))
S3T = int(os.environ.get('S3T', '32'))
S3D = int(os.environ.get('S3D', '2'))
W_CL = int(os.environ.get('W_CL', '1'))
W_SC = int(os.environ.get('W_SC', '1'))
S3OFF = int(os.environ.get('S3OFF', '0'))
FINE = int(os.environ.get('FINE', '0'))
FINE2 = int(os.environ.get('FINE2', '0'))
DMAQ = int(os.environ.get('DMAQ', '1'))
STQ = os.environ.get('STQ', 'pool')
XRING = int(os.environ.get('XRING', '4'))
FMEV = int(os.environ.get('FMEV', '1'))
DVR = int(os.environ.get('DVR', '12'))
GORD = int(os.environ.get('GORD', '1'))
import numpy as np
import concourse.bass as bass
import concourse.mybir as mybir
from concourse.bass_utils import run_bass_kernel_spmd

F32 = mybir.dt.float32
BF16 = mybir.dt.bfloat16
AF = mybir.ActivationFunctionType
ALU = mybir.AluOpType

S = 4096
D = 1024
NT = S // 128
NG = S // 512
NFM = 16
DEBUG = None


class Buf:
    __slots__ = ("name", "lastw", "readers", "excl")

    def __init__(self, name="", excl=False):
        self.name = name
        self.lastw = None
        self.readers = []
        self.excl = excl


class V:
    __slots__ = ("ap", "buf")

    def __init__(self, ap, buf):
        self.ap = ap
        self.buf = buf

    def __getitem__(self, k):
        return V(self.ap[k], self.buf)

    def r(self, pat, **kw):
        return V(self.ap.rearrange(pat, **kw), self.buf)

    def bc(self, shape):
        return V(self.ap.to_broadcast(shape), self.buf)


class Prog:
    ENGS = ("pe", "act", "dve", "pool", "sp")

    def __init__(self, nc, n_dma_sems=32, self_sync=True):
        self.nc = nc
        self.self_sync = self_sync
        self.ops = {e: [] for e in self.ENGS}
        self.cnt = {e: 0 for e in self.ENGS}
        self.semobjs = {}
        for e in ("pe", "act", "dve", "pool"):
            self.semobjs["E" + e] = nc.alloc_semaphore("sem_" + e)
        self.ndma = n_dma_sems
        for i in range(n_dma_sems):
            self.semobjs["D%d" % i] = nc.alloc_semaphore("sem_dma%d" % i)
        self.semobjs["CC"] = nc.alloc_semaphore("sem_cc")
        self.cc_val = 0
        self.dma_next = 0
        self.dma_val = {("D%d" % i): 0 for i in range(n_dma_sems)}
        self.known = {e: {} for e in self.ENGS}

    def _collect(self, eng, reads, writes):
        waits = {}

        def add(ev):
            if ev is None:
                return
            k, v = ev
            if k == "E" + eng and (not self.self_sync or eng == "pe"):
                return
            if waits.get(k, 0) < v:
                waits[k] = v
        for b in reads:
            add(b.lastw)
            if b.excl:
                for r in b.readers:
                    if r[0] != "E" + eng:
                        add(r)
        for b in writes:
            add(b.lastw)
            for r in b.readers:
                if r[0] == "E" + eng:
                    continue
                add(r)
        out = []
        kn = self.known[eng]
        for k, v in waits.items():
            if kn.get(k, 0) < v:
                kn[k] = v
                out.append((k, v))
        return out

    def _update(self, ev, reads, writes):
        for b in reads:
            b.readers.append(ev)
            if len(b.readers) > 64:
                b.readers = b.readers[-48:]
        for b in writes:
            b.lastw = ev
            b.readers = []

    def op(self, eng, fn, reads=(), writes=()):
        waits = self._collect(eng, reads, writes)
        self.cnt[eng] += 1
        ev = ("E" + eng, self.cnt[eng])
        self.ops[eng].append((waits, fn, ev, 1))
        self._update(ev, reads, writes)
        return ev

    def dma(self, eng, out, in_):
        reads, writes = [in_.buf], [out.buf]
        k = "D%d" % self.dma_next
        self.dma_next = (self.dma_next + 1) % self.ndma
        waits = self._collect(eng, reads, writes)
        prev = self.dma_val[k]
        if prev > 0 and self.known[eng].get(k, 0) < prev:
            self.known[eng][k] = prev
            waits.append((k, prev))
        self.dma_val[k] = prev + 16
        ev = (k, prev + 16)
        oa, ia = out.ap, in_.ap

        def fn(e):
            return e.dma_start(out=oa, in_=ia)
        self.ops[eng].append((waits, fn, ev, 16))
        self._update(ev, reads, writes)
        return ev

    def collective(self, fn, reads, writes):
        waits = self._collect("pool", reads, writes)
        self.cc_val += 1
        ev = ("CC", self.cc_val)
        self.ops["pool"].append((waits, fn, ev, 1))
        self._update(ev, reads, writes)
        return ev

    def wait_event(self, eng, ev):
        k, v = ev
        if self.known[eng].get(k, 0) < v:
            self.known[eng][k] = v
            self.ops[eng].append(([(k, v)], None, None, 0))

    def barrier(self):
        for e in self.ENGS:
            waits = []
            for e2 in ("pe", "act", "dve", "pool"):
                v = self.cnt[e2]
                if e2 != e and v > 0 and self.known[e].get("E" + e2, 0) < v:
                    self.known[e]["E" + e2] = v
                    waits.append(("E" + e2, v))
            for kk, v in self.dma_val.items():
                if v > 0 and self.known[e].get(kk, 0) < v:
                    self.known[e][kk] = v
                    waits.append((kk, v))
            if self.cc_val > 0 and self.known[e].get("CC", 0) < self.cc_val:
                self.known[e]["CC"] = self.cc_val
                waits.append(("CC", self.cc_val))
            if waits:
                self.ops[e].append((waits, None, None, 0))

    def emit(self):
        nc = self.nc
        engmap = {"pe": "tensor", "act": "scalar", "dve": "vector", "pool": "gpsimd", "sp": "sync"}
        with nc.Block() as block:
            for ename in self.ENGS:
                ops = self.ops[ename]
                if not ops:
                    continue

                def body(e, ops=ops):
                    for waits, fn, ev, inc in ops:
                        for k, v in waits:
                            e.wait_ge(self.semobjs[k], v)
                        if fn is None:
                            continue
                        ins = fn(e)
                        ins.then_inc(self.semobjs[ev[0]], inc)
                getattr(block, engmap[ename])(body)
        self.ops = {e: [] for e in self.ENGS}


def _bufs(*vs):
    out = []
    for v in vs:
        if isinstance(v, V) and v.buf not in out:
            out.append(v.buf)
    return out


def _a(v):
    return v.ap if isinstance(v, V) else v


class K:
    def __init__(self, P):
        self.P = P
        self.rr = 0

    def act(self, out, in_, func, bias=None, scale=None, accum=None):
        kw = {}
        if bias is not None:
            kw["bias"] = _a(bias)
        if scale is not None:
            kw["scale"] = _a(scale)
        if accum is not None:
            kw["accum_out"] = _a(accum)
        o, i = out.ap, in_.ap
        self.P.op("act", lambda e: e.activation(out=o, in_=i, func=func, **kw),
                  _bufs(in_, bias, scale), _bufs(out, accum))

    def tt(self, eng, out, in0, in1, op):
        o, a, b = out.ap, in0.ap, in1.ap
        self.P.op(eng, lambda e: e.tensor_tensor(out=o, in0=a, in1=b, op=op), _bufs(in0, in1), _bufs(out))

    def ts(self, eng, out, in0, s1, op0, s2=None, op1=None):
        o, a, x1, x2 = out.ap, in0.ap, _a(s1), _a(s2)
        if op1 is None:
            self.P.op(eng, lambda e: e.tensor_scalar(out=o, in0=a, scalar1=x1, scalar2=None, op0=op0), _bufs(in0, s1), _bufs(out))
        else:
            self.P.op(eng, lambda e: e.tensor_scalar(out=o, in0=a, scalar1=x1, scalar2=x2, op0=op0, op1=op1), _bufs(in0, s1, s2), _bufs(out))

    def stt(self, eng, out, in0, scalar, in1, op0, op1):
        o, a, s, b = out.ap, in0.ap, _a(scalar), in1.ap
        eng = "dve"
        self.P.op(eng, lambda e: e.scalar_tensor_tensor(out=o, in0=a, scalar=s, in1=b, op0=op0, op1=op1), _bufs(in0, scalar, in1), _bufs(out))

    def cp(self, eng, out, in_):
        o, i = out.ap, in_.ap
        if eng == "act":
            self.P.op("act", lambda e: e.activation(out=o, in_=i, func=AF.Copy), _bufs(in_), _bufs(out))
        else:
            self.P.op(eng, lambda e: e.tensor_copy(out=o, in_=i), _bufs(in_), _bufs(out))

    def recip(self, out, in_):
        o, i = out.ap, in_.ap
        self.P.op("dve", lambda e: e.reciprocal(out=o, in_=i), _bufs(in_), _bufs(out))

    def memset(self, eng, out, val):
        o = out.ap
        self.P.op(eng, lambda e: e.memset(o, val), [], _bufs(out))

    def mm(self, out, lhsT, rhs, start=True, stop=True):
        o, l, r = out.ap, lhsT.ap, rhs.ap
        self.P.op("pe", lambda e: e.matmul(o, lhsT=l, rhs=r, start=start, stop=stop), _bufs(lhsT, rhs), _bufs(out))

    def tr(self, out, in_, ident):
        o, i, d = out.ap, in_.ap, ident.ap
        self.P.op("pe", lambda e: e.transpose(o, i, d), _bufs(in_, ident), _bufs(out))

    def dma(self, out, in_, eng=None):
        if eng is None:
            eng = ("sp", "act")[self.rr % 2] if DMAQ == 2 else "sp"
            self.rr += 1
        return self.P.dma(eng, out, in_)


def build_nc(stages=99):
    nc = bass.Bass("TRN2", target_bir_lowering=False)
    P = Prog(nc)
    k = K(P)

    def din(name, shape, dt=F32):
        return V(nc.dram_tensor(name, list(shape), dt, kind="ExternalInput").ap(), Buf(name))

    def dscr(name, shape, dt):
        kind = "ExternalOutput" if (DEBUG and name in DEBUG) else "Internal"
        return nc.dram_tensor(name, list(shape), dt, kind=kind).ap()

    ARENA = 98 * 1024
    arena = nc.alloc_sbuf_tensor("arena", [128, ARENA], BF16)
    aoff = [0, 0]

    def sb(name, shape, dt, n=1):
        vs = []
        for i in range(n):
            ne = 1
            for d_ in shape[1:]:
                ne *= d_
            nb = ne * (2 if dt == F32 else 1)
            nb = (nb + 15) // 16 * 16
            off = aoff[0]
            aoff[0] += nb
            assert aoff[0] <= ARENA, ("arena overflow", name, aoff[0])
            ap = arena[0:shape[0], off:off + ne * (2 if dt == F32 else 1)]
            if dt == F32:
                ap = ap.bitcast(F32)
            if len(shape) == 3:
                ap = ap.rearrange("p (a b) -> p a b", a=shape[1])
            elif len(shape) == 4:
                ap = ap.rearrange("p (a b c) -> p a b c", a=shape[1], b=shape[2])
            vs.append(V(ap, Buf(name)))
        return vs if n > 1 else vs[0]

    def stage_begin():
        aoff[0] = aoff[1]

    def stage_end():
        P.barrier()

    def psum(name, shape, dt=F32):
        return V(nc.alloc_psum_tensor(name, list(shape), dt)[:], Buf(name, excl=True))

    x_in = din("x", [S, D])
    xh_in = din("xh", [S // 2, D])
    win_in = din("w_in", [D, 4128])
    wout_in = din("w_out", [2048, D])
    npre_in = din("npre", [128, 8])
    cwg_in = din("cw", [128, NFM * 5])
    gpar_in = din("gpar", [128, 48])
    nw_in = din("nw", [128, 1024 + 1024])
    msk_in = din("masks", [128, 12 * 128])
    sel_in = din("sel", [128, 2])
    y_out = V(nc.dram_tensor("y", [S // 2, D], F32, kind="ExternalOutput").ap(), Buf("y"))

    PT = dscr("PT", [128, NFM, S + 4], BF16)
    ZG = dscr("ZG", [S, 512], BF16)
    MVd = dscr("MV", [S, 512], BF16)
    ZO = dscr("ZO", [S, 512], BF16)
    GT = dscr("GT", [S, 32], F32)
    QTd = dscr("QT", [128, 4, S], BF16)
    KTd = dscr("KT", [128, 4, S], BF16)
    Kd = dscr("Ktm", [S, 512], BF16)
    Vd = dscr("Vtm", [S, 512], BF16)
    MQTd = dscr("MQT", [128, 2, S], BF16)
    MKTd = dscr("MKT", [128, 2, S], BF16)
    MKd = dscr("MKtm", [S, 256], BF16)
    OFd = dscr("OF", [S, 1024], F32)
    CINf = [nc.dram_tensor("CIN%d" % g, [128, 2048], F32).ap() for g in range(NG)]
    COUTf = [nc.dram_tensor("COUT%d" % g, [256, 2048], F32).ap() for g in range(NG)]
    CINv = [a.bitcast(BF16).rearrange("q (a t) -> (q a) t", a=8) for a in CINf]
    COUTv = [a.bitcast(BF16).rearrange("q (a t) -> (q a) t", a=8) for a in COUTf]
    CDBG = dscr("CIN", [1024, S], BF16) if (DEBUG and "CIN" in DEBUG) else None
    dbuf = {}

    def DB(name, i):
        key = (name, i)
        if key not in dbuf:
            dbuf[key] = Buf("%s%d" % key)
        return dbuf[key]

    DBG_AT = tuple(int(v) for v in os.environ.get("DBGAT", "0,0").split(","))

    def dump(name, v, d, ti):
        if not DEBUG or name not in DEBUG or (d, ti) != DBG_AT:
            return
        t = nc.dram_tensor(name, list(v.ap.shape), v.ap.dtype, kind="ExternalOutput").ap()
        k.dma(V(t, DB(name, 0)), v)

    mskf = sb("mskf", [128, 12 * 128], F32)
    k.dma(mskf, msk_in)
    mskb = sb("mskb", [128, 12 * 128], BF16)
    k.cp("dve", mskb, mskf)

    def MF(i):
        return mskf[:, i * 128:(i + 1) * 128]

    def MB(i):
        return mskb[:, i * 128:(i + 1) * 128]
    IDENT, ONES, BLK, SEL0 = 0, 1, 2, 3
    identb = MB(IDENT)
    npre = sb("npre", [128, 8], F32)
    k.dma(npre, npre_in)
    cw = sb("cw", [128, NFM * 5], F32)
    k.dma(cw, cwg_in)
    gpar = sb("gpar", [128, 48], F32)
    k.dma(gpar, gpar_in)
    nw = sb("nw", [128, 2048], F32)
    k.dma(nw, nw_in)
    gA = sb("gA", [128, 8], F32)
    k.act(gA, gpar[:, 0:8], AF.Exp)
    zeros = sb("zeros", [128, 64], BF16)
    k.memset("pool", zeros, 0.0)
    PTb = Buf("PT")
    PTv = V(PT, PTb)
    k.dma(V(PT[:, :, 0:2], PTb), zeros[:, 0:32].r("p (b t) -> p b t", t=2))
    k.dma(V(PT[:, :, S + 2:S + 4], PTb), zeros[:, 0:32].r("p (b t) -> p b t", t=2))

    eps_t = sb("eps", [128, 2], F32)
    k.memset("pool", eps_t[:, 0:1], 1e-6)
    k.memset("pool", eps_t[:, 1:2], 1.0)
    eps6 = eps_t[:, 0:1]
    one1 = eps_t[:, 1:2]
    aoff[1] = aoff[0]
    stage_begin()
    Wb0 = sb("Wb", [128, 8, 4128], BF16)
    WPC = [(i * 256, (i + 1) * 256) for i in range(8)] + [(2048, 2560), (2560, 3072), (3072, 3584), (3584, 4128)]
    Wbufs = [Buf("Wb%d" % i) for i in range(len(WPC))]

    def Wsl(kc, c0, c1):
        for i, (a0, a1) in enumerate(WPC):
            if a0 <= c0 and c1 <= a1:
                return V(Wb0.ap[:, kc, c0:c1], Wbufs[i])
        raise AssertionError((c0, c1))
    wst = sb("wst", [128, 8, 544], F32, n=2)
    win_v = win_in.r("(kc p) c -> p kc c", p=128)
    for i in range(len(WPC)):
        c0, c1 = WPC[i]
        st = wst[i % 2][:, :, 0:c1 - c0]
        k.dma(st, win_v[:, :, c0:c1])
        k.tt(("dve", "pool")[i % 2], V(Wb0.ap[:, :, c0:c1], Wbufs[i]), st, npre.r("p (k o) -> p k o", o=1).bc([128, 8, c1 - c0]), ALU.mult)

    banks = [psum("bk%d" % i, [128, 512]) for i in range(8)]

    def pbview(v):
        return V(v.ap.bitcast(BF16), v.buf)
    ps = banks[:7]
    pb = pbview(banks[7])

    xt = sb("xt", [128, D], F32, n=XRING)
    junk = sb("junk", [128, D], BF16)
    st1 = sb("st1", [128, 4], F32, n=2)
    hb = sb("hb", [128, D], BF16, n=2)
    hT = sb("hT", [128, 8, 512], BF16, n=2)
    stg = sb("stg", [128, NFM, 512], BF16, n=1)
    stg = [stg, stg]
    tmz = sb("tmz", [128, 512], BF16, n=3)
    tmo = sb("tmo", [128, 512], F32, n=2)
    tmg = sb("tmg", [128, 32], F32, n=2)
    s1c = {"prep": 0, "mm": 0}

    def s1_prep():
        for g in range(NG):
            while s1c["mm"] < g - 1:
                yield
            hTg = hT[g % 2]
            for t in range(4):
                ti = g * 4 + t
                x_ = xt[ti % XRING]
                s_ = st1[ti % 2]
                h_ = hb[ti % 2]
                k.dma(x_, x_in[ti * 128:(ti + 1) * 128, :])
                k.act(junk, x_, AF.Square, accum=s_[:, 0:1])
                k.act(s_[:, 1:2], s_[:, 0:1], AF.Sqrt, bias=eps6, scale=1.0 / D)
                k.recip(s_[:, 2:3], s_[:, 1:2])
                k.ts("dve", h_, x_, s_[:, 2:3], ALU.mult)
                yield
                for kc in range(8):
                    k.tr(pb[:, kc * 128:(kc + 1) * 128], h_[:, kc * 128:(kc + 1) * 128], identb)
                k.cp("act", hTg[:, :, t * 128:(t + 1) * 128], pb.r("p (k c) -> p k c", k=8))
                yield
            s1c["prep"] = g + 1
            yield

    def s1_mm():
        cnt = 0
        for g in range(NG):
            while s1c["prep"] <= g:
                yield
            hTg = hT[g % 2]
            sg = stg[g % 2]
            for blk in range(NFM):
                p_ = ps[blk % 2]
                for kc in range(8):
                    k.mm(p_, Wsl(kc, blk * 128, (blk + 1) * 128), hTg[:, kc, :], start=(kc == 0), stop=(kc == 7))
                k.cp(("act", "dve")[blk % 2] if FMEV == 0 else ("dve" if (FMEV == 1 or blk % 4) else "act"), sg[:, blk, :], p_)
                yield
            k.dma(V(PT[:, :, 2 + g * 512:2 + (g + 1) * 512], PTb), sg, eng=STQ)
            for t in range(4):
                ti = g * 4 + t
                lh = hTg[:, :, t * 128:(t + 1) * 128]
                rows = slice(ti * 128, (ti + 1) * 128)
                for cb in range(4):
                    p_ = ps[2 + cb]
                    for kc in range(8):
                        k.mm(p_, lh[:, kc, :], Wsl(kc, 2048 + cb * 512, 2048 + (cb + 1) * 512), start=(kc == 0), stop=(kc == 7))
                pg = ps[6][:, 0:32]
                for kc in range(8):
                    k.mm(pg, lh[:, kc, :], Wsl(kc, 4096, 4128), start=(kc == 0), stop=(kc == 7))
                z_ = tmz[cnt % 3]; cnt += 1
                k.act(z_, ps[2], AF.Silu)
                k.dma(V(ZG[rows, :], DB("ZG", ti)), z_, eng=STQ)
                z_ = tmz[cnt % 3]; cnt += 1
                k.cp("dve", z_, ps[3])
                k.dma(V(MVd[rows, :], DB("MV", ti)), z_, eng=STQ)
                o_ = tmo[ti % 2]
                k.act(o_, ps[4], AF.Sigmoid)
                o2 = tmo[(ti + 1) % 2]
                k.act(o2, ps[5], AF.Silu)
                z_ = tmz[cnt % 3]; cnt += 1
                k.tt("dve", z_, o_, o2, ALU.mult)
                k.dma(V(ZO[rows, :], DB("ZO", ti)), z_, eng=STQ)
                g_ = tmg[ti % 2]
                k.cp("dve", g_, pg)
                k.dma(V(GT[rows, :], DB("GT", ti)), g_, eng=STQ)
                yield
            s1c["mm"] = g + 1
            yield

    if stages >= 1:
        _gens = [s1_prep(), s1_mm()]
        while _gens:
            for _g in list(_gens):
                try:
                    next(_g)
                except StopIteration:
                    _gens.remove(_g)

    stage_end()
    stage_begin()
    ptg = sb("ptg", [128, NFM, 516], BF16, n=2)
    dg = sb("dg", [128, NFM * 5, 128], BF16)
    for i in range(NFM * 5):
        k.ts(("dve", "pool")[i % 2], dg[:, i, :], MF(IDENT), cw[:, i:i + 1], ALU.mult)
    sl = sb("sl", [128, 512], F32, n=10)
    sq = sb("sq", [128, 512], BF16, n=3)
    rn = sb("rn", [128, 512], F32, n=3)
    fmo = sb("fmo", [128, NFM, 512], BF16, n=2)
    tmk = sb("tmk", [128, 512], BF16, n=4)
    onesb = MB(ONES)
    s2c = {"a": 0, "b": 0, "c": 0}

    def s2_conv():
        for g in range(NG):
            while s2c["b"] < g or s2c["c"] < g - 1:
                yield
            pt_ = ptg[g % 2]
            k.dma(pt_, V(PT[:, :, g * 512:g * 512 + 516], PTb))
            fo = fmo[g % 2]
            for blk in range(NFM):
                a_ = ps[blk % 4]
                for j in range(5):
                    k.mm(a_, dg[:, blk * 5 + j, :], pt_[:, blk, j:j + 512], start=(j == 0), stop=(j == 4))
                if blk < 8:
                    k.act(sl[blk], a_, AF.Silu)
                elif blk < 14:
                    k.act(fo[:, blk, :], a_, AF.Silu)
                else:
                    s_ = sl[8 + blk % 2]
                    k.act(s_, a_, AF.Silu)
                    k.ts("dve", fo[:, blk, :], s_, 0.125, ALU.mult)
                yield
            s2c["a"] = g + 1
            yield

    def s2_norm():
        i3 = 0
        for g in range(NG):
            while s2c["a"] <= g:
                yield
            fo = fmo[g % 2]
            for blk in range(8):
                s_ = sl[blk]
                q_ = sq[i3 % 3]
                r_ = rn[i3 % 3]
                p_ = ps[4 + i3 % 3]
                i3 += 1
                k.act(q_, s_, AF.Square)
                k.mm(p_, onesb, q_)
                k.act(r_, p_, AF.Sqrt, bias=eps6)
                yield
                k.recip(r_, r_)
                if blk < 4:
                    k.stt("dve", fo[:, blk, :], s_, 128.0 ** -0.5, r_, ALU.mult, ALU.mult)
                else:
                    k.tt("dve", fo[:, blk, :], s_, r_, ALU.mult)
                yield
            s2c["b"] = g + 1
            yield

    def s2_out():
        for g in range(NG):
            while s2c["b"] <= g:
                yield
            fo = fmo[g % 2]
            cs = slice(g * 512, (g + 1) * 512)
            k.dma(V(QTd[:, :, cs], DB("QT", g)), fo[:, 0:4, :], eng=STQ)
            k.dma(V(KTd[:, :, cs], DB("KT", g)), fo[:, 4:8, :], eng=STQ)
            k.dma(V(MQTd[:, :, cs], DB("MQT", g)), fo[:, 12:14, :], eng=STQ)
            k.dma(V(MKTd[:, :, cs], DB("MKT", g)), fo[:, 14:16, :], eng=STQ)
            for t in range(4):
                ti = g * 4 + t
                rows = slice(ti * 128, (ti + 1) * 128)
                tsl = slice(t * 128, (t + 1) * 128)
                for h in range(4):
                    k.tr(pb[:, h * 128:(h + 1) * 128], fo[:, 4 + h, tsl], identb)
                for h in range(4):
                    k.tr(pb[:, 512 + h * 128:512 + (h + 1) * 128], fo[:, 8 + h, tsl], identb)
                a_ = tmk[(2 * ti) % 4]
                k.cp("act", a_, pb[:, 0:512])
                k.dma(V(Kd[rows, :], DB("Ktm", ti)), a_, eng=STQ)
                b_ = tmk[(2 * ti + 1) % 4]
                k.cp("dve", b_, pb[:, 512:1024])
                k.dma(V(Vd[rows, :], DB("Vtm", ti)), b_, eng=STQ)
                yield
                for b2 in range(2):
                    k.tr(pb[:, b2 * 128:(b2 + 1) * 128], fo[:, 14 + b2, tsl], identb)
                c_ = tmk[(2 * ti) % 4]
                k.cp("act", c_[:, 0:256], pb[:, 0:256])
                k.dma(V(MKd[rows, :], DB("MKtm", ti)), c_[:, 0:256], eng=STQ)
                yield
            s2c["c"] = g + 1
            yield

    if stages >= 2:
        _gens = [s2_conv(), s2_norm(), s2_out()]
        while _gens:
            for _g in list(_gens):
                try:
                    next(_g)
                except StopIteration:
                    _gens.remove(_g)

    stage_end()
    stage_begin()
    OBd = dscr("OB", [S, 1024], F32)

    def mkbufs(sx):
        B = {}

        def a(name, shape, dt):
            B[name] = sb(name + sx, shape, dt)
        for nm in ("qT", "kT", "Kt", "Vt", "Bk0", "Bk1", "kbg", "bv", "vn0", "vn1", "mqp", "Sb16",
                   "ktl0", "ktl1", "nWT0", "nWT1", "qkm0", "qkm1", "qd0", "qd1", "PTm0", "PTm1"):
            a(nm, [128, 4, 128], BF16)
        for nm in ("mqT", "mkT", "mqd0", "mqd1"):
            a(nm, [128, 2, 128], BF16)
        for nm in ("mKt", "wk00", "wk10", "wk01", "wk11"):
            a(nm, [128, 4, 64], BF16)
        for nm in ("CT0", "CT1"):
            a(nm, [128, 4, 2, 128], BF16)
        a("Sf", [128, 4, 128], F32)
        for nm in ("gUh", "gUl", "gXh", "gXl", "fUh", "fUl"):
            a(nm, [128, 4, 128], BF16)
        a("fXh", [128, 4, 64], BF16)
        a("fXl", [128, 4, 64], BF16)
        for nm in ("Eij", "Eji", "tm1", "tm2", "erow", "Wji", "Ut0", "Ut1"):
            a(nm, [128, 512], F32)
        for nm in ("Og0", "Og1", "Om0", "Om1"):
            a(nm, [128, 4, 128], F32)
        a("vaug0", [128, 4, 130], BF16)
        a("vaug1", [128, 4, 130], BF16)
        a("Cf", [128, 4, 130], F32)
        a("Cb16", [128, 4, 130], BF16)
        a("dn", [128, 8], F32)
        for nm in ("vn0", "vn1", "wk00", "wk10", "wk01", "wk11", "mqp"):
            k.memset("pool", B[nm], 0.0)
        for nm in ("vaug0", "vaug1"):
            k.memset("pool", B[nm], 1.0)
        B["cl_done"] = 0
        B["sg_done"] = 0
        B["sm_done"] = 0
        off = 0 if sx == "a" else 4
        B["ps"] = [banks[(i + off) % 8] for i in range(7)]
        B["pb"] = pbview(banks[(7 + off) % 8])
        return B

    def bc4(v, n=128, p=128):
        return v.r("p (h o) -> p h o", o=1).bc([p, 4, n])

    def mbc(i, n=128):
        return MF(i)[:, 0:n].r("p (o c) -> p o c", o=1).bc([128, 4, n])

    def fl(v):
        return v.r("p h c -> p (h c)")

    def h4(v):
        return v.r("p (h c) -> p h c", h=4)

    GP = {}
    NEG = {}
    if stages >= 3:
        for d_ in range(2):
            for nm_, mi_ in (("S", 7 + 3 * (1 - d_)), ("V", 5 + 3 * d_)):
                t_ = sb("neg%s%d" % (nm_, d_), [128, 4, 128], BF16)
                k.ts("dve", t_, MF(mi_).r("p (o c) -> p o c", o=1).bc([128, 4, 128]), -1.0, ALU.add, 30000.0, ALU.mult)
                NEG[(nm_, d_)] = t_
        GAt = sb("GAt", [128, NT, 32], F32)
        k.dma(GAt, V(GT.rearrange("(t p) c -> p t c", p=128), Buf("GTall")))
        for nm in ("Gg", "Bt", "IPb", "LF", "GG", "GTt", "CD0", "CD1", "BB", "BTt", "CM0", "CM1", "EG", "ET", "BG", "EW"):
            GP[nm] = sb("gp_" + nm, [128, 2, NT, 4], F32)

        def gcol(c0, d_):
            return GAt[:, :, c0 + 4 * d_:c0 + 4 * d_ + 4]

        def pbc(v):
            return v.r("p (o h) -> p o h", o=1).bc([128, NT, 4])

        def f2(v):
            return v.r("p t h -> p (t h)")
        for d_ in range(2):
            k.tt("dve", GP["Gg"][:, d_], gcol(0, d_), pbc(gpar[:, 8 + 4 * d_:12 + 4 * d_]), ALU.add)
            k.tt("dve", GP["LF"][:, d_], gcol(24, d_), pbc(gpar[:, 24 + 4 * d_:28 + 4 * d_]), ALU.add)
            k.tt("pool", GP["IPb"][:, d_], gcol(16, d_), pbc(gpar[:, 16 + 4 * d_:20 + 4 * d_]), ALU.add)
        for d_ in range(2):
            k.act(GP["Gg"][:, d_], GP["Gg"][:, d_], AF.Exp)
            k.act(GP["LF"][:, d_], GP["LF"][:, d_], AF.Exp, scale=-1.0)
        for d_ in range(2):
            k.act(GP["Gg"][:, d_], GP["Gg"][:, d_], AF.Ln, bias=one1)
            k.act(GP["LF"][:, d_], GP["LF"][:, d_], AF.Ln, bias=one1)
        for d_ in range(2):
            k.act(GP["Bt"][:, d_], gcol(8, d_), AF.Sigmoid)
            k.stt("dve", GP["Gg"][:, d_], GP["Gg"][:, d_], -1.0, pbc(gA[:, 4 * d_:4 * d_ + 4]), ALU.mult, ALU.mult)
            k.ts("dve", GP["LF"][:, d_], GP["LF"][:, d_], -1.0, ALU.mult)
        for nm in ("Ghi", "Glo", "Lhi", "Llo"):
            GP[nm] = sb("gp_" + nm, [128, 2, NT, 4], BF16)
        gtmp = sb("gp_tmp", [128, 2, NT, 4], F32)
        for (s_, h_, l_) in (("Gg", "Ghi", "Glo"), ("LF", "Lhi", "Llo")):
            k.cp("dve", GP[h_], GP[s_])
            k.tt("dve", gtmp, GP[s_], GP[h_], ALU.subtract)
            k.cp("dve", GP[l_], gtmp)
        for d_ in range(2):
            for si, (srcn, dsts) in enumerate((("Gg", ("GG", "GTt", "CD0", "CD1")), ("LF", ("BB", "BTt", "CM0", "CM1")))):
                bank = ps[2 * d_ + si]
                rhs_ = f2(GP[srcn][:, d_])
                for mi, msk in enumerate((5 + 3 * d_, BLK, SEL0, SEL0 + 1)):
                    k.mm(bank[:, mi * 128:(mi + 1) * 128], MF(msk), rhs_)
                k.cp("dve", f2(GP[dsts[0]][:, d_]), bank[:, 0:128])
                k.cp("dve", f2(GP[dsts[1]][:, d_]), bank[:, 128:256])
                k.act(f2(GP[dsts[2]][:, d_]), bank[:, 256:384], AF.Exp)
                k.act(f2(GP[dsts[3]][:, d_]), bank[:, 384:512], AF.Exp)
        for d_ in range(2):
            k.act(GP["EG"][:, d_], GP["GG"][:, d_], AF.Exp)
            k.tt("dve", GP["ET"][:, d_], GP["GTt"][:, d_], GP["GG"][:, d_], ALU.subtract)
            k.act(GP["ET"][:, d_], GP["ET"][:, d_], AF.Exp)
            k.tt("dve", GP["BG"][:, d_], GP["Bt"][:, d_], GP["EG"][:, d_], ALU.mult)
            k.tt("dve", GP["EW"][:, d_], GP["BTt"][:, d_], GP["BB"][:, d_], ALU.subtract)
            k.tt("dve", GP["EW"][:, d_], GP["EW"][:, d_], GP["IPb"][:, d_], ALU.add)
            k.act(GP["EW"][:, d_], GP["EW"][:, d_], AF.Exp)

    def cl_gen(d, B):
        ps, pb = B["ps"], B["pb"]
        CUMT, AFTER = 5 + 3 * d, 6 + 3 * d
        STRIJ = 7 + 3 * (1 - d)
        order = list(range(NT)) if d == 0 else list(range(NT - 1, -1, -1))
        Od_mine, Od_other = (OFd, OBd) if d == 0 else (OBd, OFd)
        nm_mine, nm_other = ("OF", "OB") if d == 0 else ("OB", "OF")
        for it, ti in enumerate(order[:S3T]):
            while min(B["sg_done"], B["sm_done"]) < it - 1:
                yield
            sl_ = it % 2
            rows = slice(ti * 128, (ti + 1) * 128)
            cs = rows
            gq = ti // 4
            q_, k_, K_, V_ = B["qT"], B["kT"], B["Kt"], B["Vt"]
            mq_, mk_, mK_ = B["mqT"], B["mkT"], B["mKt"]
            k.dma(k_, V(KTd[:, :, cs], DB("KT", gq)))
            k.dma(q_, V(QTd[:, :, cs], DB("QT", gq)))
            k.dma(fl(K_), V(Kd[rows, :], DB("Ktm", ti)))
            k.dma(fl(V_), V(Vd[rows, :], DB("Vtm", ti)))
            k.dma(mq_, V(MQTd[:, :, cs], DB("MQT", gq)))
            k.dma(mk_, V(MKTd[:, :, cs], DB("MKT", gq)))
            k.dma(fl(mK_), V(MKd[rows, :], DB("MKtm", ti)))
            va = B["vaug%d" % sl_]
            k.dma(va[:, :, 0:128], V(MVd[rows, :].rearrange("p (h c) -> p h c", h=4), DB("MV", ti)))
            mqp_ = B["mqp"]
            mqpv = mqp_.r("p (b e) c -> p b e c", e=2)
            k.dma(mqpv[0:64, :, 0, :], V(MQTd[0:64, :, cs], DB("MQT", gq)))
            k.dma(mqpv[64:128, :, 1, :], V(MQTd[64:128, :, cs], DB("MQT", gq)))
            g_c, be_c, bg_c, et_c = GP["Gg"][:, d, ti, :], GP["Bt"][:, d, ti, :], GP["BG"][:, d, ti, :], GP["ET"][:, d, ti, :]
            lf_c, ip_c, ew_c = GP["LF"][:, d, ti, :], GP["IPb"][:, d, ti, :], GP["EW"][:, d, ti, :]
            ghi_c, glo_c = GP["Ghi"][:, d, ti, :], GP["Glo"][:, d, ti, :]
            lhi_c, llo_c = GP["Lhi"][:, d, ti, :], GP["Llo"][:, d, ti, :]
            yield
            def mbcb(i, n=128):
                return MB(i)[:, 0:n].r("p (o c) -> p o c", o=1).bc([128, 4, n])
            gU2 = (B["gUh"], B["gUl"])
            gX2 = (B["gXh"], B["gXl"])
            fU2 = (B["fUh"], B["fUl"])
            fX2 = (B["fXh"], B["fXl"])
            for x_, (gc_, lc_) in enumerate(((ghi_c, lhi_c), (glo_c, llo_c))):
                k.tt("pool", gU2[x_], mbcb(AFTER), bc4(gc_), ALU.mult)
                k.tt("pool", gX2[x_], mbcb(ONES), bc4(gc_), ALU.mult)
                k.tt("pool", fU2[x_], mbcb(AFTER), bc4(lc_), ALU.mult)
                k.tt("pool", fX2[x_], mbcb(ONES, 64), bc4(lc_, 64), ALU.mult)
            yield
            for x_ in range(2):
                k.mm(ps[0], MB(CUMT), fl(gU2[x_]), start=(x_ == 0), stop=(x_ == 1))
            E1 = B["Eij"]
            k.act(E1, ps[0], AF.Exp)
            if FINE2:
                yield
            for h in range(4):
                for x_ in range(2):
                    k.mm(ps[1][:, h * 128:(h + 1) * 128], gU2[x_][:, h, :], MB(CUMT), start=(x_ == 0), stop=(x_ == 1))
            E2 = B["Eji"]
            k.act(E2, ps[1], AF.Exp)
            yield
            for h in range(4):
                for x_ in range(2):
                    k.mm(ps[2][:, h * 128:(h + 1) * 128], gX2[x_][:, h, :], MB(CUMT), start=(x_ == 0), stop=(x_ == 1))
            er = B["erow"]
            k.act(er, ps[2], AF.Exp)
            qd_ = B["qd%d" % sl_]
            k.tt("pool" if DVR & 1 else "dve", fl(qd_), fl(q_), er, ALU.mult)
            if FINE2:
                yield
            for h in range(4):
                k.mm(ps[3][:, h * 128:(h + 1) * 128], k_[:, h, :], k_[:, h, :])
            t1 = B["tm1"]
            k.tt("pool", h4(t1), h4(E1), mbc(STRIJ), ALU.mult)
            k.tt("dve", t1, t1, ps[3], ALU.mult)
            B0 = B["Bk0"]
            k.tt("pool" if DVR & 2 else "dve", B0, h4(t1), bc4(be_c), ALU.mult)
            yield
            for h in range(4):
                k.tr(pb[:, h * 128:(h + 1) * 128], B0[:, h, :], identb)
            C0 = B["CT0"]
            k.cp("act", C0[:, :, 0, :], h4(pb[:, 0:512]))
            k.tt("dve", C0[:, :, 1, :], MB(IDENT).r("p (o c) -> p o c", o=1).bc([128, 4, 128]), h4(pb[:, 0:512]), ALU.subtract)
            if FINE2:
                yield
            qk_ = B["qkm%d" % sl_]
            for h in range(4):
                k.mm(ps[4][:, h * 128:(h + 1) * 128], k_[:, h, :], q_[:, h, :])
            t2 = B["tm2"]
            k.tt("pool", h4(t2), h4(E2), mbc(CUMT), ALU.mult)
            k.tt("dve", fl(qk_), t2, ps[4], ALU.mult)
            yield
            Bk = [B["Bk0"], B["Bk1"]]
            CT = [B["CT0"], B["CT1"]]
            for h in range(4):
                k.mm(ps[0][:, h * 128:(h + 1) * 128], CT[0][:, h, 0, :], Bk[0][:, h, :])
            for h in range(4):
                k.mm(ps[1][:, h * 128:(h + 1) * 128], Bk[0][:, h, :], CT[0][:, h, 0, :])
            k.cp("act", Bk[1], h4(ps[0]))
            k.cp("act" if DVR & 4 else "dve", CT[1][:, :, 0, :], h4(ps[1]))
            k.cp("pool", CT[1][:, :, 1, :], CT[0][:, :, 1, :])
            yield
            cur = 1
            for m in range(1, 6):
                nxt = 1 - cur
                last = (m == 5)
                for h in range(4):
                    pp = ps[2 + h // 2][:, (h % 2) * 256:(h % 2) * 256 + 256]
                    if last:
                        k.mm(pp[:, 128:256], Bk[cur][:, h, :], CT[cur][:, h, 1, :])
                    else:
                        k.mm(pp, Bk[cur][:, h, :], CT[cur][:, h, :, :].r("p t c -> p (t c)"))
                if not last:
                    if FINE2:
                        yield
                    for h in range(4):
                        k.mm(ps[0][:, h * 128:(h + 1) * 128], CT[cur][:, h, 0, :], Bk[cur][:, h, :])
                    k.cp("act", Bk[nxt], h4(ps[0]))
                for hh2 in range(2):
                    pv = ps[2 + hh2].r("p (h t c) -> p h t c", h=2, t=2)
                    if not last:
                        k.cp("act", CT[nxt][:, 2 * hh2:2 * hh2 + 2, 0, :], pv[:, :, 0, :])
                    k.tt("dve", CT[nxt][:, 2 * hh2:2 * hh2 + 2, 1, :], CT[cur][:, 2 * hh2:2 * hh2 + 2, 1, :], pv[:, :, 1, :], ALU.add)
                cur = nxt
                yield
            TT = CT[cur]
            kb_, bv_, kt_ = B["kbg"], B["bv"], B["ktl%d" % sl_]
            k.tt("pool", kb_, K_, bc4(bg_c), ALU.mult)
            k.tt("pool", bv_, V_, bc4(be_c), ALU.mult)
            k.tt("pool", kt_, K_, bc4(et_c), ALU.mult)
            for h in range(4):
                k.mm(ps[5][:, h * 128:(h + 1) * 128], kb_[:, h, :], TT[:, h, 1, :])
            nW = B["nWT%d" % sl_]
            if DVR & 8:
                k.act(fl(nW), ps[5], AF.Copy, scale=-1.0)
            else:
                k.ts("dve", fl(nW), ps[5], -1.0, ALU.mult)
            yield
            for h in range(4):
                for x_ in range(2):
                    k.mm(ps[0][:, h * 128:(h + 1) * 128], fU2[x_][:, h, :], MB(CUMT), start=(x_ == 0), stop=(x_ == 1))
            Wj = B["Wji"]
            for h in range(4):
                k.act(Wj[:, h * 128:(h + 1) * 128], ps[0][:, h * 128:(h + 1) * 128], AF.Exp, bias=ip_c[:, h:h + 1])
            for h in range(4):
                k.mm(ps[1][:, h * 128:(h + 1) * 128], mk_[:, h // 2, :], mqp_[:, h, :])
            k.tt("pool", h4(Wj), h4(Wj), mbc(CUMT), ALU.mult)
            PT_ = B["PTm%d" % sl_]
            k.tt("dve", fl(PT_), Wj, ps[1], ALU.mult)
            yield
            for b2 in range(2):
                for x_ in range(2):
                    k.mm(ps[2][:, b2 * 128:(b2 + 1) * 128], fX2[x_][:, 2 * b2:2 * b2 + 2, :].r("p h c -> p (h c)"), MB(CUMT), start=(x_ == 0), stop=(x_ == 1))
            k.act(er[:, 0:256], ps[2][:, 0:256], AF.Exp)
            mqd_ = B["mqd%d" % sl_]
            k.tt("pool" if DVR & 1 else "dve", mqd_.r("p b c -> p (b c)"), mq_.r("p b c -> p (b c)"), er[:, 0:256], ALU.mult)
            wkc = [B["wk0%d" % sl_], B["wk1%d" % sl_]]
            for c in range(2):
                rc = slice(64 * c, 64 * c + 64)
                k.tt("pool", wkc[c][rc], mK_[rc], bc4(ew_c[rc], 64, 64), ALU.mult)
            yield
            U_ = B["Ut%d" % sl_]
            for h in range(4):
                k.mm(ps[3][:, h * 128:(h + 1) * 128], TT[:, h, 1, :], bv_[:, h, :])
            k.cp("act", U_, ps[3])
            for nm_, v_ in (("d_U", U_), ("d_nW", nW), ("d_qd", qd_), ("d_kt", kt_), ("d_PT", PT_), ("d_va", va)):
                dump(nm_, v_, d, it)
            B["cl_done"] = it + 1
            yield

    def scan_g_gen(d, B):
        ps = B["ps"]
        Sf, Sb16 = B["Sf"], B["Sb16"]
        k.memset("dve", Sf, 0.0)
        k.memset("dve", Sb16, 0.0)
        order = list(range(NT)) if d == 0 else list(range(NT - 1, -1, -1))
        Od_mine = OFd if d == 0 else OBd
        nm_mine = "OFg" if d == 0 else "OBg"
        vn = [B["vn0"], B["vn1"]]
        CDg = (GP["CD0"], GP["CD1"])
        for it, ti in enumerate(order[:S3T]):
            while B["cl_done"] <= it:
                yield
            sl_ = it % 2
            rows = slice(ti * 128, (ti + 1) * 128)
            qd_, qk_, kt_, nW = B["qd%d" % sl_], B["qkm%d" % sl_], B["ktl%d" % sl_], B["nWT%d" % sl_]
            U_ = B["Ut%d" % sl_]
            O_ = B["Og%d" % sl_]
            for c in ((0, 1) if d == 0 else (1, 0)):
                rc = slice(64 * c, 64 * c + 64)
                vn_ = vn[c]
                for h in range(4):
                    k.mm(ps[3][:, h * 128:(h + 1) * 128], nW[:, h, :], Sb16[:, h, :])
                k.tt("dve", fl(vn_[rc]), U_[rc], ps[3][rc], ALU.add)
                yield
                for h in range(4):
                    k.mm(ps[4][:, h * 128:(h + 1) * 128], qd_[:, h, :], Sb16[:, h, :], start=True, stop=False)
                    k.mm(ps[4][:, h * 128:(h + 1) * 128], qk_[:, h, :], vn_[:, h, :], start=False, stop=True)
                for h in range(4):
                    k.mm(ps[5][:, h * 128:(h + 1) * 128], kt_[:, h, :], vn_[:, h, :])
                k.cp("act", fl(O_[rc]), ps[4][rc])
                for h in range(4):
                    k.stt("dve", Sf[:, h, :], Sf[:, h, :], CDg[c][:, d, ti, h:h + 1], ps[5][:, h * 128:(h + 1) * 128], ALU.mult, ALU.add)
                k.cp("act", Sb16, Sf)
                yield
            k.dma(V(Od_mine[rows, 0:512], DB(nm_mine, ti)), fl(O_))
            B["sg_done"] = it + 1
            yield

    def scan_m_gen(d, B):
        ps = B["ps"]
        Cf, Cb16 = B["Cf"], B["Cb16"]
        k.memset("pool", Cf, 0.0)
        k.memset("pool", Cb16, 0.0)
        order = list(range(NT)) if d == 0 else list(range(NT - 1, -1, -1))
        Od_mine = OFd if d == 0 else OBd
        nm_mine = "OFm" if d == 0 else "OBm"
        dn_ = B["dn"]
        CDm = (GP["CM0"], GP["CM1"])
        for it, ti in enumerate(order[:S3T]):
            while B["cl_done"] <= it:
                yield
            sl_ = it % 2
            rows = slice(ti * 128, (ti + 1) * 128)
            PT_, mqd_, va = B["PTm%d" % sl_], B["mqd%d" % sl_], B["vaug%d" % sl_]
            wkc = [B["wk0%d" % sl_], B["wk1%d" % sl_]]
            O_ = B["Om%d" % sl_]
            for c in ((0, 1) if d == 0 else (1, 0)):
                rc = slice(64 * c, 64 * c + 64)
                for h in range(4):
                    pp = ps[h // 2][:, (h % 2) * 130:(h % 2) * 130 + 130]
                    k.mm(pp, mqd_[:, h // 2, :], Cb16[:, h, :], start=True, stop=False)
                    k.mm(pp, PT_[:, h, :], va[:, h, :], start=False, stop=True)
                for h in range(4):
                    pp = (ps[2] if h < 2 else ps[6])[:, 32 + (h % 2) * 130:32 + (h % 2) * 130 + 130]
                    k.mm(pp, wkc[c][:, 2 * (h // 2):2 * (h // 2) + 2, :].r("p h c -> p (h c)"), va[:, h, :])
                for b2 in range(2):
                    pv = ps[b2][:, 0:260].r("p (h c) -> p h c", h=2)
                    k.act(dn_[rc, 2 * b2:2 * b2 + 2].r("p (h o) -> p h o", o=1), pv[rc, :, 128:129], AF.Abs)
                k.ts("dve", dn_[rc, 0:4], dn_[rc, 0:4], 1.0, ALU.max)
                k.recip(dn_[rc, 4:8], dn_[rc, 0:4])
                for b2 in range(2):
                    pv = ps[b2][:, 0:260].r("p (h c) -> p h c", h=2)
                    k.tt("dve", O_[rc, 2 * b2:2 * b2 + 2, :], pv[rc, :, 0:128], dn_[rc, 4 + 2 * b2:6 + 2 * b2].r("p (h o) -> p h o", o=1).bc([64, 2, 128]), ALU.mult)
                for h in range(4):
                    pr = slice(64 * (h % 2), 64 * (h % 2) + 64)
                    pp = (ps[2] if h < 2 else ps[6])[:, 32 + (h % 2) * 130:32 + (h % 2) * 130 + 130]
                    k.stt("dve", Cf[pr, h, :], Cf[pr, h, :], CDm[c][pr, d, ti, h:h + 1], pp[pr, :], ALU.mult, ALU.add)
                k.cp("act", Cb16, Cf)
                yield
            k.dma(V(Od_mine[rows, 512:1024], DB(nm_mine, ti)), fl(O_))
            B["sm_done"] = it + 1
            yield

    if stages >= 3:
        Ba, Bb = mkbufs("a"), mkbufs("b")
        gb_ = cl_gen(1, Bb)
        for _ in range(S3OFF):
            next(gb_)
        gens = [(cl_gen(0, Ba), W_CL), (scan_g_gen(0, Ba), W_SC), (scan_m_gen(0, Ba), W_SC), (gb_, W_CL), (scan_g_gen(1, Bb), W_SC), (scan_m_gen(1, Bb), W_SC)]
        if GORD == 1:
            gens = [gens[0], gens[3], gens[1], gens[4], gens[2], gens[5]]
        while gens:
            for gw_ in list(gens):
                g_, n_ = gw_
                for _ in range(n_):
                    try:
                        next(g_)
                    except StopIteration:
                        gens.remove(gw_)
                        break
    print("S3 arena", aoff[0])
    stage_end()
    stage_begin()
    if stages >= 3:
        RD = 3
        ofl = sb("ofl", [128, 8, 128], F32, n=RD)
        obl = sb("obl", [128, 8, 128], F32, n=RD)
        zgl = sb("zgl", [128, 512], BF16, n=RD)
        zol = sb("zol", [128, 512], BF16, n=RD)
        znl = sb("znl", [128, 1024], F32, n=RD)
        cst = sb("cst", [128, 16], F32, n=RD)
        mix = sb("mix", [128, 8, 128], BF16, n=RD)
        mixT = sb("mixT", [128, 8, 128], BF16, n=RD)
        jk2 = sb("jk2", [128, 128], F32)
        def s3b_gen(r2):
            for ti in range(r2, NT, RD):
                rows = slice(ti * 128, (ti + 1) * 128)
                gq = ti // 4
                of_, ob_, zg_, zo_ = ofl[r2], obl[r2], zgl[r2], zol[r2]
                zn_ = znl[r2]
                k.dma(fl(of_), V(OFd[rows, :], DB("OF", ti)))
                k.dma(fl(ob_), V(OBd[rows, :], DB("OB", ti)))
                k.dma(zg_, V(ZG[rows, :], DB("ZG", ti)))
                k.dma(zo_, V(ZO[rows, :], DB("ZO", ti)))
                yield
                k.tt("pool", zn_[:, 0:512], zg_, nw[:, 0:512], ALU.mult)
                k.tt("pool", zn_[:, 512:1024], zo_, nw[:, 512:1024], ALU.mult)
                k.tt("dve", of_, of_, ob_, ALU.add)
                yield
                c_ = cst[r2]
                for h in range(8):
                    k.act(jk2, of_[:, h, :], AF.Square, accum=c_[:, h:h + 1])
                k.act(c_[:, 8:16], c_[:, 0:8], AF.Sqrt, bias=eps6, scale=1.0 / 128)
                yield
                k.recip(c_[:, 0:8], c_[:, 8:16])
                k.tt("dve", of_, of_, c_[:, 0:8].r("p (h o) -> p h o", o=1).bc([128, 8, 128]), ALU.mult)
                mx = mix[r2]
                k.tt("dve", fl(mx), fl(of_), zn_, ALU.mult)
                yield
                for h in range(8):
                    k.tr(pb[:, h * 128:(h + 1) * 128], mx[:, h, :], identb)
                mt = mixT[r2]
                k.cp("act", fl(mt), pb)
                tq = slice((ti % 4) * 128, (ti % 4) * 128 + 128)
                k.dma(V(CINv[gq].rearrange("(h p) t -> p h t", p=128)[:, :, tq], DB("CIN", gq)), mt, eng="pool")
                if CDBG is not None:
                    k.dma(V(CDBG.rearrange("(h p) t -> p h t", p=128)[:, :, rows], DB("CINdbg", 0)), mt, eng="pool")
                if ti % 4 == 3 and stages >= 4:
                    ci_, co_ = CINf[gq], COUTf[gq]
                    P.collective(lambda e, ci_=ci_, co_=co_: e.collective_compute(
                        "AllGather", ALU.bypass, replica_groups=[[0, 1], [2, 3], [4, 5], [6, 7]],
                        ins=[ci_.opt()], outs=[co_.opt()]), [DB("CIN", gq)], [DB("COUT", gq)])
                yield
        _gens = [s3b_gen(0), s3b_gen(1), s3b_gen(2)]
        while _gens:
            for _g in list(_gens):
                try:
                    next(_g)
                except StopIteration:
                    _gens.remove(_g)
    stage_end()
    stage_begin()
    def ring(name, shape, dt, n=2):
        return sb(name, shape, dt, n=n)
    if stages >= 4:
        Wo = sb("Wo", [128, 16, D], BF16)
        wov = wout_in.r("(kc p) c -> p kc c", p=128)
        wst2 = sb("wst2", [128, 2, D], F32, n=2)
        for i in range(8):
            stv = wst2[i % 2]
            k.dma(stv, wov[:, 2 * i:2 * i + 2, :])
            k.cp(("dve", "pool")[i % 2], Wo[:, 2 * i:2 * i + 2, :], stv)
        mxl = ring("mxl", [128, 16, 128], BF16, n=3)
        xr = ring("xr", [128, D], F32, n=3)
        yo = ring("yo", [128, D], F32, n=3)
        c4 = ring("c4", [128, 4], F32, n=3)
        selv = sb("selv", [128, 2], F32)
        k.dma(selv, sel_in)
        mxa = ring("mxa", [128, 16, 128], BF16, n=3)
        mxb = ring("mxb", [128, 16, 128], BF16, n=3)
        coutv = [a.rearrange("(kc p) t -> p kc t", p=128) for a in COUTv]
        evs = []
        def s4_gen(r2):
            pA, pB = ps[2 * r2], ps[2 * r2 + 1]
            for t in range(r2, 16, 3):
                m_ = mxl[r2]
                ma, mb_ = mxa[r2], mxb[r2]
                tq = slice((t % 4) * 128, (t % 4) * 128 + 128)
                k.dma(ma, V(coutv[t // 4][:, :, tq], DB("COUT", t // 4)))
                k.dma(mb_, V(coutv[4 + t // 4][:, :, tq], DB("COUT", 4 + t // 4)))
                x_ = xr[r2]
                k.dma(x_, xh_in[t * 128:(t + 1) * 128, :])
                yield
                k.ts("dve", ma, ma, selv[:, 0:1], ALU.mult)
                k.stt("dve", m_, mb_, selv[:, 1:2], ma, ALU.mult, ALU.add)
                yield
                for nb, pp in ((0, pA), (1, pB)):
                    for kc in range(16):
                        k.mm(pp, m_[:, kc, :], Wo[:, kc, nb * 512:(nb + 1) * 512], start=(kc == 0), stop=(kc == 15))
                c_ = c4[r2]
                y_ = yo[r2]
                for nb, pp in ((0, pA), (1, pB)):
                    k.act(y_[:, nb * 512:(nb + 1) * 512], pp, AF.Square, accum=c_[:, nb:nb + 1])
                k.tt("dve", c_[:, 2:3], c_[:, 0:1], c_[:, 1:2], ALU.add)
                k.act(c_[:, 3:4], c_[:, 2:3], AF.Sqrt, bias=eps6, scale=1.0 / D)
                k.recip(c_[:, 2:3], c_[:, 3:4])
                for nb, pp in ((0, pA), (1, pB)):
                    k.stt("dve", y_[:, nb * 512:(nb + 1) * 512], pp, c_[:, 2:3], nw[:, 1024 + nb * 512:1024 + (nb + 1) * 512], ALU.mult, ALU.mult)
                k.tt("dve", y_, y_, x_, ALU.add)
                evs.append(k.dma(y_out[t * 128:(t + 1) * 128, :], y_, eng=STQ))
                yield
        _gens = [s4_gen(0), s4_gen(1), s4_gen(2)]
        while _gens:
            for _g in list(_gens):
                try:
                    next(_g)
                except StopIteration:
                    _gens.remove(_g)
        for ev in evs:
            P.wait_event("sp", ev)
    else:
        for (name, i), b in list(dbuf.items()):
            if b.lastw is not None:
                P.wait_event("sp", b.lastw)
        if PTb.lastw is not None:
            P.wait_event("sp", PTb.lastw)
    stage_end()
    P.emit()
    return nc


def _masks():
    idx = np.arange(128)
    same = (idx[:, None] // 64) == (idx[None, :] // 64)
    m = np.zeros((12, 128, 128), np.float32)
    m[0] = np.eye(128)
    m[1] = 1.0
    m[2] = same
    m[3] = (idx[:, None] < 64) * np.ones((1, 128))
    m[4] = (idx[:, None] >= 64) * np.ones((1, 128))
    r, c = idx[:, None], idx[None, :]
    for d in range(2):
        le = (r <= c) if d == 0 else (r >= c)
        lt = (r < c) if d == 0 else (r > c)
        gt = (r > c) if d == 0 else (r < c)
        m[5 + 3 * d] = same & le
        m[6 + 3 * d] = same & gt
        m[7 + 3 * d] = same & lt
    return np.ascontiguousarray(m.transpose(1, 0, 2).reshape(128, 12 * 128)).astype(np.float32)


def _core_inputs(c, x, norm_pre_w, w_in, gdn_conv_w, gdn_a_log, gdn_dt_bias, gdn_norm_w,
                 mlstm_conv_w, mlstm_gate_bias, mlstm_norm_w, w_out, norm_post_w):
    b, hh = c // 2, c % 2
    H = [4 * hh + i for i in range(4)]
    hs = np.concatenate([np.arange(h * 128, (h + 1) * 128) for h in H])
    M0 = 4128
    mqk = np.arange(4 * hh * 64, (4 * hh + 4) * 64)
    fm = np.concatenate([hs, 1024 + hs, 2048 + hs, M0 + mqk, M0 + 512 + mqk])
    tm = np.concatenate([3072 + hs, M0 + 1024 + hs, M0 + 2048 + hs, M0 + 3072 + hs])
    h4 = np.array(H)
    gates = np.concatenate([4096 + h4, 4096 + 8 + h4, 4112 + h4, 4112 + 8 + h4,
                            M0 + 4096 + h4, M0 + 4096 + 8 + h4, M0 + 4096 + 16 + h4, M0 + 4096 + 24 + h4])
    cols = np.concatenate([fm, tm, gates])
    W = np.ascontiguousarray(w_in[0][:, cols])
    gch = np.concatenate([hs, 1024 + hs, 2048 + hs])
    mch = np.concatenate([mqk, 512 + mqk])
    cwfull = np.concatenate([gdn_conv_w[0][:, gch], mlstm_conv_w[0][:, mch]], axis=1)
    cw = np.ascontiguousarray(cwfull.reshape(5, NFM, 128).transpose(2, 1, 0).reshape(128, NFM * 5))
    gp = np.zeros((48,), np.float32)
    gp[0:8] = gdn_a_log[0][:, h4].reshape(-1)
    gp[8:16] = gdn_dt_bias[0][:, h4].reshape(-1)
    gp[16:32] = mlstm_gate_bias[0][:, h4].reshape(-1)
    gpar = np.ascontiguousarray(np.broadcast_to(gp[None, :], (128, 48)))
    nwv = np.concatenate([np.tile(gdn_norm_w[0], 4), mlstm_norm_w[0][hs], norm_post_w[0]])
    nw = np.ascontiguousarray(np.broadcast_to(nwv[None, :], (128, 2048)))
    npre = np.ascontiguousarray(norm_pre_w[0].reshape(8, 128).T)
    rows = []
    for r in range(2):
        hr = np.concatenate([np.arange(h * 128, (h + 1) * 128) for h in range(4 * r, 4 * r + 4)])
        rows += [hr, 1024 + hr]
    wo = np.ascontiguousarray(w_out[0][np.concatenate(rows), :])
    sel = np.zeros((128, 2), np.float32)
    sel[:, hh] = 1.0
    return {"x": np.ascontiguousarray(x[b]), "xh": np.ascontiguousarray(x[b, hh * 2048:(hh + 1) * 2048]),
            "w_in": W, "w_out": wo, "npre": npre, "cw": cw, "gpar": gpar, "nw": nw,
            "masks": _masks(), "sel": sel}


def kernel(x, norm_pre_w, w_in, gdn_conv_w, gdn_a_log, gdn_dt_bias, gdn_norm_w,
           mlstm_conv_w, mlstm_gate_bias, mlstm_norm_w, w_out, norm_post_w):
    args = [np.asarray(a, dtype=np.float32) for a in (x, norm_pre_w, w_in, gdn_conv_w, gdn_a_log, gdn_dt_bias, gdn_norm_w,
                                                      mlstm_conv_w, mlstm_gate_bias, mlstm_norm_w, w_out, norm_post_w)]
    nc = build_nc()
    in_maps = [_core_inputs(c, *args) for c in range(8)]
    res = run_bass_kernel_spmd(nc, in_maps, core_ids=list(range(8)))
    out = np.zeros((4, S, D), np.float32)
    for c in range(8):
        b, hh = c // 2, c % 2
        out[b, hh * 2048:(hh + 1) * 2048] = res.results[c]["y"]
    return out
```

```python
import contextlib
import os
CUT = int(os.environ.get('S2CUT', '9'))
S3C = int(os.environ.get('S3C', '99'))
S3T = int(os.environ.get('S3T', '32'))
S3D = int(os.environ.get('S3D', '2'))
W_CL = int(os.environ.get('W_CL', '1'))
W_SC = int(os.environ.get('W_SC', '1'))
S3OFF = int(os.environ.get('S3OFF', '0'))
FINE = int(os.environ.get('FINE', '0'))
FINE2 = int(os.environ.get('FINE2', '0'))
DMAQ = int(os.environ.get('DMAQ', '1'))
STQ = os.environ.get('STQ', 'pool')
XRING = int(os.environ.get('XRING', '4'))
FMEV = int(os.environ.get('FMEV', '1'))
GORD = int(os.environ.get('GORD', '1'))
import numpy as np
import concourse.bass as bass
import concourse.mybir as mybir
from concourse.bass_utils import run_bass_kernel_spmd

F32 = mybir.dt.float32
BF16 = mybir.dt.bfloat16
AF = mybir.ActivationFunctionType
ALU = mybir.AluOpType

S = 4096
D = 1024
NT = S // 128
NG = S // 512
NFM = 16
DEBUG = None


class Buf:
    __slots__ = ("name", "lastw", "readers", "excl")

    def __init__(self, name="", excl=False):
        self.name = name
        self.lastw = None
        self.readers = []
        self.excl = excl


class V:
    __slots__ = ("ap", "buf")

    def __init__(self, ap, buf):
        self.ap = ap
        self.buf = buf

    def __getitem__(self, k):
        return V(self.ap[k], self.buf)

    def r(self, pat, **kw):
        return V(self.ap.rearrange(pat, **kw), self.buf)

    def bc(self, shape):
        return V(self.ap.to_broadcast(shape), self.buf)


class Prog:
    ENGS = ("pe", "act", "dve", "pool", "sp")

    def __init__(self, nc, n_dma_sems=32, self_sync=True):
        self.nc = nc
        self.self_sync = self_sync
        self.ops = {e: [] for e in self.ENGS}
        self.cnt = {e: 0 for e in self.ENGS}
        self.semobjs = {}
        for e in ("pe", "act", "dve", "pool"):
            self.semobjs["E" + e] = nc.alloc_semaphore("sem_" + e)
        self.ndma = n_dma_sems
        for i in range(n_dma_sems):
            self.semobjs["D%d" % i] = nc.alloc_semaphore("sem_dma%d" % i)
        self.semobjs["CC"] = nc.alloc_semaphore("sem_cc")
        self.cc_val = 0
        self.dma_next = 0
        self.dma_val = {("D%d" % i): 0 for i in range(n_dma_sems)}
        self.known = {e: {} for e in self.ENGS}

    def _collect(self, eng, reads, writes):
        waits = {}

        def add(ev):
            if ev is None:
                return
            k, v = ev
            if k == "E" + eng and (not self.self_sync or eng == "pe"):
                return
            if waits.get(k, 0) < v:
                waits[k] = v
        for b in reads:
            add(b.lastw)
            if b.excl:
                for r in b.readers:
                    if r[0] != "E" + eng:
                        add(r)
        for b in writes:
            add(b.lastw)
            for r in b.readers:
                if r[0] == "E" + eng:
                    continue
                add(r)
        out = []
        kn = self.known[eng]
        for k, v in waits.items():
            if kn.get(k, 0) < v:
                kn[k] = v
                out.append((k, v))
        return out

    def _update(self, ev, reads, writes):
        for b in reads:
            b.readers.append(ev)
            if len(b.readers) > 64:
                b.readers = b.readers[-48:]
        for b in writes:
            b.lastw = ev
            b.readers = []

    def op(self, eng, fn, reads=(), writes=()):
        waits = self._collect(eng, reads, writes)
        self.cnt[eng] += 1
        ev = ("E" + eng, self.cnt[eng])
        self.ops[eng].append((waits, fn, ev, 1))
        self._update(ev, reads, writes)
        return ev

    def dma(self, eng, out, in_):
        reads, writes = [in_.buf], [out.buf]
        k = "D%d" % self.dma_next
        self.dma_next = (self.dma_next + 1) % self.ndma
        waits = self._collect(eng, reads, writes)
        prev = self.dma_val[k]
        if prev > 0 and self.known[eng].get(k, 0) < prev:
            self.known[eng][k] = prev
            waits.append((k, prev))
        self.dma_val[k] = prev + 16
        ev = (k, prev + 16)
        oa, ia = out.ap, in_.ap

        def fn(e):
            return e.dma_start(out=oa, in_=ia)
        self.ops[eng].append((waits, fn, ev, 16))
        self._update(ev, reads, writes)
        return ev

    def collective(self, fn, reads, writes):
        waits = self._collect("pool", reads, writes)
        self.cc_val += 1
        ev = ("CC", self.cc_val)
        self.ops["pool"].append((waits, fn, ev, 1))
        self._update(ev, reads, writes)
        return ev

    def wait_event(self, eng, ev):
        k, v = ev
        if self.known[eng].get(k, 0) < v:
            self.known[eng][k] = v
            self.ops[eng].append(([(k, v)], None, None, 0))

    def barrier(self):
        for e in self.ENGS:
            waits = []
            for e2 in ("pe", "act", "dve", "pool"):
                v = self.cnt[e2]
                if e2 != e and v > 0 and self.known[e].get("E" + e2, 0) < v:
                    self.known[e]["E" + e2] = v
                    waits.append(("E" + e2, v))
            for kk, v in self.dma_val.items():
                if v > 0 and self.known[e].get(kk, 0) < v:
                    self.known[e][kk] = v
                    waits.append((kk, v))
            if self.cc_val > 0 and self.known[e].get("CC", 0) < self.cc_val:
                self.known[e]["CC"] = self.cc_val
                waits.append(("CC", self.cc_val))
            if waits:
                self.ops[e].append((waits, None, None, 0))

    def emit(self):
        nc = self.nc
        engmap = {"pe": "tensor", "act": "scalar", "dve": "vector", "pool": "gpsimd", "sp": "sync"}
        with nc.Block() as block:
            for ename in self.ENGS:
                ops = self.ops[ename]
                if not ops:
                    continue

                def body(e, ops=ops):
                    for waits, fn, ev, inc in ops:
                        for k, v in waits:
                            e.wait_ge(self.semobjs[k], v)
                        if fn is None:
                            continue
                        ins = fn(e)
                        ins.then_inc(self.semobjs[ev[0]], inc)
                getattr(block, engmap[ename])(body)
        self.ops = {e: [] for e in self.ENGS}


def _bufs(*vs):
    out = []
    for v in vs:
        if isinstance(v, V) and v.buf not in out:
            out.append(v.buf)
    return out


def _a(v):
    return v.ap if isinstance(v, V) else v


class K:
    def __init__(self, P):
        self.P = P
        self.rr = 0

    def act(self, out, in_, func, bias=None, scale=None, accum=None):
        kw = {}
        if bias is not None:
            kw["bias"] = _a(bias)
        if scale is not None:
            kw["scale"] = _a(scale)
        if accum is not None:
            kw["accum_out"] = _a(accum)
        o, i = out.ap, in_.ap
        self.P.op("act", lambda e: e.activation(out=o, in_=i, func=func, **kw),
                  _bufs(in_, bias, scale), _bufs(out, accum))

    def tt(self, eng, out, in0, in1, op):
        o, a, b = out.ap, in0.ap, in1.ap
        self.P.op(eng, lambda e: e.tensor_tensor(out=o, in0=a, in1=b, op=op), _bufs(in0, in1), _bufs(out))

    def ts(self, eng, out, in0, s1, op0, s2=None, op1=None):
        o, a, x1, x2 = out.ap, in0.ap, _a(s1), _a(s2)
        if op1 is None:
            self.P.op(eng, lambda e: e.tensor_scalar(out=o, in0=a, scalar1=x1, scalar2=None, op0=op0), _bufs(in0, s1), _bufs(out))
        else:
            self.P.op(eng, lambda e: e.tensor_scalar(out=o, in0=a, scalar1=x1, scalar2=x2, op0=op0, op1=op1), _bufs(in0, s1, s2), _bufs(out))

    def stt(self, eng, out, in0, scalar, in1, op0, op1):
        o, a, s, b = out.ap, in0.ap, _a(scalar), in1.ap
        eng = "dve"
        self.P.op(eng, lambda e: e.scalar_tensor_tensor(out=o, in0=a, scalar=s, in1=b, op0=op0, op1=op1), _bufs(in0, scalar, in1), _bufs(out))

    def cp(self, eng, out, in_):
        o, i = out.ap, in_.ap
        if eng == "act":
            self.P.op("act", lambda e: e.activation(out=o, in_=i, func=AF.Copy), _bufs(in_), _bufs(out))
        else:
            self.P.op(eng, lambda e: e.tensor_copy(out=o, in_=i), _bufs(in_), _bufs(out))

    def recip(self, out, in_):
        o, i = out.ap, in_.ap
        self.P.op("dve", lambda e: e.reciprocal(out=o, in_=i), _bufs(in_), _bufs(out))

    def memset(self, eng, out, val):
        o = out.ap
        self.P.op(eng, lambda e: e.memset(o, val), [], _bufs(out))

    def mm(self, out, lhsT, rhs, start=True, stop=True):
        o, l, r = out.ap, lhsT.ap, rhs.ap
        self.P.op("pe", lambda e: e.matmul(o, lhsT=l, rhs=r, start=start, stop=stop), _bufs(lhsT, rhs), _bufs(out))

    def tr(self, out, in_, ident):
        o, i, d = out.ap, in_.ap, ident.ap
        self.P.op("pe", lambda e: e.transpose(o, i, d), _bufs(in_, ident), _bufs(out))

    def dma(self, out, in_, eng=None):
        if eng is None:
            eng = ("sp", "act")[self.rr % 2] if DMAQ == 2 else "sp"
            self.rr += 1
        return self.P.dma(eng, out, in_)


def build_nc(stages=99):
    nc = bass.Bass("TRN2", target_bir_lowering=False)
    P = Prog(nc)
    k = K(P)

    def din(name, shape, dt=F32):
        return V(nc.dram_tensor(name, list(shape), dt, kind="ExternalInput").ap(), Buf(name))

    def dscr(name, shape, dt):
        kind = "ExternalOutput" if (DEBUG and name in DEBUG) else "Internal"
        return nc.dram_tensor(name, list(shape), dt, kind=kind).ap()

    ARENA = 98 * 1024
    arena = nc.alloc_sbuf_tensor("arena", [128, ARENA], BF16)
    aoff = [0, 0]

    def sb(name, shape, dt, n=1):
        vs = []
        for i in range(n):
            ne = 1
            for d_ in shape[1:]:
                ne *= d_
            nb = ne * (2 if dt == F32 else 1)
            nb = (nb + 15) // 16 * 16
            off = aoff[0]
            aoff[0] += nb
            assert aoff[0] <= ARENA, ("arena overflow", name, aoff[0])
            ap = arena[0:shape[0], off:off + ne * (2 if dt == F32 else 1)]
            if dt == F32:
                ap = ap.bitcast(F32)
            if len(shape) == 3:
                ap = ap.rearrange("p (a b) -> p a b", a=shape[1])
            elif len(shape) == 4:
                ap = ap.rearrange("p (a b c) -> p a b c", a=shape[1], b=shape[2])
            vs.append(V(ap, Buf(name)))
        return vs if n > 1 else vs[0]

    def stage_begin():
        aoff[0] = aoff[1]

    def stage_end():
        P.barrier()

    def psum(name, shape, dt=F32):
        return V(nc.alloc_psum_tensor(name, list(shape), dt)[:], Buf(name, excl=True))

    x_in = din("x", [S, D])
    xh_in = din("xh", [S // 2, D])
    win_in = din("w_in", [D, 4128])
    wout_in = din("w_out", [2048, D])
    npre_in = din("npre", [128, 8])
    cwg_in = din("cw", [128, NFM * 5])
    gpar_in = din("gpar", [128, 48])
    nw_in = din("nw", [128, 1024 + 1024])
    msk_in = din("masks", [128, 12 * 128])
    sel_in = din("sel", [128, 2])
    y_out = V(nc.dram_tensor("y", [S // 2, D], F32, kind="ExternalOutput").ap(), Buf("y"))

    PT = dscr("PT", [128, NFM, S + 4], BF16)
    ZG = dscr("ZG", [S, 512], BF16)
    MVd = dscr("MV", [S, 512], BF16)
    ZO = dscr("ZO", [S, 512], BF16)
    GT = dscr("GT", [S, 32], F32)
    QTd = dscr("QT", [128, 4, S], BF16)
    KTd = dscr("KT", [128, 4, S], BF16)
    Kd = dscr("Ktm", [S, 512], BF16)
    Vd = dscr("Vtm", [S, 512], BF16)
    MQTd = dscr("MQT", [128, 2, S], BF16)
    MKTd = dscr("MKT", [128, 2, S], BF16)
    MKd = dscr("MKtm", [S, 256], BF16)
    OFd = dscr("OF", [S, 1024], F32)
    CINf = [nc.dram_tensor("CIN%d" % g, [128, 2048], F32).ap() for g in range(NG)]
    COUTf = [nc.dram_tensor("COUT%d" % g, [256, 2048], F32).ap() for g in range(NG)]
    CINv = [a.bitcast(BF16).rearrange("q (a t) -> (q a) t", a=8) for a in CINf]
    COUTv = [a.bitcast(BF16).rearrange("q (a t) -> (q a) t", a=8) for a in COUTf]
    CDBG = dscr("CIN", [1024, S], BF16) if (DEBUG and "CIN" in DEBUG) else None
    dbuf = {}

    def DB(name, i):
        key = (name, i)
        if key not in dbuf:
            dbuf[key] = Buf("%s%d" % key)
        return dbuf[key]

    DBG_AT = tuple(int(v) for v in os.environ.get("DBGAT", "0,0").split(","))

    def dump(name, v, d, ti):
        if not DEBUG or name not in DEBUG or (d, ti) != DBG_AT:
            return
        t = nc.dram_tensor(name, list(v.ap.shape), v.ap.dtype, kind="ExternalOutput").ap()
        k.dma(V(t, DB(name, 0)), v)

    mskf = sb("mskf", [128, 12 * 128], F32)
    k.dma(mskf, msk_in)
    mskb = sb("mskb", [128, 12 * 128], BF16)
    k.cp("dve", mskb, mskf)

    def MF(i):
        return mskf[:, i * 128:(i + 1) * 128]

    def MB(i):
        return mskb[:, i * 128:(i + 1) * 128]
    IDENT, ONES, BLK, SEL0 = 0, 1, 2, 3
    identb = MB(IDENT)
    npre = sb("npre", [128, 8], F32)
    k.dma(npre, npre_in)
    cw = sb("cw", [128, NFM * 5], F32)
    k.dma(cw, cwg_in)
    gpar = sb("gpar", [128, 48], F32)
    k.dma(gpar, gpar_in)
    nw = sb("nw", [128, 2048], F32)
    k.dma(nw, nw_in)
    gA = sb("gA", [128, 8], F32)
    k.act(gA, gpar[:, 0:8], AF.Exp)
    zeros = sb("zeros", [128, 64], BF16)
    k.memset("pool", zeros, 0.0)
    PTb = Buf("PT")
    PTv = V(PT, PTb)
    k.dma(V(PT[:, :, 0:2], PTb), zeros[:, 0:32].r("p (b t) -> p b t", t=2))
    k.dma(V(PT[:, :, S + 2:S + 4], PTb), zeros[:, 0:32].r("p (b t) -> p b t", t=2))

    eps_t = sb("eps", [128, 2], F32)
    k.memset("pool", eps_t[:, 0:1], 1e-6)
    k.memset("pool", eps_t[:, 1:2], 1.0)
    eps6 = eps_t[:, 0:1]
    one1 = eps_t[:, 1:2]
    aoff[1] = aoff[0]
    stage_begin()
    Wb0 = sb("Wb", [128, 8, 4128], BF16)
    WPC = [(i * 256, (i + 1) * 256) for i in range(8)] + [(2048, 2560), (2560, 3072), (3072, 3584), (3584, 4128)]
    Wbufs = [Buf("Wb%d" % i) for i in range(len(WPC))]

    def Wsl(kc, c0, c1):
        for i, (a0, a1) in enumerate(WPC):
            if a0 <= c0 and c1 <= a1:
                return V(Wb0.ap[:, kc, c0:c1], Wbufs[i])
        raise AssertionError((c0, c1))
    wst = sb("wst", [128, 8, 544], F32, n=2)
    win_v = win_in.r("(kc p) c -> p kc c", p=128)
    for i in range(len(WPC)):
        c0, c1 = WPC[i]
        st = wst[i % 2][:, :, 0:c1 - c0]
        k.dma(st, win_v[:, :, c0:c1])
        k.tt(("dve", "pool")[i % 2], V(Wb0.ap[:, :, c0:c1], Wbufs[i]), st, npre.r("p (k o) -> p k o", o=1).bc([128, 8, c1 - c0]), ALU.mult)

    banks = [psum("bk%d" % i, [128, 512]) for i in range(8)]

    def pbview(v):
        return V(v.ap.bitcast(BF16), v.buf)
    ps = banks[:7]
    pb = pbview(banks[7])

    xt = sb("xt", [128, D], F32, n=XRING)
    junk = sb("junk", [128, D], BF16)
    st1 = sb("st1", [128, 4], F32, n=2)
    hb = sb("hb", [128, D], BF16, n=2)
    hT = sb("hT", [128, 8, 512], BF16, n=2)
    stg = sb("stg", [128, NFM, 512], BF16, n=1)
    stg = [stg, stg]
    tmz = sb("tmz", [128, 512], BF16, n=9)
    tmo = sb("tmo", [128, 512], F32, n=4)
    tmg = sb("tmg", [128, 32], F32, n=6)
    s1c = {"prep": 0, "mm": 0}

    def s1_prep():
        for g in range(NG):
            while s1c["mm"] < g - 1:
                yield
            hTg = hT[g % 2]
            for t in range(4):
                ti = g * 4 + t
                x_ = xt[ti % XRING]
                s_ = st1[ti % 2]
                h_ = hb[ti % 2]
                k.dma(x_, x_in[ti * 128:(ti + 1) * 128, :])
                k.act(junk, x_, AF.Square, accum=s_[:, 0:1])
                k.act(s_[:, 1:2], s_[:, 0:1], AF.Sqrt, bias=eps6, scale=1.0 / D)
                k.recip(s_[:, 2:3], s_[:, 1:2])
                k.ts("dve", h_, x_, s_[:, 2:3], ALU.mult)
                yield
                for kc in range(8):
                    k.tr(pb[:, kc * 128:(kc + 1) * 128], h_[:, kc * 128:(kc + 1) * 128], identb)
                k.cp("act", hTg[:, :, t * 128:(t + 1) * 128], pb.r("p (k c) -> p k c", k=8))
                yield
            s1c["prep"] = g + 1
            yield

    def s1_mm():
        cnt = 0
        for g in range(NG):
            while s1c["prep"] <= g:
                yield
            hTg = hT[g % 2]
            sg = stg[g % 2]
            for blk in range(NFM):
                p_ = ps[blk % 2]
                for kc in range(8):
                    k.mm(p_, Wsl(kc, blk * 128, (blk + 1) * 128), hTg[:, kc, :], start=(kc == 0), stop=(kc == 7))
                k.cp(("act", "dve")[blk % 2] if FMEV == 0 else ("dve" if (FMEV == 1 or blk % 4) else "act"), sg[:, blk, :], p_)
                yield
            k.dma(V(PT[:, :, 2 + g * 512:2 + (g + 1) * 512], PTb), sg, eng=STQ)
            for t in range(4):
                ti = g * 4 + t
                lh = hTg[:, :, t * 128:(t + 1) * 128]
                rows = slice(ti * 128, (ti + 1) * 128)
                for cb in range(4):
                    p_ = ps[2 + cb]
                    for kc in range(8):
                        k.mm(p_, lh[:, kc, :], Wsl(kc, 2048 + cb * 512, 2048 + (cb + 1) * 512), start=(kc == 0), stop=(kc == 7))
                pg = ps[6][:, 0:32]
                for kc in range(8):
                    k.mm(pg, lh[:, kc, :], Wsl(kc, 4096, 4128), start=(kc == 0), stop=(kc == 7))
                z_ = tmz[cnt % 9]; cnt += 1
                k.act(z_, ps[2], AF.Silu)
                k.dma(V(ZG[rows, :], DB("ZG", ti)), z_, eng=STQ)
                z_ = tmz[cnt % 9]; cnt += 1
                k.cp("dve", z_, ps[3])
                k.dma(V(MVd[rows, :], DB("MV", ti)), z_, eng=STQ)
                o_ = tmo[(2 * ti) % 4]
                k.act(o_, ps[4], AF.Sigmoid)
                o2 = tmo[(2 * ti + 1) % 4]
                k.act(o2, ps[5], AF.Silu)
                z_ = tmz[cnt % 9]; cnt += 1
                k.tt("dve", z_, o_, o2, ALU.mult)
                k.dma(V(ZO[rows, :], DB("ZO", ti)), z_, eng=STQ)
                g_ = tmg[ti % 6]
                k.cp("dve", g_, pg)
                k.dma(V(GT[rows, :], DB("GT", ti)), g_, eng=STQ)
                yield
            s1c["mm"] = g + 1
            yield

    if stages >= 1:
        _gens = [s1_prep(), s1_mm()]
        while _gens:
            for _g in list(_gens):
                try:
                    next(_g)
                except StopIteration:
                    _gens.remove(_g)

    stage_end()
    stage_begin()
    ptg = sb("ptg", [128, NFM, 516], BF16, n=2)
    dg = sb("dg", [128, NFM * 5, 128], BF16)
    for i in range(NFM * 5):
        k.ts(("dve", "pool")[i % 2], dg[:, i, :], MF(IDENT), cw[:, i:i + 1], ALU.mult)
    sl = sb("sl", [128, 512], F32, n=10)
    sq = sb("sq", [128, 512], BF16, n=3)
    rn = sb("rn", [128, 512], F32, n=3)
    fmo = sb("fmo", [128, NFM, 512], BF16, n=3)
    tmk = sb("tmk", [128, 512], BF16, n=9)
    onesb = MB(ONES)
    s2c = {"a": 0, "b": 0, "c": 0}

    def s2_conv():
        for g in range(NG):
            while s2c["b"] < g or s2c["c"] < g - 2:
                yield
            pt_ = ptg[g % 2]
            k.dma(pt_, V(PT[:, :, g * 512:g * 512 + 516], PTb))
            fo = fmo[g % 3]
            for blk in range(NFM):
                a_ = ps[blk % 4]
                for j in range(5):
                    k.mm(a_, dg[:, blk * 5 + j, :], pt_[:, blk, j:j + 512], start=(j == 0), stop=(j == 4))
                if blk < 8:
                    k.act(sl[blk], a_, AF.Silu)
                elif blk < 14:
                    k.act(fo[:, blk, :], a_, AF.Silu)
                else:
                    s_ = sl[8 + blk % 2]
                    k.act(s_, a_, AF.Silu)
                    k.ts("dve", fo[:, blk, :], s_, 0.125, ALU.mult)
                yield
            s2c["a"] = g + 1
            yield

    def s2_norm():
        i3 = 0
        for g in range(NG):
            while s2c["a"] <= g:
                yield
            fo = fmo[g % 3]
            for blk in range(8):
                s_ = sl[blk]
                q_ = sq[i3 % 3]
                r_ = rn[i3 % 3]
                p_ = ps[4 + i3 % 3]
                i3 += 1
                k.act(q_, s_, AF.Square)
                k.mm(p_, onesb, q_)
                k.act(r_, p_, AF.Sqrt, bias=eps6)
                yield
                k.recip(r_, r_)
                if blk < 4:
                    k.stt("dve", fo[:, blk, :], s_, 128.0 ** -0.5, r_, ALU.mult, ALU.mult)
                else:
                    k.tt("dve", fo[:, blk, :], s_, r_, ALU.mult)
                yield
            s2c["b"] = g + 1
            yield

    def s2_out():
        for g in range(NG):
            while s2c["b"] <= g:
                yield
            fo = fmo[g % 3]
            cs = slice(g * 512, (g + 1) * 512)
            k.dma(V(QTd[:, :, cs], DB("QT", g)), fo[:, 0:4, :], eng=STQ)
            k.dma(V(KTd[:, :, cs], DB("KT", g)), fo[:, 4:8, :], eng=STQ)
            k.dma(V(MQTd[:, :, cs], DB("MQT", g)), fo[:, 12:14, :], eng=STQ)
            k.dma(V(MKTd[:, :, cs], DB("MKT", g)), fo[:, 14:16, :], eng=STQ)
            for t in range(4):
                ti = g * 4 + t
                rows = slice(ti * 128, (ti + 1) * 128)
                tsl = slice(t * 128, (t + 1) * 128)
                for h in range(4):
                    k.tr(pb[:, h * 128:(h + 1) * 128], fo[:, 4 + h, tsl], identb)
                for h in range(4):
                    k.tr(pb[:, 512 + h * 128:512 + (h + 1) * 128], fo[:, 8 + h, tsl], identb)
                a_ = tmk[(3 * ti) % 9]
                k.cp("act", a_, pb[:, 0:512])
                k.dma(V(Kd[rows, :], DB("Ktm", ti)), a_, eng=STQ)
                b_ = tmk[(3 * ti + 1) % 9]
                k.cp("dve", b_, pb[:, 512:1024])
                k.dma(V(Vd[rows, :], DB("Vtm", ti)), b_, eng=STQ)
                yield
                for b2 in range(2):
                    k.tr(pb[:, b2 * 128:(b2 + 1) * 128], fo[:, 14 + b2, tsl], identb)
                c_ = tmk[(3 * ti + 2) % 9]
                k.cp("act", c_[:, 0:256], pb[:, 0:256])
                k.dma(V(MKd[rows, :], DB("MKtm", ti)), c_[:, 0:256], eng=STQ)
                yield
            s2c["c"] = g + 1
            yield

    if stages >= 2:
        _gens = [s2_conv(), s2_norm(), s2_out()]
        while _gens:
            for _g in list(_gens):
                try:
                    next(_g)
                except StopIteration:
                    _gens.remove(_g)

    stage_end()
    stage_begin()
    OBd = dscr("OB", [S, 1024], F32)

    def mkbufs(sx):
        B = {}

        def a(name, shape, dt):
            B[name] = sb(name + sx, shape, dt)
        for nm in ("qT", "kT", "Kt", "Vt", "Bk0", "Bk1", "kbg", "bv", "vn0", "vn1", "mqp", "Sb16",
                   "ktl0", "ktl1", "nWT0", "nWT1", "qkm0", "qkm1", "qd0", "qd1", "PTm0", "PTm1"):
            a(nm, [128, 4, 128], BF16)
        for nm in ("mqT", "mkT", "mqd0", "mqd1"):
            a(nm, [128, 2, 128], BF16)
        for nm in ("mKt", "wk00", "wk10", "wk01", "wk11"):
            a(nm, [128, 4, 64], BF16)
        for nm in ("CT0", "CT1"):
            a(nm, [128, 4, 2, 128], BF16)
        a("Sf", [128, 4, 128], F32)
        for nm in ("gUh", "gUl", "gXh", "gXl", "fUh", "fUl"):
            a(nm, [128, 4, 128], BF16)
        a("fXh", [128, 4, 64], BF16)
        a("fXl", [128, 4, 64], BF16)
        for nm in ("Eij", "Eji", "tm1", "tm2", "erow", "Wji", "Ut0", "Ut1"):
            a(nm, [128, 512], F32)
        for nm in ("Og0", "Og1", "Om0", "Om1"):
            a(nm, [128, 4, 128], F32)
        a("vaug0", [128, 4, 130], BF16)
        a("vaug1", [128, 4, 130], BF16)
        a("Cf", [128, 4, 130], F32)
        a("Cb16", [128, 4, 130], BF16)
        a("dn", [128, 8], F32)
        for nm in ("vn0", "vn1", "wk00", "wk10", "wk01", "wk11", "mqp"):
            k.memset("pool", B[nm], 0.0)
        for nm in ("vaug0", "vaug1"):
            k.memset("pool", B[nm], 1.0)
        B["cl_done"] = 0
        B["sg_done"] = 0
        B["sm_done"] = 0
        off = 0 if sx == "a" else 4
        B["ps"] = [banks[(i + off) % 8] for i in range(7)]
        B["pb"] = pbview(banks[(7 + off) % 8])
        return B

    def bc4(v, n=128, p=128):
        return v.r("p (h o) -> p h o", o=1).bc([p, 4, n])

    def mbc(i, n=128):
        return MF(i)[:, 0:n].r("p (o c) -> p o c", o=1).bc([128, 4, n])

    def fl(v):
        return v.r("p h c -> p (h c)")

    def h4(v):
        return v.r("p (h c) -> p h c", h=4)

    GP = {}
    NEG = {}
    if stages >= 3:
        for d_ in range(2):
            for nm_, mi_ in (("S", 7 + 3 * (1 - d_)), ("V", 5 + 3 * d_)):
                t_ = sb("neg%s%d" % (nm_, d_), [128, 4, 128], BF16)
                k.ts("dve", t_, MF(mi_).r("p (o c) -> p o c", o=1).bc([128, 4, 128]), -1.0, ALU.add, 30000.0, ALU.mult)
                NEG[(nm_, d_)] = t_
        GAt = sb("GAt", [128, NT, 32], F32)
        k.dma(GAt, V(GT.rearrange("(t p) c -> p t c", p=128), Buf("GTall")))
        for nm in ("Gg", "Bt", "IPb", "LF", "GG", "GTt", "CD0", "CD1", "BB", "BTt", "CM0", "CM1", "EG", "ET", "BG", "EW"):
            GP[nm] = sb("gp_" + nm, [128, 2, NT, 4], F32)

        def gcol(c0, d_):
            return GAt[:, :, c0 + 4 * d_:c0 + 4 * d_ + 4]

        def pbc(v):
            return v.r("p (o h) -> p o h", o=1).bc([128, NT, 4])

        def f2(v):
            return v.r("p t h -> p (t h)")
        for d_ in range(2):
            k.tt("dve", GP["Gg"][:, d_], gcol(0, d_), pbc(gpar[:, 8 + 4 * d_:12 + 4 * d_]), ALU.add)
            k.tt("dve", GP["LF"][:, d_], gcol(24, d_), pbc(gpar[:, 24 + 4 * d_:28 + 4 * d_]), ALU.add)
            k.tt("pool", GP["IPb"][:, d_], gcol(16, d_), pbc(gpar[:, 16 + 4 * d_:20 + 4 * d_]), ALU.add)
        for d_ in range(2):
            k.act(GP["Gg"][:, d_], GP["Gg"][:, d_], AF.Exp)
            k.act(GP["LF"][:, d_], GP["LF"][:, d_], AF.Exp, scale=-1.0)
        for d_ in range(2):
            k.act(GP["Gg"][:, d_], GP["Gg"][:, d_], AF.Ln, bias=one1)
            k.act(GP["LF"][:, d_], GP["LF"][:, d_], AF.Ln, bias=one1)
        for d_ in range(2):
            k.act(GP["Bt"][:, d_], gcol(8, d_), AF.Sigmoid)
            k.stt("dve", GP["Gg"][:, d_], GP["Gg"][:, d_], -1.0, pbc(gA[:, 4 * d_:4 * d_ + 4]), ALU.mult, ALU.mult)
            k.ts("dve", GP["LF"][:, d_], GP["LF"][:, d_], -1.0, ALU.mult)
        for nm in ("Ghi", "Glo", "Lhi", "Llo"):
            GP[nm] = sb("gp_" + nm, [128, 2, NT, 4], BF16)
        gtmp = sb("gp_tmp", [128, 2, NT, 4], F32)
        for (s_, h_, l_) in (("Gg", "Ghi", "Glo"), ("LF", "Lhi", "Llo")):
            k.cp("dve", GP[h_], GP[s_])
            k.tt("dve", gtmp, GP[s_], GP[h_], ALU.subtract)
            k.cp("dve", GP[l_], gtmp)
        for d_ in range(2):
            for si, (srcn, dsts) in enumerate((("Gg", ("GG", "GTt", "CD0", "CD1")), ("LF", ("BB", "BTt", "CM0", "CM1")))):
                bank = ps[2 * d_ + si]
                rhs_ = f2(GP[srcn][:, d_])
                for mi, msk in enumerate((5 + 3 * d_, BLK, SEL0, SEL0 + 1)):
                    k.mm(bank[:, mi * 128:(mi + 1) * 128], MF(msk), rhs_)
                k.cp("dve", f2(GP[dsts[0]][:, d_]), bank[:, 0:128])
                k.cp("dve", f2(GP[dsts[1]][:, d_]), bank[:, 128:256])
                k.act(f2(GP[dsts[2]][:, d_]), bank[:, 256:384], AF.Exp)
                k.act(f2(GP[dsts[3]][:, d_]), bank[:, 384:512], AF.Exp)
        for d_ in range(2):
            k.act(GP["EG"][:, d_], GP["GG"][:, d_], AF.Exp)
            k.tt("dve", GP["ET"][:, d_], GP["GTt"][:, d_], GP["GG"][:, d_], ALU.subtract)
            k.act(GP["ET"][:, d_], GP["ET"][:, d_], AF.Exp)
            k.tt("dve", GP["BG"][:, d_], GP["Bt"][:, d_], GP["EG"][:, d_], ALU.mult)
            k.tt("dve", GP["EW"][:, d_], GP["BTt"][:, d_], GP["BB"][:, d_], ALU.subtract)
            k.tt("dve", GP["EW"][:, d_], GP["EW"][:, d_], GP["IPb"][:, d_], ALU.add)
            k.act(GP["EW"][:, d_], GP["EW"][:, d_], AF.Exp)

    def cl_gen(d, B):
        ps, pb = B["ps"], B["pb"]
        CUMT, AFTER = 5 + 3 * d, 6 + 3 * d
        STRIJ = 7 + 3 * (1 - d)
        order = list(range(NT)) if d == 0 else list(range(NT - 1, -1, -1))
        Od_mine, Od_other = (OFd, OBd) if d == 0 else (OBd, OFd)
        nm_mine, nm_other = ("OF", "OB") if d == 0 else ("OB", "OF")
        for it, ti in enumerate(order[:S3T]):
            while min(B["sg_done"], B["sm_done"]) < it - 1:
                yield
            sl_ = it % 2
            rows = slice(ti * 128, (ti + 1) * 128)
            cs = rows
            gq = ti // 4
            q_, k_, K_, V_ = B["qT"], B["kT"], B["Kt"], B["Vt"]
            mq_, mk_, mK_ = B["mqT"], B["mkT"], B["mKt"]
            k.dma(k_, V(KTd[:, :, cs], DB("KT", gq)))
            k.dma(q_, V(QTd[:, :, cs], DB("QT", gq)))
            k.dma(fl(K_), V(Kd[rows, :], DB("Ktm", ti)))
            k.dma(fl(V_), V(Vd[rows, :], DB("Vtm", ti)))
            k.dma(mq_, V(MQTd[:, :, cs], DB("MQT", gq)))
            k.dma(mk_, V(MKTd[:, :, cs], DB("MKT", gq)))
            k.dma(fl(mK_), V(MKd[rows, :], DB("MKtm", ti)))
            va = B["vaug%d" % sl_]
            k.dma(va[:, :, 0:128], V(MVd[rows, :].rearrange("p (h c) -> p h c", h=4), DB("MV", ti)))
            mqp_ = B["mqp"]
            mqpv = mqp_.r("p (b e) c -> p b e c", e=2)
            k.dma(mqpv[0:64, :, 0, :], V(MQTd[0:64, :, cs], DB("MQT", gq)))
            k.dma(mqpv[64:128, :, 1, :], V(MQTd[64:128, :, cs], DB("MQT", gq)))
            g_c, be_c, bg_c, et_c = GP["Gg"][:, d, ti, :], GP["Bt"][:, d, ti, :], GP["BG"][:, d, ti, :], GP["ET"][:, d, ti, :]
            lf_c, ip_c, ew_c = GP["LF"][:, d, ti, :], GP["IPb"][:, d, ti, :], GP["EW"][:, d, ti, :]
            ghi_c, glo_c = GP["Ghi"][:, d, ti, :], GP["Glo"][:, d, ti, :]
            lhi_c, llo_c = GP["Lhi"][:, d, ti, :], GP["Llo"][:, d, ti, :]
            yield
            def mbcb(i, n=128):
                return MB(i)[:, 0:n].r("p (o c) -> p o c", o=1).bc([128, 4, n])
            gU2 = (B["gUh"], B["gUl"])
            gX2 = (B["gXh"], B["gXl"])
            fU2 = (B["fUh"], B["fUl"])
            fX2 = (B["fXh"], B["fXl"])
            for x_, (gc_, lc_) in enumerate(((ghi_c, lhi_c), (glo_c, llo_c))):
                k.tt("pool", gU2[x_], mbcb(AFTER), bc4(gc_), ALU.mult)
                k.tt("pool", gX2[x_], mbcb(ONES), bc4(gc_), ALU.mult)
                k.tt("pool", fU2[x_], mbcb(AFTER), bc4(lc_), ALU.mult)
                k.tt("pool", fX2[x_], mbcb(ONES, 64), bc4(lc_, 64), ALU.mult)
            yield
            for x_ in range(2):
                k.mm(ps[0], MB(CUMT), fl(gU2[x_]), start=(x_ == 0), stop=(x_ == 1))
            E1 = B["Eij"]
            k.act(E1, ps[0], AF.Exp)
            if FINE2:
                yield
            for h in range(4):
                for x_ in range(2):
                    k.mm(ps[1][:, h * 128:(h + 1) * 128], gU2[x_][:, h, :], MB(CUMT), start=(x_ == 0), stop=(x_ == 1))
            E2 = B["Eji"]
            k.act(E2, ps[1], AF.Exp)
            yield
            for h in range(4):
                for x_ in range(2):
                    k.mm(ps[2][:, h * 128:(h + 1) * 128], gX2[x_][:, h, :], MB(CUMT), start=(x_ == 0), stop=(x_ == 1))
            er = B["erow"]
            k.act(er, ps[2], AF.Exp)
            qd_ = B["qd%d" % sl_]
            k.tt("dve", fl(qd_), fl(q_), er, ALU.mult)
            if FINE2:
                yield
            for h in range(4):
                k.mm(ps[3][:, h * 128:(h + 1) * 128], k_[:, h, :], k_[:, h, :])
            t1 = B["tm1"]
            k.tt("pool", h4(t1), h4(E1), mbc(STRIJ), ALU.mult)
            k.tt("dve", t1, t1, ps[3], ALU.mult)
            B0 = B["Bk0"]
            k.tt("dve", B0, h4(t1), bc4(be_c), ALU.mult)
            yield
            for h in range(4):
                k.tr(pb[:, h * 128:(h + 1) * 128], B0[:, h, :], identb)
            C0 = B["CT0"]
            k.cp("act", C0[:, :, 0, :], h4(pb[:, 0:512]))
            k.tt("dve", C0[:, :, 1, :], MB(IDENT).r("p (o c) -> p o c", o=1).bc([128, 4, 128]), h4(pb[:, 0:512]), ALU.subtract)
            if FINE2:
                yield
            qk_ = B["qkm%d" % sl_]
            for h in range(4):
                k.mm(ps[4][:, h * 128:(h + 1) * 128], k_[:, h, :], q_[:, h, :])
            t2 = B["tm2"]
            k.tt("pool", h4(t2), h4(E2), mbc(CUMT), ALU.mult)
            k.tt("dve", fl(qk_), t2, ps[4], ALU.mult)
            yield
            Bk = [B["Bk0"], B["Bk1"]]
            CT = [B["CT0"], B["CT1"]]
            for h in range(4):
                k.mm(ps[0][:, h * 128:(h + 1) * 128], CT[0][:, h, 0, :], Bk[0][:, h, :])
            for h in range(4):
                k.mm(ps[1][:, h * 128:(h + 1) * 128], Bk[0][:, h, :], CT[0][:, h, 0, :])
            k.cp("act", Bk[1], h4(ps[0]))
            k.cp("dve", CT[1][:, :, 0, :], h4(ps[1]))
            k.cp("pool", CT[1][:, :, 1, :], CT[0][:, :, 1, :])
            yield
            cur = 1
            for m in range(1, 6):
                nxt = 1 - cur
                last = (m == 5)
                for h in range(4):
                    pp = ps[2 + h // 2][:, (h % 2) * 256:(h % 2) * 256 + 256]
                    if last:
                        k.mm(pp[:, 128:256], Bk[cur][:, h, :], CT[cur][:, h, 1, :])
                    else:
                        k.mm(pp, Bk[cur][:, h, :], CT[cur][:, h, :, :].r("p t c -> p (t c)"))
                if not last:
                    if FINE2:
                        yield
                    for h in range(4):
                        k.mm(ps[0][:, h * 128:(h + 1) * 128], CT[cur][:, h, 0, :], Bk[cur][:, h, :])
                    k.cp("act", Bk[nxt], h4(ps[0]))
                for hh2 in range(2):
                    pv = ps[2 + hh2].r("p (h t c) -> p h t c", h=2, t=2)
                    if not last:
                        k.cp("act", CT[nxt][:, 2 * hh2:2 * hh2 + 2, 0, :], pv[:, :, 0, :])
                    k.tt("dve", CT[nxt][:, 2 * hh2:2 * hh2 + 2, 1, :], CT[cur][:, 2 * hh2:2 * hh2 + 2, 1, :], pv[:, :, 1, :], ALU.add)
                cur = nxt
                yield
            TT = CT[cur]
            kb_, bv_, kt_ = B["kbg"], B["bv"], B["ktl%d" % sl_]
            k.tt("pool", kb_, K_, bc4(bg_c), ALU.mult)
            k.tt("pool", bv_, V_, bc4(be_c), ALU.mult)
            k.tt("pool", kt_, K_, bc4(et_c), ALU.mult)
            for h in range(4):
                k.mm(ps[5][:, h * 128:(h + 1) * 128], kb_[:, h, :], TT[:, h, 1, :])
            nW = B["nWT%d" % sl_]
            k.ts("dve", fl(nW), ps[5], -1.0, ALU.mult)
            yield
            for h in range(4):
                for x_ in range(2):
                    k.mm(ps[0][:, h * 128:(h + 1) * 128], fU2[x_][:, h, :], MB(CUMT), start=(x_ == 0), stop=(x_ == 1))
            Wj = B["Wji"]
            for h in range(4):
                k.act(Wj[:, h * 128:(h + 1) * 128], ps[0][:, h * 128:(h + 1) * 128], AF.Exp, bias=ip_c[:, h:h + 1])
            for h in range(4):
                k.mm(ps[1][:, h * 128:(h + 1) * 128], mk_[:, h // 2, :], mqp_[:, h, :])
            k.tt("pool", h4(Wj), h4(Wj), mbc(CUMT), ALU.mult)
            PT_ = B["PTm%d" % sl_]
            k.tt("dve", fl(PT_), Wj, ps[1], ALU.mult)
            yield
            for b2 in range(2):
                for x_ in range(2):
                    k.mm(ps[2][:, b2 * 128:(b2 + 1) * 128], fX2[x_][:, 2 * b2:2 * b2 + 2, :].r("p h c -> p (h c)"), MB(CUMT), start=(x_ == 0), stop=(x_ == 1))
            k.act(er[:, 0:256], ps[2][:, 0:256], AF.Exp)
            mqd_ = B["mqd%d" % sl_]
            k.tt("dve", mqd_.r("p b c -> p (b c)"), mq_.r("p b c -> p (b c)"), er[:, 0:256], ALU.mult)
            wkc = [B["wk0%d" % sl_], B["wk1%d" % sl_]]
            for c in range(2):
                rc = slice(64 * c, 64 * c + 64)
                k.tt("pool", wkc[c][rc], mK_[rc], bc4(ew_c[rc], 64, 64), ALU.mult)
            yield
            U_ = B["Ut%d" % sl_]
            for h in range(4):
                k.mm(ps[3][:, h * 128:(h + 1) * 128], TT[:, h, 1, :], bv_[:, h, :])
            k.cp("act", U_, ps[3])
            for nm_, v_ in (("d_U", U_), ("d_nW", nW), ("d_qd", qd_), ("d_kt", kt_), ("d_PT", PT_), ("d_va", va)):
                dump(nm_, v_, d, it)
            B["cl_done"] = it + 1
            yield

    def scan_g_gen(d, B):
        ps = B["ps"]
        Sf, Sb16 = B["Sf"], B["Sb16"]
        k.memset("dve", Sf, 0.0)
        k.memset("dve", Sb16, 0.0)
        order = list(range(NT)) if d == 0 else list(range(NT - 1, -1, -1))
        Od_mine = OFd if d == 0 else OBd
        nm_mine = "OFg" if d == 0 else "OBg"
        vn = [B["vn0"], B["vn1"]]
        CDg = (GP["CD0"], GP["CD1"])
        for it, ti in enumerate(order[:S3T]):
            while B["cl_done"] <= it:
                yield
            sl_ = it % 2
            rows = slice(ti * 128, (ti + 1) * 128)
            qd_, qk_, kt_, nW = B["qd%d" % sl_], B["qkm%d" % sl_], B["ktl%d" % sl_], B["nWT%d" % sl_]
            U_ = B["Ut%d" % sl_]
            O_ = B["Og%d" % sl_]
            for c in ((0, 1) if d == 0 else (1, 0)):
                rc = slice(64 * c, 64 * c + 64)
                vn_ = vn[c]
                for h in range(4):
                    k.mm(ps[3][:, h * 128:(h + 1) * 128], nW[:, h, :], Sb16[:, h, :])
                k.tt("dve", fl(vn_[rc]), U_[rc], ps[3][rc], ALU.add)
                yield
                for h in range(4):
                    k.mm(ps[4][:, h * 128:(h + 1) * 128], qd_[:, h, :], Sb16[:, h, :], start=True, stop=False)
                    k.mm(ps[4][:, h * 128:(h + 1) * 128], qk_[:, h, :], vn_[:, h, :], start=False, stop=True)
                for h in range(4):
                    k.mm(ps[5][:, h * 128:(h + 1) * 128], kt_[:, h, :], vn_[:, h, :])
                k.cp("act", fl(O_[rc]), ps[4][rc])
                for h in range(4):
                    k.stt("dve", Sf[:, h, :], Sf[:, h, :], CDg[c][:, d, ti, h:h + 1], ps[5][:, h * 128:(h + 1) * 128], ALU.mult, ALU.add)
                k.cp("act", Sb16, Sf)
                yield
            k.dma(V(Od_mine[rows, 0:512], DB(nm_mine, ti)), fl(O_))
            B["sg_done"] = it + 1
            yield

    def scan_m_gen(d, B):
        ps = B["ps"]
        Cf, Cb16 = B["Cf"], B["Cb16"]
        k.memset("pool", Cf, 0.0)
        k.memset("pool", Cb16, 0.0)
        order = list(range(NT)) if d == 0 else list(range(NT - 1, -1, -1))
        Od_mine = OFd if d == 0 else OBd
        nm_mine = "OFm" if d == 0 else "OBm"
        dn_ = B["dn"]
        CDm = (GP["CM0"], GP["CM1"])
        for it, ti in enumerate(order[:S3T]):
            while B["cl_done"] <= it:
                yield
            sl_ = it % 2
            rows = slice(ti * 128, (ti + 1) * 128)
            PT_, mqd_, va = B["PTm%d" % sl_], B["mqd%d" % sl_], B["vaug%d" % sl_]
            wkc = [B["wk0%d" % sl_], B["wk1%d" % sl_]]
            O_ = B["Om%d" % sl_]
            for c in ((0, 1) if d == 0 else (1, 0)):
                rc = slice(64 * c, 64 * c + 64)
                for h in range(4):
                    pp = ps[h // 2][:, (h % 2) * 130:(h % 2) * 130 + 130]
                    k.mm(pp, mqd_[:, h // 2, :], Cb16[:, h, :], start=True, stop=False)
                    k.mm(pp, PT_[:, h, :], va[:, h, :], start=False, stop=True)
                for h in range(4):
                    pp = (ps[2] if h < 2 else ps[6])[:, 32 + (h % 2) * 130:32 + (h % 2) * 130 + 130]
                    k.mm(pp, wkc[c][:, 2 * (h // 2):2 * (h // 2) + 2, :].r("p h c -> p (h c)"), va[:, h, :])
                for b2 in range(2):
                    pv = ps[b2][:, 0:260].r("p (h c) -> p h c", h=2)
                    k.act(dn_[rc, 2 * b2:2 * b2 + 2].r("p (h o) -> p h o", o=1), pv[rc, :, 128:129], AF.Abs)
                k.ts("dve", dn_[rc, 0:4], dn_[rc, 0:4], 1.0, ALU.max)
                k.recip(dn_[rc, 4:8], dn_[rc, 0:4])
                for b2 in range(2):
                    pv = ps[b2][:, 0:260].r("p (h c) -> p h c", h=2)
                    k.tt("dve", O_[rc, 2 * b2:2 * b2 + 2, :], pv[rc, :, 0:128], dn_[rc, 4 + 2 * b2:6 + 2 * b2].r("p (h o) -> p h o", o=1).bc([64, 2, 128]), ALU.mult)
                for h in range(4):
                    pr = slice(64 * (h % 2), 64 * (h % 2) + 64)
                    pp = (ps[2] if h < 2 else ps[6])[:, 32 + (h % 2) * 130:32 + (h % 2) * 130 + 130]
                    k.stt("dve", Cf[pr, h, :], Cf[pr, h, :], CDm[c][pr, d, ti, h:h + 1], pp[pr, :], ALU.mult, ALU.add)
                k.cp("act", Cb16, Cf)
                yield
            k.dma(V(Od_mine[rows, 512:1024], DB(nm_mine, ti)), fl(O_))
            B["sm_done"] = it + 1
            yield

    if stages >= 3:
        Ba, Bb = mkbufs("a"), mkbufs("b")
        gb_ = cl_gen(1, Bb)
        for _ in range(S3OFF):
            next(gb_)
        gens = [(cl_gen(0, Ba), W_CL), (scan_g_gen(0, Ba), W_SC), (scan_m_gen(0, Ba), W_SC), (gb_, W_CL), (scan_g_gen(1, Bb), W_SC), (scan_m_gen(1, Bb), W_SC)]
        if GORD == 1:
            gens = [gens[0], gens[3], gens[1], gens[4], gens[2], gens[5]]
        while gens:
            for gw_ in list(gens):
                g_, n_ = gw_
                for _ in range(n_):
                    try:
                        next(g_)
                    except StopIteration:
                        gens.remove(gw_)
                        break
    print("S3 arena", aoff[0])
    stage_end()
    stage_begin()
    if stages >= 3:
        RD = 3
        ofl = sb("ofl", [128, 8, 128], F32, n=2 * RD)
        obl = sb("obl", [128, 8, 128], F32, n=2 * RD)
        zgl = sb("zgl", [128, 512], BF16, n=2 * RD)
        zol = sb("zol", [128, 512], BF16, n=2 * RD)
        znl = sb("znl", [128, 1024], F32, n=RD)
        cst = sb("cst", [128, 16], F32, n=RD)
        mix = sb("mix", [128, 8, 128], BF16, n=RD)
        mixT = sb("mixT", [128, 8, 128], BF16, n=RD)
        jk2 = sb("jk2", [128, 128], F32)
        def s3b_gen(r2):
            for kk_, ti in enumerate(range(r2, NT, RD)):
                rows = slice(ti * 128, (ti + 1) * 128)
                gq = ti // 4
                s2_ = r2 + RD * (kk_ % 2)
                of_, ob_, zg_, zo_ = ofl[s2_], obl[s2_], zgl[s2_], zol[s2_]
                zn_ = znl[r2]
                k.dma(fl(of_), V(OFd[rows, :], DB("OF", ti)))
                k.dma(fl(ob_), V(OBd[rows, :], DB("OB", ti)))
                k.dma(zg_, V(ZG[rows, :], DB("ZG", ti)))
                k.dma(zo_, V(ZO[rows, :], DB("ZO", ti)))
                yield
                k.tt("pool", zn_[:, 0:512], zg_, nw[:, 0:512], ALU.mult)
                k.tt("pool", zn_[:, 512:1024], zo_, nw[:, 512:1024], ALU.mult)
                k.tt("dve", of_, of_, ob_, ALU.add)
                yield
                c_ = cst[r2]
                for h in range(8):
                    k.act(jk2, of_[:, h, :], AF.Square, accum=c_[:, h:h + 1])
                k.act(c_[:, 8:16], c_[:, 0:8], AF.Sqrt, bias=eps6, scale=1.0 / 128)
                yield
                k.recip(c_[:, 0:8], c_[:, 8:16])
                k.tt("dve", of_, of_, c_[:, 0:8].r("p (h o) -> p h o", o=1).bc([128, 8, 128]), ALU.mult)
                mx = mix[r2]
                k.tt("dve", fl(mx), fl(of_), zn_, ALU.mult)
                yield
                for h in range(8):
                    k.tr(pb[:, h * 128:(h + 1) * 128], mx[:, h, :], identb)
                mt = mixT[r2]
                k.cp("act", fl(mt), pb)
                tq = slice((ti % 4) * 128, (ti % 4) * 128 + 128)
                k.dma(V(CINv[gq].rearrange("(h p) t -> p h t", p=128)[:, :, tq], DB("CIN", gq)), mt, eng="pool")
                if CDBG is not None:
                    k.dma(V(CDBG.rearrange("(h p) t -> p h t", p=128)[:, :, rows], DB("CINdbg", 0)), mt, eng="pool")
                if ti % 4 == 3 and stages >= 4:
                    ci_, co_ = CINf[gq], COUTf[gq]
                    P.collective(lambda e, ci_=ci_, co_=co_: e.collective_compute(
                        "AllGather", ALU.bypass, replica_groups=[[0, 1], [2, 3], [4, 5], [6, 7]],
                        ins=[ci_.opt()], outs=[co_.opt()]), [DB("CIN", gq)], [DB("COUT", gq)])
                yield
        _gens = [s3b_gen(0), s3b_gen(1), s3b_gen(2)]
        while _gens:
            for _g in list(_gens):
                try:
                    next(_g)
                except StopIteration:
                    _gens.remove(_g)
    stage_end()
    stage_begin()
    def ring(name, shape, dt, n=2):
        return sb(name, shape, dt, n=n)
    if stages >= 4:
        Wo = sb("Wo", [128, 16, D], BF16)
        wov = wout_in.r("(kc p) c -> p kc c", p=128)
        wst2 = sb("wst2", [128, 2, D], F32, n=2)
        for i in range(8):
            stv = wst2[i % 2]
            k.dma(stv, wov[:, 2 * i:2 * i + 2, :])
            k.cp(("dve", "pool")[i % 2], Wo[:, 2 * i:2 * i + 2, :], stv)
        mxl = ring("mxl", [128, 16, 128], BF16, n=3)
        xr = ring("xr", [128, D], F32, n=6)
        yo = ring("yo", [128, D], F32, n=3)
        c4 = ring("c4", [128, 4], F32, n=3)
        selv = sb("selv", [128, 2], F32)
        k.dma(selv, sel_in)
        mxa = ring("mxa", [128, 16, 128], BF16, n=6)
        mxb = ring("mxb", [128, 16, 128], BF16, n=6)
        coutv = [a.rearrange("(kc p) t -> p kc t", p=128) for a in COUTv]
        evs = []
        def s4_gen(r2):
            pA, pB = ps[2 * r2], ps[2 * r2 + 1]
            for kk_, t in enumerate(range(r2, 16, 3)):
                m_ = mxl[r2]
                s2_ = r2 + 3 * (kk_ % 2)
                ma, mb_ = mxa[s2_], mxb[s2_]
                tq = slice((t % 4) * 128, (t % 4) * 128 + 128)
                k.dma(ma, V(coutv[t // 4][:, :, tq], DB("COUT", t // 4)))
                k.dma(mb_, V(coutv[4 + t // 4][:, :, tq], DB("COUT", 4 + t // 4)))
                x_ = xr[s2_]
                k.dma(x_, xh_in[t * 128:(t + 1) * 128, :])
                yield
                k.ts("dve", ma, ma, selv[:, 0:1], ALU.mult)
                k.stt("dve", m_, mb_, selv[:, 1:2], ma, ALU.mult, ALU.add)
                yield
                for nb, pp in ((0, pA), (1, pB)):
                    for kc in range(16):
                        k.mm(pp, m_[:, kc, :], Wo[:, kc, nb * 512:(nb + 1) * 512], start=(kc == 0), stop=(kc == 15))
                c_ = c4[r2]
                y_ = yo[r2]
                for nb, pp in ((0, pA), (1, pB)):
                    k.act(y_[:, nb * 512:(nb + 1) * 512], pp, AF.Square, accum=c_[:, nb:nb + 1])
                k.tt("dve", c_[:, 2:3], c_[:, 0:1], c_[:, 1:2], ALU.add)
                k.act(c_[:, 3:4], c_[:, 2:3], AF.Sqrt, bias=eps6, scale=1.0 / D)
                k.recip(c_[:, 2:3], c_[:, 3:4])
                for nb, pp in ((0, pA), (1, pB)):
                    k.stt("dve", y_[:, nb * 512:(nb + 1) * 512], pp, c_[:, 2:3], nw[:, 1024 + nb * 512:1024 + (nb + 1) * 512], ALU.mult, ALU.mult)
                k.tt("dve", y_, y_, x_, ALU.add)
                evs.append(k.dma(y_out[t * 128:(t + 1) * 128, :], y_, eng=STQ))
                yield
        _gens = [s4_gen(0), s4_gen(1), s4_gen(2)]
        while _gens:
            for _g in list(_gens):
                try:
                    next(_g)
                except StopIteration:
                    _gens.remove(_g)
        for ev in evs:
            P.wait_event("sp", ev)
    else:
        for (name, i), b in list(dbuf.items()):
            if b.lastw is not None:
                P.wait_event("sp", b.lastw)
        if PTb.lastw is not None:
            P.wait_event("sp", PTb.lastw)
    stage_end()
    P.emit()
    return nc


def _masks():
    idx = np.arange(128)
    same = (idx[:, None] // 64) == (idx[None, :] // 64)
    m = np.zeros((12, 128, 128), np.float32)
    m[0] = np.eye(128)
    m[1] = 1.0
    m[2] = same
    m[3] = (idx[:, None] < 64) * np.ones((1, 128))
    m[4] = (idx[:, None] >= 64) * np.ones((1, 128))
    r, c = idx[:, None], idx[None, :]
    for d in range(2):
        le = (r <= c) if d == 0 else (r >= c)
        lt = (r < c) if d == 0 else (r > c)
        gt = (r > c) if d == 0 else (r < c)
        m[5 + 3 * d] = same & le
        m[6 + 3 * d] = same & gt
        m[7 + 3 * d] = same & lt
    return np.ascontiguousarray(m.transpose(1, 0, 2).reshape(128, 12 * 128)).astype(np.float32)


def _core_inputs(c, x, norm_pre_w, w_in, gdn_conv_w, gdn_a_log, gdn_dt_bias, gdn_norm_w,
                 mlstm_conv_w, mlstm_gate_bias, mlstm_norm_w, w_out, norm_post_w):
    b, hh = c // 2, c % 2
    H = [4 * hh + i for i in range(4)]
    hs = np.concatenate([np.arange(h * 128, (h + 1) * 128) for h in H])
    M0 = 4128
    mqk = np.arange(4 * hh * 64, (4 * hh + 4) * 64)
    fm = np.concatenate([hs, 1024 + hs, 2048 + hs, M0 + mqk, M0 + 512 + mqk])
    tm = np.concatenate([3072 + hs, M0 + 1024 + hs, M0 + 2048 + hs, M0 + 3072 + hs])
    h4 = np.array(H)
    gates = np.concatenate([4096 + h4, 4096 + 8 + h4, 4112 + h4, 4112 + 8 + h4,
                            M0 + 4096 + h4, M0 + 4096 + 8 + h4, M0 + 4096 + 16 + h4, M0 + 4096 + 24 + h4])
    cols = np.concatenate([fm, tm, gates])
    W = np.ascontiguousarray(w_in[0][:, cols])
    gch = np.concatenate([hs, 1024 + hs, 2048 + hs])
    mch = np.concatenate([mqk, 512 + mqk])
    cwfull = np.concatenate([gdn_conv_w[0][:, gch], mlstm_conv_w[0][:, mch]], axis=1)
    cw = np.ascontiguousarray(cwfull.reshape(5, NFM, 128).transpose(2, 1, 0).reshape(128, NFM * 5))
    gp = np.zeros((48,), np.float32)
    gp[0:8] = gdn_a_log[0][:, h4].reshape(-1)
    gp[8:16] = gdn_dt_bias[0][:, h4].reshape(-1)
    gp[16:32] = mlstm_gate_bias[0][:, h4].reshape(-1)
    gpar = np.ascontiguousarray(np.broadcast_to(gp[None, :], (128, 48)))
    nwv = np.concatenate([np.tile(gdn_norm_w[0], 4), mlstm_norm_w[0][hs], norm_post_w[0]])
    nw = np.ascontiguousarray(np.broadcast_to(nwv[None, :], (128, 2048)))
    npre = np.ascontiguousarray(norm_pre_w[0].reshape(8, 128).T)
    rows = []
    for r in range(2):
        hr = np.concatenate([np.arange(h * 128, (h + 1) * 128) for h in range(4 * r, 4 * r + 4)])
        rows += [hr, 1024 + hr]
    wo = np.ascontiguousarray(w_out[0][np.concatenate(rows), :])
    sel = np.zeros((128, 2), np.float32)
    sel[:, hh] = 1.0
    return {"x": np.ascontiguousarray(x[b]), "xh": np.ascontiguousarray(x[b, hh * 2048:(hh + 1) * 2048]),
            "w_in": W, "w_out": wo, "npre": npre, "cw": cw, "gpar": gpar, "nw": nw,
            "masks": _masks(), "sel": sel}


def kernel(x, norm_pre_w, w_in, gdn_conv_w, gdn_a_log, gdn_dt_bias, gdn_norm_w,
           mlstm_conv_w, mlstm_gate_bias, mlstm_norm_w, w_out, norm_post_w):
    args = [np.asarray(a, dtype=np.float32) for a in (x, norm_pre_w, w_in, gdn_conv_w, gdn_a_log, gdn_dt_bias, gdn_norm_w,
                                                      mlstm_conv_w, mlstm_gate_bias, mlstm_norm_w, w_out, norm_post_w)]
    nc = build_nc()
    in_maps = [_core_inputs(c, *args) for c in range(8)]
    res = run_bass_kernel_spmd(nc, in_maps, core_ids=list(range(8)))
    out = np.zeros((4, S, D), np.float32)
    for c in range(8):
        b, hh = c // 2, c % 2
        out[b, hh * 2048:(hh + 1) * 2048] = res.results[c]["y"]
    return out
```
